# Optimizing a Trainium2 kernel written in Bass

```python
import math
import jax, jax.numpy as jnp
from jax import lax
import numpy as np

D_MODEL = 2048
BATCH = 4
SEQ = 8192
DEPTH = 4
DEC_BATCH = 8
DEC_SEQ = 32
PAST_LEN = 1024

CHUNK = 64
D_MIX = D_MODEL
D_POOL = D_MIX // 4
POOL_WINDOWS = (2, 4, 8, 16)
N_POOL_GROUPS = len(POOL_WINDOWS)
POOL_GROUP = D_POOL // N_POOL_GROUPS
POOL_HIST = max(POOL_WINDOWS) - 1
HEAD_DIM = 64
D_ATTN = D_MIX // 2
N_Q_HEADS = D_ATTN // HEAD_DIM
N_KV_HEADS = 2
GQA_GROUP = N_Q_HEADS // N_KV_HEADS
WINDOW = 128
WIN_CHUNKS = WINDOW // CHUNK
BAND = (WIN_CHUNKS + 1) * CHUNK
NUM_BUCKETS = 32
MAX_DISTANCE = 128
NEG = -1e30
D_RWKV = D_MIX - D_POOL - D_ATTN
RWKV_HEAD = 64
N_RWKV_HEADS = D_RWKV // RWKV_HEAD
DECAY_LORA = 64
ICLR_LORA = 64
D_SHIFT = 3 * D_RWKV + DECAY_LORA + ICLR_LORA
NORM_EPS = 1e-5
LNX_EPS = 1e-5 * RWKV_HEAD
SPLITS = (D_POOL, D_POOL, N_Q_HEADS * HEAD_DIM, N_KV_HEADS * HEAD_DIM, N_KV_HEADS * HEAD_DIM, D_ATTN, D_SHIFT, D_RWKV)
D_IN = sum(SPLITS)
SPLIT_IDX = [int(s) for s in np.cumsum(SPLITS)[:-1]]
RWKV_SPLIT_IDX = [D_RWKV, 2 * D_RWKV, 3 * D_RWKV, 3 * D_RWKV + DECAY_LORA]

kernel_name = "hybrid_pool_swa_rwkv7_stream_step"


def rms_norm(x, w):
    xf = x.astype(jnp.float32)
    y = xf * lax.rsqrt(jnp.mean(xf * xf, axis=-1, keepdims=True) + NORM_EPS)
    return (y * w.astype(jnp.float32)).astype(x.dtype)


def t5_bucket(rel):
    nb = NUM_BUCKETS // 2
    max_exact = nb // 2
    ret = jnp.where(rel > 0, nb, 0)
    n = jnp.abs(rel)
    nf = jnp.maximum(n, 1).astype(jnp.float32)
    large = max_exact + (jnp.log(nf / max_exact) / math.log(MAX_DISTANCE / max_exact) * (nb - max_exact)).astype(jnp.int32)
    large = jnp.minimum(large, nb - 1)
    return ret + jnp.where(n < max_exact, n, large)


def rel_bias(rel, table):
    b = table.astype(jnp.float32)[t5_bucket(rel)]
    return jnp.transpose(b, (2, 0, 1)).reshape(N_KV_HEADS, GQA_GROUP, rel.shape[0], rel.shape[1])


def sink_softmax(logits, sinks):
    s = sinks[..., None, None]
    m = jnp.maximum(jnp.max(logits, axis=-1, keepdims=True), s)
    e = jnp.exp(logits - m)
    return e / (jnp.sum(e, axis=-1, keepdims=True) + jnp.exp(s - m))


def swa_prompt(q, k, v, sinks, table):
    B, T = q.shape[:2]
    nC = T // CHUNK
    qc = q.reshape(B, nC, CHUNK, N_KV_HEADS, GQA_GROUP, HEAD_DIM)

    def band(x):
        xc = x.reshape(B, nC, CHUNK, N_KV_HEADS, HEAD_DIM)
        xp = jnp.pad(xc, ((0, 0), (WIN_CHUNKS, 0), (0, 0), (0, 0), (0, 0)))
        return jnp.concatenate([xp[:, j:j + nC] for j in range(WIN_CHUNKS + 1)], axis=2)

    kb, vb = band(k), band(v)
    logits = jnp.einsum('bcqkgd,bcskd->bckgqs', qc, kb).astype(jnp.float32) * (HEAD_DIM ** -0.5)
    i = jnp.arange(CHUNK)
    j = jnp.arange(BAND)
    rel = j[None, :] - WIN_CHUNKS * CHUNK - i[:, None]
    logits = logits + rel_bias(rel, table)
    key_chunk = jnp.arange(nC)[:, None] - WIN_CHUNKS + j[None, :] // CHUNK
    valid = key_chunk >= 0
    logits = jnp.where(valid[None, :, None, None, None, :], logits, NEG)
    p = sink_softmax(logits, sinks).astype(v.dtype)
    o = jnp.einsum('bckgqs,bcskd->bcqkgd', p, vb)
    return o.reshape(B, T, D_ATTN)


def swa_sample(q, k, v, k_cache, v_cache, sinks, table):
    Bd, Ts = q.shape[:2]
    kf = jnp.concatenate([k_cache.astype(k.dtype), k], axis=1)
    vf = jnp.concatenate([v_cache.astype(v.dtype), v], axis=1)
    qg = q.reshape(Bd, Ts, N_KV_HEADS, GQA_GROUP, HEAD_DIM)
    logits = jnp.einsum('bqkgd,bskd->bkgqs', qg, kf).astype(jnp.float32) * (HEAD_DIM ** -0.5)
    q_pos = PAST_LEN + jnp.arange(Ts)
    k_pos = PAST_LEN - WINDOW + jnp.arange(WINDOW + Ts)
    rel = k_pos[None, :] - q_pos[:, None]
    logits = logits + rel_bias(rel, table)
    qcn = q_pos // CHUNK
    kcn = k_pos // CHUNK
    valid = (kcn[None, :] <= qcn[:, None]) & (kcn[None, :] >= qcn[:, None] - WIN_CHUNKS)
    logits = jnp.where(valid, logits, NEG)
    p = sink_softmax(logits, sinks).astype(vf.dtype)
    o = jnp.einsum('bkgqs,bskd->bqkgd', p, vf).reshape(Bd, Ts, D_ATTN)
    return o, kf[:, -WINDOW:], vf[:, -WINDOW:]


def pool_mixer(p_hist, p, pos0, pool_w, pool_scale):
    B, T = p.shape[:2]
    ext = jnp.concatenate([p_hist.astype(p.dtype), p], axis=1).astype(jnp.float32)
    S = jnp.concatenate([jnp.zeros_like(ext[:, :1]), jnp.cumsum(ext, axis=1)], axis=1)
    end = S[:, POOL_HIST + 1:]
    pos = pos0 + jnp.arange(T)
    outs = []
    for g, w in enumerate(POOL_WINDOWS):
        sl = slice(g * POOL_GROUP, (g + 1) * POOL_GROUP)
        start = S[:, POOL_HIST + 1 - w:POOL_HIST + 1 - w + T, sl]
        cnt = jnp.minimum(pos + 1, w).astype(jnp.float32)[:, None]
        outs.append((end[..., sl] - start) / cnt - ext[:, POOL_HIST:, sl])
    d = jnp.stack(outs, axis=2).astype(p.dtype)
    y = jnp.einsum('btgc,gcd->btgd', d, pool_w).reshape(B, T, D_POOL) * pool_scale
    return y, ext[:, -POOL_HIST:].astype(p.dtype)


def wkv_scan(S0, r, w, k, v, a_vec, b_vec):
    def step(S, inp):
        r_t, w_t, k_t, v_t, a_t, b_t = inp
        sa = jnp.einsum('bhij,bhj->bhi', S, a_t)
        S = S * w_t[:, :, None, :] + sa[..., None] * b_t[:, :, None, :] + v_t[..., None] * k_t[:, :, None, :]
        o = jnp.einsum('bhij,bhj->bhi', S, r_t)
        return S, o
    xs = tuple(jnp.moveaxis(t, 1, 0) for t in (r, w, k, v, a_vec, b_vec))
    S, o = lax.scan(step, S0, xs)
    return S, jnp.moveaxis(o, 0, 1)


def rwkv_mixer(xc, shift_prev, S0, mu, w0, w_up, a0, a_up, k_k, k_a, r_k, lnx_w, lnx_b):
    B, T = xc.shape[:2]
    prev = jnp.concatenate([shift_prev[:, None].astype(xc.dtype), xc[:, :-1]], axis=1)
    xs = xc + (prev - xc) * mu
    r, k, v, wl, al = jnp.split(xs.astype(jnp.float32), RWKV_SPLIT_IDX, axis=-1)
    w_raw = -jax.nn.softplus(-(w0 + jnp.tanh(wl) @ w_up)) - 0.5
    decay = jnp.exp(-jnp.exp(w_raw))
    a = jax.nn.sigmoid(a0 + al @ a_up)
    heads = lambda t: t.reshape(B, T, N_RWKV_HEADS, RWKV_HEAD)
    kk = heads(k * k_k)
    kk = kk / jnp.maximum(jnp.linalg.norm(kk, axis=-1, keepdims=True), 1e-12)
    k = k * (1.0 + (a - 1.0) * k_a)
    rh, kh, vh, wh, ah = heads(r), heads(k), heads(v), heads(decay), heads(a)
    S, o = wkv_scan(S0.astype(jnp.float32), rh, wh, kh, vh, -kk, kk * ah)
    mean = jnp.mean(o, axis=-1, keepdims=True)
    var = jnp.mean(jnp.square(o - mean), axis=-1, keepdims=True)
    o = ((o - mean) * lax.rsqrt(var + LNX_EPS)).reshape(B, T, D_RWKV) * lnx_w + lnx_b
    bonus = jnp.sum(rh * kh * r_k, axis=-1, keepdims=True) * vh
    o = o + bonus.reshape(B, T, D_RWKV)
    return o.astype(xc.dtype), xc[:, -1], S.astype(S0.dtype)


def mixer_layer(h, lp, table, pool_hist, shift_prev, wkv0, kv_cache, pos0):
    B, T = h.shape[:2]
    xn = rms_norm(h, lp['norm_w'])
    u = xn @ lp['w_in']
    p, g_pool, q, k, v, g_attn, xc, g_rwkv = jnp.split(u, SPLIT_IDX, axis=-1)
    y_pool, new_pool = pool_mixer(pool_hist, p, pos0, lp['pool_w'], lp['pool_scale'])
    q = q.reshape(B, T, N_Q_HEADS, HEAD_DIM)
    k = k.reshape(B, T, N_KV_HEADS, HEAD_DIM)
    v = v.reshape(B, T, N_KV_HEADS, HEAD_DIM)
    if kv_cache is None:
        y_attn = swa_prompt(q, k, v, lp['sinks'], table)
        new_k, new_v = k[:, -WINDOW:], v[:, -WINDOW:]
    else:
        y_attn, new_k, new_v = swa_sample(q, k, v, kv_cache[0], kv_cache[1], lp['sinks'], table)
    y_rwkv, new_shift, new_wkv = rwkv_mixer(xc, shift_prev, wkv0, lp['mu'], lp['w0'], lp['w_up'], lp['a0'], lp['a_up'],
                                            lp['k_k'], lp['k_a'], lp['r_k'], lp['lnx_w'], lp['lnx_b'])
    mixed = jnp.concatenate([y_pool * jax.nn.silu(g_pool), y_attn * jax.nn.silu(g_attn), y_rwkv * jax.nn.silu(g_rwkv)], axis=-1)
    return h + mixed @ lp['w_out'], (new_pool, new_k, new_v, new_shift, new_wkv)


def setup_inputs(seed: int = 0) -> dict:
    key = jax.random.key(seed)
    ks = jax.random.split(key, 26)
    nrm = lambda kk, shape, s: jax.random.normal(kk, shape, jnp.float32) * s
    return {
        "x_prompt": nrm(ks[0], (BATCH, SEQ, D_MODEL), 1.0),
        "x_sample": nrm(ks[1], (DEC_BATCH, DEC_SEQ, D_MODEL), 1.0),
        "state_pool": nrm(ks[2], (DEPTH, DEC_BATCH, POOL_HIST, D_POOL), 1.0),
        "cache_swa_k": nrm(ks[3], (DEPTH, DEC_BATCH, WINDOW, N_KV_HEADS, HEAD_DIM), 1.0),
        "cache_swa_v": nrm(ks[4], (DEPTH, DEC_BATCH, WINDOW, N_KV_HEADS, HEAD_DIM), 1.0),
        "state_rwkv_shift": nrm(ks[5], (DEPTH, DEC_BATCH, D_SHIFT), 1.0),
        "state_rwkv_wkv": nrm(ks[6], (DEPTH, DEC_BATCH, N_RWKV_HEADS, RWKV_HEAD, RWKV_HEAD), 0.1),
        "norm_w": 1.0 + nrm(ks[7], (DEPTH, D_MODEL), 0.02),
        "w_in": nrm(ks[8], (DEPTH, D_MODEL, D_IN), D_MODEL ** -0.5),
        "w_out": nrm(ks[9], (DEPTH, D_MIX, D_MODEL), 0.5 * D_MIX ** -0.5),
        "pool_w": nrm(ks[10], (DEPTH, N_POOL_GROUPS, POOL_GROUP, POOL_GROUP), POOL_GROUP ** -0.5),
        "pool_scale": 1.0 + nrm(ks[11], (DEPTH, D_POOL), 0.02),
        "attn_sinks": nrm(ks[12], (DEPTH, N_Q_HEADS), 0.5),
        "rel_bias_table": nrm(ks[13], (NUM_BUCKETS, N_Q_HEADS), 0.5),
        "rwkv_mu": jax.random.uniform(ks[14], (DEPTH, D_SHIFT), jnp.float32),
        "rwkv_w0": jax.random.uniform(ks[15], (DEPTH, D_RWKV), jnp.float32, -4.0, -1.0),
        "rwkv_w_up": nrm(ks[16], (DEPTH, DECAY_LORA, D_RWKV), 0.5 * DECAY_LORA ** -0.5),
        "rwkv_a0": nrm(ks[17], (DEPTH, D_RWKV), 0.1),
        "rwkv_a_up": nrm(ks[18], (DEPTH, ICLR_LORA, D_RWKV), 0.5 * ICLR_LORA ** -0.5),
        "rwkv_k_k": 0.85 + nrm(ks[19], (DEPTH, D_RWKV), 0.02),
        "rwkv_k_a": 1.0 + nrm(ks[20], (DEPTH, D_RWKV), 0.02),
        "rwkv_r_k": nrm(ks[21], (DEPTH, N_RWKV_HEADS, RWKV_HEAD), 0.1),
        "rwkv_lnx_w": 1.0 + nrm(ks[22], (DEPTH, D_RWKV), 0.02),
        "rwkv_lnx_b": nrm(ks[23], (DEPTH, D_RWKV), 0.02),
        "final_norm_w": 1.0 + nrm(ks[24], (D_MODEL,), 0.02),
    }


def reference(x_prompt, x_sample, state_pool, cache_swa_k, cache_swa_v, state_rwkv_shift, state_rwkv_wkv,
              norm_w, w_in, w_out, pool_w, pool_scale, attn_sinks, rel_bias_table,
              rwkv_mu, rwkv_w0, rwkv_w_up, rwkv_a0, rwkv_a_up, rwkv_k_k, rwkv_k_a, rwkv_r_k,
              rwkv_lnx_w, rwkv_lnx_b, final_norm_w):
    hp, hs = x_prompt, x_sample
    prompt_states, sample_states = [], []
    for l in range(DEPTH):
        lp = {
            'norm_w': norm_w[l], 'w_in': w_in[l], 'w_out': w_out[l],
            'pool_w': pool_w[l], 'pool_scale': pool_scale[l],
            'sinks': attn_sinks[l].astype(jnp.float32).reshape(N_KV_HEADS, GQA_GROUP),
            'mu': rwkv_mu[l], 'w0': rwkv_w0[l].astype(jnp.float32), 'w_up': rwkv_w_up[l].astype(jnp.float32),
            'a0': rwkv_a0[l].astype(jnp.float32), 'a_up': rwkv_a_up[l].astype(jnp.float32),
            'k_k': rwkv_k_k[l].astype(jnp.float32), 'k_a': rwkv_k_a[l].astype(jnp.float32),
            'r_k': rwkv_r_k[l].astype(jnp.float32),
            'lnx_w': rwkv_lnx_w[l].astype(jnp.float32), 'lnx_b': rwkv_lnx_b[l].astype(jnp.float32),
        }
        Bp = hp.shape[0]
        pool0 = jnp.zeros((Bp, POOL_HIST, D_POOL), hp.dtype)
        shift0 = jnp.zeros((Bp, D_SHIFT), hp.dtype)
        wkv_init = jnp.zeros((Bp, N_RWKV_HEADS, RWKV_HEAD, RWKV_HEAD), state_rwkv_wkv.dtype)
        hp, sp = mixer_layer(hp, lp, rel_bias_table, pool0, shift0, wkv_init, None, 0)
        hs, ss = mixer_layer(hs, lp, rel_bias_table, state_pool[l], state_rwkv_shift[l], state_rwkv_wkv[l],
                             (cache_swa_k[l], cache_swa_v[l]), PAST_LEN)
        prompt_states.append(sp)
        sample_states.append(ss)
    new_pool_p, new_k_p, new_v_p, new_shift_p, new_wkv_p = [jnp.stack(t) for t in zip(*prompt_states)]
    new_pool_s, new_k_s, new_v_s, new_shift_s, new_wkv_s = [jnp.stack(t) for t in zip(*sample_states)]
    y_prompt = rms_norm(hp, final_norm_w)
    y_sample = rms_norm(hs, final_norm_w)
    return (y_prompt, y_sample, new_pool_p, new_k_p, new_v_p, new_shift_p, new_wkv_p,
            new_pool_s, new_k_s, new_v_s, new_shift_s, new_wkv_s)
```

```python
import math
from contextlib import ExitStack

import numpy as np
import concourse.bass as bass
import concourse.mybir as mybir
from concourse.bass_utils import run_bass_kernel_spmd

F32 = mybir.dt.float32
BF16 = mybir.dt.bfloat16
AF = mybir.ActivationFunctionType
ALU = mybir.AluOpType

D = 2048
NCH = 16
TT = 256
DEC = 32
NBLK_IN = 22
NBLK_OUT = 8
DSH = 1664
CDEC = math.exp(-0.5)
NEGB = -30000.0
NSLOT = 3


class Sched:
    ENG = ("pe", "act", "dve", "pool", "sp")
    ROT = {"ld": 6, "cast": 8, "x": 2, "yout": 2, "sout": 4}

    def __init__(self, sems, dma_sems):
        self.rot = {}
        self.sem = sems
        self.cnt = {k: 0 for k in self.ENG}
        self.seen = {k: {} for k in self.ENG}
        self.dma_sems = dma_sems
        self.dma_map = {}
        self.lastw = {}
        self.reads = {}
        self.q = {k: [] for k in self.ENG}
        self.nops = 0

    def _wait(self, eng, deps):
        best = {}
        for ev in deps:
            if ev is None:
                continue
            k, s, v = ev
            if k not in best or best[k][1] < v:
                best[k] = (s, v)
        for k, (s, v) in best.items():
            if self.seen[eng].get(k, 0) >= v:
                continue
            if k == eng and eng == "pe":
                continue
            if k == eng and v > self.cnt[eng]:
                raise RuntimeError(f"self-deadlock on {eng}")
            self.q[eng].append(("w", s, v))
            self.seen[eng][k] = v

    def _deps(self, reads, writes):
        deps = []
        for r in reads:
            deps.append(self.lastw.get(r))
        for w in writes:
            deps.append(self.lastw.get(w))
            deps.extend(self.reads.get(w, {}).values())
        return deps

    def _commit(self, ev, reads, writes):
        k = ev[0]
        for r in reads:
            self.reads.setdefault(r, {})[k] = ev
        for w in writes:
            self.lastw[w] = ev
            self.reads[w] = {}

    def op(self, eng, meth, reads, writes, signal=True, **kw):
        writes = list(writes) + [r for r in reads if isinstance(r, str) and r[:2] in ("ps", "pa") and r not in writes]
        self._wait(eng, self._deps(reads, writes))
        ev = (eng, self.sem[eng], self.cnt[eng] + 1)
        if signal:
            self.q[eng].append(("o", meth, kw, self.sem[eng], 1))
            self.cnt[eng] += 1
        else:
            self.q[eng].append(("o", meth, kw, None, 0))
        self._commit(ev, reads, writes)
        self.nops += 1

    def dma(self, eng, key, out, in_, reads=(), writes=()):
        if isinstance(key, str) and key in self.ROT:
            n = self.rot.get(key, 0)
            self.rot[key] = n + 1
            key = f"{key}{n % self.ROT[key]}"
        if key not in self.dma_map:
            self.dma_map[key] = [self.dma_sems.pop(), 0]
        ent = self.dma_map[key]
        prev = [("dma:" + str(key), ent[0], ent[1])] if ent[1] else []
        self._wait(eng, self._deps(reads, writes) + prev)
        ent[1] += 16
        self.q[eng].append(("o", "dma_start", dict(out=out, in_=in_), ent[0], 16))
        ev = ("dma:" + str(key), ent[0], ent[1])
        self._commit(ev, reads, writes)
        self.nops += 1

    def finish(self, eng):
        deps = []
        for key, (s, v) in self.dma_map.items():
            if v:
                deps.append(("dma:" + str(key), s, v))
        for k in self.ENG:
            if self.cnt[k] and k != eng:
                deps.append((k, self.sem[k], self.cnt[k]))
        self._wait(eng, deps)

    def replay(self, eng, e):
        for it in self.q[eng]:
            if it[0] == "w":
                e.wait_ge(it[1], it[2])
            else:
                ins = getattr(e, it[1])(**it[2])
                if it[3] is not None:
                    ins.then_inc(it[3], it[4])

    def emit(self, block):
        @block.tensor
        def _(e):
            self.replay("pe", e)

        @block.scalar
        def _(e):
            self.replay("act", e)

        @block.vector
        def _(e):
            self.replay("dve", e)

        @block.gpsimd
        def _(e):
            self.replay("pool", e)

        @block.sync
        def _(e):
            self.replay("sp", e)


def _bucket_np(rel):
    nb, me = 16, 8
    ret = np.where(rel > 0, nb, 0)
    n = np.abs(rel)
    nf = np.maximum(n, 1).astype(np.float32)
    large = me + (np.log(nf / np.float32(me)) / np.float32(math.log(128 / me)) * np.float32(nb - me)).astype(np.int32)
    large = np.minimum(large, nb - 1)
    return ret + np.where(n < me, n, large)


def host_consts():
    c = {}
    s = np.arange(192)[:, None]
    i = np.arange(64)[None, :]
    b = _bucket_np((s - 128 - i).astype(np.int32))
    oh = np.zeros((32, 64, 192), np.float32)
    for ii in range(64):
        oh[b[:, ii], ii, np.arange(192)] = 1.0
    c["c_oh"] = oh
    t = np.arange(64)
    su = (t[:, None] < t[None, :]).astype(np.float32)
    c["c_msk"] = np.ascontiguousarray(np.stack([su, su.T, (t[:, None] <= t[None, :]).astype(np.float32), np.eye(64, dtype=np.float32)], axis=1))
    c["c_ident"] = np.eye(128, dtype=np.float32)
    bo = np.zeros((128, 128), np.float32)
    bo[:64, :64] = 1.0
    bo[64:, 64:] = 1.0
    c["c_bones"] = bo
    cm = np.ones((128, TT), np.float32)
    cm[:, ::64] = 0.0
    c["c_cmask"] = cm
    rc = np.zeros((128, 4, 16), np.float32)
    for g, w in enumerate((2, 4, 8, 16)):
        rc[:, g, :] = 1.0 / np.minimum(np.arange(16) + 1, w)
    c["c_rcnt"] = rc
    return c


class _Stop(Exception):
    pass


def build_nc(SEQ, DEPTH):
    import os
    KSTOP = float(os.environ.get("KSTOP", "0"))

    def chk(level):
        if KSTOP and level >= KSTOP:
            raise _Stop()

    L = DEPTH
    NT = SEQ // TT
    nc = bass.Bass("TRN2", target_bir_lowering=False)
    dt_in = lambda n, sh: nc.dram_tensor(n, sh, F32, kind="ExternalInput").ap()
    dt_out = lambda n, sh: nc.dram_tensor(n, sh, F32, kind="ExternalOutput").ap()
    xp = dt_in("xp", [SEQ, D])
    xs = dt_in("xs", [DEC, D])
    st_pool = dt_in("st_pool", [L, 15, 512])
    st_k = dt_in("st_k", [L, 128, 128])
    st_v = dt_in("st_v", [L, 128, 128])
    st_shift = dt_in("st_shift", [L, 13, 128])
    st_wkv = dt_in("st_wkv", [L, 8, 64, 64])
    norm_w = dt_in("norm_w", [L * 16, 128])
    fnorm_w = dt_in("fnorm_w", [16, 128])
    w_in = dt_in("w_in", [L, D, 5504])
    w_out = dt_in("w_out", [L, D, D])
    pool_w = dt_in("pool_w", [L, 4, 128, 128])
    vec512 = dt_in("vec512", [8 * L * 4, 128])
    sinks = dt_in("sinks", [1, L * 16])
    table = dt_in("table", [32, 16])
    mu = dt_in("mu", [L * 13, 128])
    w_up = dt_in("w_up", [L, 64, 512])
    a_up = dt_in("a_up", [L, 64, 512])
    c_oh = dt_in("c_oh", [32, 64, 192])
    c_msk = dt_in("c_msk", [64, 4, 64])
    c_ident = dt_in("c_ident", [128, 128])
    c_bones = dt_in("c_bones", [128, 128])
    c_cmask = dt_in("c_cmask", [128, TT])
    c_rcnt = dt_in("c_rcnt", [128, 4, 16])

    yp = dt_out("yp", [SEQ, D])
    ys = dt_out("ys", [DEC, D])
    outs = {}
    for g in ("p", "s"):
        outs[g] = dict(
            pool=dt_out(f"o_pool_{g}", [L, 15, 512]),
            k=dt_out(f"o_k_{g}", [L, 128, 128]),
            v=dt_out(f"o_v_{g}", [L, 128, 128]),
            shift=dt_out(f"o_shift_{g}", [L, 13, 128]),
            wkv=dt_out(f"o_wkv_{g}", [L, 8, 64, 64]),
        )
    wib = nc.dram_tensor("wib", [L, NBLK_IN, 128, NCH, 256], BF16, kind="Internal").ap()
    wob = nc.dram_tensor("wob", [L, NBLK_OUT, 128, NCH, 256], BF16, kind="Internal").ap()

    with ExitStack() as es:
        def SB(n, sh, dt=F32):
            return es.enter_context(nc.sbuf_tensor(n, sh, dt))

        def PS(n, sh, dt=F32):
            return es.enter_context(nc.psum_tensor(n, sh, dt))

        ident = SB("ident", [128, 128])
        identb = SB("identb", [128, 128], BF16)
        bones_b = SB("bones_b", [128, 128], BF16)
        bmean = SB("bmean", [128, 128])
        ones_b = SB("ones_b", [128, 128], BF16)
        msk = SB("msk", [64, 4, 64])
        cmask = SB("cmask", [128, TT])
        rcnt = SB("rcnt", [128, 4, 16])
        biasA = SB("biasA", [128, 2, 2, 512], BF16)
        biasNeg = SB("biasNeg", [128, 512], BF16)
        biasB = SB("biasB", [128, 2, 512], BF16)
        nw = SB("nw", [128, L * 16 + 16])
        v512 = SB("v512", [128, 8 * L * 4])
        muT = SB("muT", [128, L * 13])
        omu = SB("omu", [128, L * 13])
        es_raw = SB("es_raw", [128, L * 16])
        es_t = SB("es_t", [128, L, 2, 4])
        poolw_b = SB("poolw_b", [128, L * 4, 128], BF16)
        lora_b = SB("lora_b", [128, L, 512], BF16)
        stage = SB("stage", [128, 1024])
        tab8 = SB("tab8", [32, 16])
        tab8p = SB("tab8p", [32, 2, 2, 4])
        biasA32 = SB("biasA32", [128, 2, 256], BF16)
        biasB32 = SB("biasB32", [128, 2, 256], BF16)

        def vec(vi, l, c):
            j = (vi * L + l) * 4 + c
            return v512[:, j:j + 1]

        h = SB("h", [128, NCH, TT])
        xn = SB("xn", [128, NCH, TT], BF16)
        mixed = SB("mixed", [128, NCH, TT], BF16)
        rstd = SB("rstd", [128, TT])
        wslot = [SB(f"wslot{i}", [128, NCH, 256], BF16) for i in range(NSLOT)]
        pext = SB("pext", [128, 4, 16 + TT])
        ptA = SB("ptA", [128, 4, 16 + TT])
        ptB = SB("ptB", [128, 4, 16 + TT])
        pd = SB("pd", [128, 4, TT], BF16)
        qT = SB("qT", [128, 8, TT], BF16)
        kT = SB("kT", [128, 2, 128 + TT], BF16)
        vT = SB("vT", [128, 128 + TT])
        k32 = SB("k32", [128, 128 + TT])
        Vwin = SB("Vwin", [128, 4, 128], BF16)
        Vcur = SB("Vcur", [64, 4, 128], BF16)
        E1 = [SB(f"E1_{i}", [128, 512], BF16) for i in range(1)]
        E2 = [SB(f"E2_{i}", [64, 512], BF16) for i in range(1)]
        rden = SB("rden", [128, 256])
        sgt = [SB(f"sgt{i}", [128, TT]) for i in range(2)]
        Rr = SB("Rr", [128, 4, TT])
        Kr = SB("Kr", [128, 4, TT])
        Vr = SB("Vr", [128, 4, TT])
        WA = SB("WA", [128, TT])
        twal = SB("twal", [128, TT], BF16)
        tm = [SB(f"tm{i}", [128, TT]) for i in range(12)]
        tmb = SB("tmb", [128, TT], BF16)
        tmb2 = SB("tmb2", [128, TT], BF16)
        rt = SB("rt", [64, 8, TT], BF16)
        kt = SB("kt", [64, 8, TT], BF16)
        bt = SB("bt", [64, 8, TT], BF16)
        at = SB("at", [64, 8, TT], BF16)
        glast = SB("glast", [64, 8, 4])
        bonus = SB("bonus", [128, 4, TT])
        Vt = SB("Vt", [64, 4, 512], BF16)
        Kpt = SB("Kpt", [64, 4, 512], BF16)
        Bpt = SB("Bpt", [64, 4, 512], BF16)
        Pm = [SB(f"Pm{i}", [64, 512], BF16) for i in range(2)]
        Qm = [SB(f"Qm{i}", [64, 512], BF16) for i in range(2)]
        Rm = [SB(f"Rm{i}", [64, 512], BF16) for i in range(2)]
        AakT = SB("AakT", [64, 512], BF16)
        ArbT = SB("ArbT", [64, 512], BF16)
        ArkT = SB("ArkT", [64, 512], BF16)
        RHSb = SB("RHSb", [64, 512], BF16)
        SAb = SB("SAb", [64, 512], BF16)
        STg = SB("STg", [64, 8, 64])
        STb = SB("STb", [64, 8, 64], BF16)
        OT = SB("OT", [128, 4, TT])
        hist = SB("hist", [128, L, 4, 16])
        khalo = SB("khalo", [128, L, 2, 128], BF16)
        k32halo = SB("k32halo", [128, L, 128])
        vhalo = SB("vhalo", [128, L, 128])
        shst = SB("shst", [128, L, 13])
        ST = SB("ST", [64, L, 8, 64])

        psA = [PS(f"psA{i}", [128, 2, 256]) for i in range(2)]
        psL1 = PS("psL1", [128, 512])
        psL2 = PS("psL2", [128, 512])
        psO = PS("psO", [128, 512])
        psR = [PS(f"psR{i}", [128, 512]) for i in range(3)]

        sems = {k: es.enter_context(nc.semaphore("s_" + k)) for k in Sched.ENG}
        dsems = [es.enter_context(nc.semaphore(f"dq{i}")) for i in range(40)]
        block = es.enter_context(nc.Block())
        S = Sched(sems, dsems)

        pa_ctr = [0]

        def next_pa():
            i = pa_ctr[0] % 2
            pa_ctr[0] += 1
            return psA[i][:, 0, :], f"pa{i}"

        ev_ctr = [0]

        def evac_eng():
            ev_ctr[0] += 1
            return "act" if ev_ctr[0] % 2 else "dve"

        def CP(eng, out, in_, r, w):
            S.op(eng, "copy" if eng == "act" else "tensor_copy", r, w, out=out, in_=in_)

        def TTO(eng, out, in0, in1, op, r, w):
            S.op(eng, "tensor_tensor", r, w, out=out, in0=in0, in1=in1, op=op)

        def STT(eng, out, in0, scalar, in1, op0, op1, r, w):
            S.op(eng, "scalar_tensor_tensor", r, w, out=out, in0=in0, scalar=scalar, in1=in1, op0=op0, op1=op1)

        def TS(eng, out, in0, s1, s2, op0, op1, r, w):
            S.op(eng, "tensor_scalar", r, w, out=out, in0=in0, scalar1=s1, scalar2=s2, op0=op0, op1=op1)

        def ACTV(out, in_, func, r, w, **kw):
            S.op("act", "activation", r, w, out=out, in_=in_, func=func, **kw)

        def MM(out, lhsT, rhs, r, w, start=True, stop=True, signal=True, tp=None):
            kw = dict(out=out, lhsT=lhsT, rhs=rhs, start=start, stop=stop)
            if tp is not None:
                kw["tile_position"] = tp
            S.op("pe", "matmul", r, w, signal=signal, **kw)

        def TR(out, in_, idn, r, w):
            S.op("pe", "transpose", r + ["ident"], w, out=out, in_=in_, identity=idn)

        def MSET(eng, ap, val, r, w):
            S.op(eng, "memset", r, w, ap=ap, constant=val)

        ci = [0]

        def cast_dma(dst, src, res):
            S.dma("pool", "cast", dst, src, writes=[res])
            ci[0] += 1

        def ext_src(e):
            if e < 4:
                return e * 128
            if e < 12:
                return 1024 + (e - 4) * 128
            if e < 14:
                return None
            if e == 14:
                return 2176
            if e < 28:
                return 3328 + (e - 15) * 128
            if e < 32:
                return 512 + (e - 28) * 128
            if e < 40:
                return 2304 + (e - 32) * 128
            return 4992 + (e - 40) * 128

        for l in range(L):
            wv = w_in[l].rearrange("(kc p) c -> p kc c", p=128)
            for b in range(NBLK_IN):
                res = ("wib", l, b)
                s0, s1 = ext_src(2 * b), ext_src(2 * b + 1)
                if s0 is None:
                    for dc, sc in ((0, 2048), (64, 2048), (128, 2112), (192, 2112)):
                        cast_dma(wib[l, b, :, :, dc:dc + 64], wv[:, :, sc:sc + 64], res)
                elif s1 == s0 + 128:
                    cast_dma(wib[l, b], wv[:, :, s0:s0 + 256], res)
                else:
                    cast_dma(wib[l, b, :, :, 0:128], wv[:, :, s0:s0 + 128], res)
                    cast_dma(wib[l, b, :, :, 128:256], wv[:, :, s1:s1 + 128], res)
            wo = w_out[l].rearrange("(kc p) c -> p kc c", p=128)
            for b in range(NBLK_OUT):
                cast_dma(wob[l, b], wo[:, :, b * 256:(b + 1) * 256], ("wob", l, b))

        def ld(dst, src, res):
            S.dma("sp", "ld", dst, src, writes=[res])

        ld(ident[:], c_ident[:, :], "ident")
        ld(msk[:], c_msk[:, :, :], "msk")
        ld(cmask[:], c_cmask[:, :], "cmask")
        ld(rcnt[:], c_rcnt[:, :, :], "rcnt")
        ld(bmean[:], c_bones[:, :], "bmean")
        CP("dve", identb[:], ident[:], ["ident"], ["identb"])
        CP("dve", bones_b[:], bmean[:], ["bmean"], ["bones_b"])
        S.op("dve", "tensor_scalar_mul", ["bmean", "bones_b"], ["bmean"], out=bmean[:], in0=bmean[:], scalar1=1.0 / 64.0)
        MSET("pool", ones_b[:], 1.0, [], ["ones_b"])

        def load_T(dst, src, nrows, res):
            for r0 in range(0, nrows, 128):
                n = min(128, nrows - r0)
                S.dma("sp", "ld", stage[0:n, 0:128], src[r0:r0 + n, :], writes=["stage"])
                pa, pr = next_pa()
                TR(pa[:, 0:n], stage[0:n, 0:128], ident[0:n, 0:n], ["stage"], [pr])
                CP("dve", dst[:, r0:r0 + n], pa[:, 0:n], [pr], [res])

        load_T(nw[:, 0:L * 16], norm_w, L * 16, "nw")
        load_T(nw[:, L * 16:L * 16 + 16], fnorm_w, 16, "nw")
        load_T(v512, vec512, 8 * L * 4, "v512")
        load_T(muT, mu, L * 13, "muT")
        TS("dve", omu[:], muT[:], -1.0, 1.0, ALU.mult, ALU.add, ["muT"], ["omu"])

        S.dma("sp", "ld", es_raw[:], sinks.to_broadcast([128, L * 16]), writes=["es_raw"])
        ACTV(es_raw[:], es_raw[:], AF.Exp, ["es_raw"], ["es_raw"])
        for par in range(2):
            CP("dve", es_t[par * 64:(par + 1) * 64, :, :, :],
               es_raw[par * 64:(par + 1) * 64, :].rearrange("p (l kh g2 q) -> p l kh g2 q", l=L, kh=2, g2=4, q=2)[:, :, :, :, par],
               ["es_raw"], ["es_t"])

        for l in range(L):
            S.dma("sp", "ld", stage[:, 0:512].rearrange("p (g d) -> p g d", g=4), pool_w[l].rearrange("g c d -> c g d"), writes=["stage"])
            CP("dve", poolw_b[:, l * 4:(l + 1) * 4, :], stage[:, 0:512].rearrange("p (g d) -> p g d", g=4), ["stage"], ["poolw_b"])
            S.dma("sp", "ld", stage[0:64, 512:1024], w_up[l], writes=["stage"])
            S.dma("sp", "ld", stage[64:128, 512:1024], a_up[l], writes=["stage"])
            CP("dve", lora_b[:, l, :], stage[:, 512:1024], ["stage"], ["lora_b"])

        S.dma("sp", "ld", tab8[:], table[:, :], writes=["tab8"])
        S.op("dve", "tensor_scalar_mul", ["tab8"], ["tab8"], out=tab8[:], in0=tab8[:], scalar1=8.0)
        CP("dve", tab8p[:, :, :, :], tab8[:, :].rearrange("b (kh g2 q) -> b kh q g2", kh=2, g2=4, q=2), ["tab8"], ["tab8p"])
        ohs = stage[0:32, 0:768].rearrange("p (a s) -> p a s", s=192)
        MSET("pool", biasB[:], 0.0, [], ["biasB"])
        MSET("pool", biasB32[:], 0.0, [], ["biasB32"])
        for kh in range(2):
            for i0 in range(0, 64, 4):
                S.dma("sp", "ld", ohs, c_oh[:, i0:i0 + 4, :], writes=["stage"])
                for ii in range(4):
                    i = i0 + ii
                    rhs8 = tab8p[:, kh, :, :].rearrange("b q g -> b (q g)")
                    MM(psL1[:, :].rearrange("p (g i) -> p g i", i=64)[:, :, i], ohs[:, ii, 0:128], rhs8, ["stage", "tab8p"], ["psL1"])
                    MM(psL2[0:64, :].rearrange("p (g i) -> p g i", i=64)[:, :, i], ohs[:, ii, 128:192], rhs8, ["stage", "tab8p"], ["psL2"])
            CP("dve", biasA[:, 0, kh, :], psL1[:, :], ["psL1"], ["biasA"])
            CP("dve", biasB[0:64, kh, :], psL2[0:64, :], ["psL2"], ["biasB"])
        MSET("pool", biasNeg[:], NEGB, [], ["biasA"])
        CP("pool", biasA[:, 1, :, :], biasA[:, 0, :, :], ["biasA"], ["biasA"])
        MSET("pool", biasA[0:64, 1, :, :], NEGB, ["biasA"], ["biasA"])
        for kh in range(2):
            CP("pool", biasA32[:, kh, :].rearrange("p (a i) -> p a i", i=32), biasA[:, 0, kh, :].rearrange("p (a i) -> p a i", i=64)[:, :, 0:32], ["biasA"], ["biasA32"])
            CP("pool", biasB32[0:32, kh, :].rearrange("p (a i) -> p a i", i=32), biasB[0:32, kh, :].rearrange("p (a i) -> p a i", i=64)[:, :, 0:32], ["biasB"], ["biasB32"])

        hres = [("h", kc) for kc in range(NCH)]
        xnres = [("xn", kc) for kc in range(NCH)]
        mxres = [("mx", m) for m in range(NCH)]
        shres = [("shst", j) for j in range(13)]
        for t_, r_ in ((hist, ["hist"]), (khalo, ["khalo"]), (k32halo, ["k32halo"]), (vhalo, ["vhalo"]), (shst, shres), (ST, ["ST"])):
            MSET("pool", t_[:], 0.0, [], r_)

        wctr = [0]

        def proj(wsrc, wres, l, blocks, rhs_fn, rhs_res_fn, T, evac_fn, plevel=99):
            for b in blocks:
                proj_block(wsrc, wres, l, b, rhs_fn, rhs_res_fn, T, evac_fn, plevel)

        def proj_block(wsrc, wres, l, b, rhs_fn, rhs_res_fn, T, evac_fn, plevel=99):
            if True:
                slot = wctr[0] % NSLOT
                wctr[0] += 1
                S.dma("sp", ("w", slot), wslot[slot][:], wsrc[l, b], reads=[(wres, l, b)], writes=[("ws", slot)])
                for half in range(2):
                    pa, pr = next_pa()
                    for kc in range(NCH):
                        MM(pa[:, 0:T], wslot[slot][:, kc, half * 128:(half + 1) * 128], rhs_fn(kc), [("ws", slot), rhs_res_fn(kc)], [pr],
                           start=(kc == 0), stop=(kc == NCH - 1), signal=(kc == NCH - 1))
                    evac_fn(2 * b + half, pa, pr)
                    chk(plevel + (2 * b + half + 1) * 0.01)

        def rms(T):
            pa, pr = next_pa()
            for kc in range(NCH):
                ACTV(xn[:, kc, 0:T], h[:, kc, 0:T], AF.Square, [("h", kc)], [("xn", kc)])
            for kc in range(NCH):
                MM(pa[:, 0:T], ones_b[:], xn[:, kc, 0:T], [("xn", kc), "ones_b"], [pr], start=(kc == 0), stop=(kc == NCH - 1), signal=(kc == NCH - 1))
            ACTV(rstd[:, 0:T], pa[:, 0:T], AF.Sqrt, [pr], ["rstd"], scale=1.0 / D, bias=1e-5)
            S.op("dve", "reciprocal", ["rstd"], ["rstd"], out=rstd[:, 0:T], in_=rstd[:, 0:T])

        def layer(l, g, T, Lc, first_tile):
            nchk = T // Lc
            K = int(round(math.log2(Lc)))

            def v4(ap, par):
                return ap.rearrange("p (g2 q i) -> p g2 q i", q=2, i=64)[:, :, par, 0:Lc]

            def v8(ap):
                return ap.rearrange("p (g i) -> p g i", i=64)[:, :, 0:Lc]

            def v4o(ap):
                return ap.rearrange("p (g2 i) -> p g2 i", i=64)[:, :, 0:Lc]

            def m8(k):
                return msk[0:Lc, k, 0:Lc].unsqueeze(1).to_broadcast([Lc, 8, Lc])

            rms(T)
            for kc in range(NCH):
                STT("dve", xn[:, kc, 0:T], h[:, kc, 0:T], nw[:, l * 16 + kc:l * 16 + kc + 1], rstd[:, 0:T], ALU.mult, ALU.mult,
                    [("h", kc), "rstd", "nw"], [("xn", kc)])
            chk(3)
            CP("pool", pext[:, :, 0:16], hist[:, l, :, :], ["hist"], ["pext_h"])
            CP("pool", kT[:, :, 0:128], khalo[:, l, :, :], ["khalo"], ["kT_h"])
            CP("pool", vT[:, 0:128], vhalo[:, l, :], ["vhalo"], ["vT_h"])
            CP("pool", k32[:, 0:128], k32halo[:, l, :], ["k32halo"], ["k32_h"])

            def evac1(e_, pa, pr):
                eng = evac_eng()
                if e_ < 4:
                    CP(eng, pext[:, e_, 16:16 + T], pa[:, 0:T], [pr], [("pext", e_)])
                elif e_ < 12:
                    CP(eng, qT[:, e_ - 4, 0:T], pa[:, 0:T], [pr], ["qT"])
                elif e_ < 14:
                    kh = e_ - 12
                    CP(eng, kT[:, kh, 128:128 + T], pa[:, 0:T], [pr], ["kT"])
                    if not os.environ.get("KNO32"):
                        eng2 = eng
                        CP(eng2, k32[kh * 64:(kh + 1) * 64, 128:128 + T], pa[kh * 64:(kh + 1) * 64, 0:T], [pr], ["k32"])
                elif e_ == 14:
                    CP(eng, vT[:, 128:128 + T], pa[:, 0:T], [pr], ["vT"])
                else:
                    j = e_ - 15
                    dst = (Rr[:, j, :] if j < 4 else Kr[:, j - 4, :] if j < 8 else Vr[:, j - 8, :] if j < 12 else WA[:, :])
                    dres = ("xs", j)
                    tmp, tres = tm[j % 2], f"tm{j % 2}"
                    mcol = muT[:, l * 13 + j:l * 13 + j + 1]
                    ocol = omu[:, l * 13 + j:l * 13 + j + 1]
                    ACTV(tmp[:, 0:T], pa[:, 0:T], AF.Copy, [pr, "omu"], [tres], scale=ocol)
                    STT("dve", dst[:, 1:T], pa[:, 0:T - 1], mcol, tmp[:, 1:T], ALU.mult, ALU.add, [pr, tres, "muT"], [dres])
                    STT("dve", dst[:, 0:1], shst[:, l, j:j + 1], mcol, tmp[:, 0:1], ALU.mult, ALU.add, [tres, "muT", ("shst", j)], [dres])
                    CP("dve", shst[:, l, j:j + 1], pa[:, T - 1:T], [pr], [("shst", j)])

            proj(wib, "wib", l, range(0, 14), lambda kc: xn[:, kc, 0:T], lambda kc: ("xn", kc), T, evac1, plevel=3)

            chk(4)
            allp = [("pext", i) for i in range(4)] + ["pext_h"]
            W = 16 + T
            TTO("dve", ptA[:, :, 1:W], pext[:, :, 1:W], pext[:, :, 0:W - 1], ALU.add, allp, ["ptA"])
            CP("pool", hist[:, l, :, :], pext[:, :, T:T + 16], allp, ["hist"])
            TTO("pool", ptB[:, 1:4, 3:W], ptA[:, 1:4, 3:W], ptA[:, 1:4, 1:W - 2], ALU.add, ["ptA"], ["ptB"])
            TTO("dve", ptA[:, 2:4, 7:W], ptB[:, 2:4, 7:W], ptB[:, 2:4, 3:W - 4], ALU.add, ["ptB"], ["ptA"])
            TTO("pool", ptB[:, 3:4, 15:W], ptA[:, 3:4, 15:W], ptA[:, 3:4, 7:W - 8], ALU.add, ["ptA"], ["ptB"])
            for gi, w in enumerate((2, 4, 8, 16)):
                src, sres = (ptA, "ptA") if gi % 2 == 0 else (ptB, "ptB")
                STT("dve", pd[:, gi, 0:T], src[:, gi, 16:16 + T], 1.0 / w, pext[:, gi, 16:16 + T], ALU.mult, ALU.subtract, [sres] + allp, ["pd"])
                if first_tile and g == 0:
                    TTO("dve", tm[2][:, 0:16], src[:, gi, 16:32], rcnt[:, gi, :], ALU.mult, [sres, "rcnt"], ["tm2"])
                    TTO("dve", pd[:, gi, 0:16], tm[2][:, 0:16], pext[:, gi, 16:32], ALU.subtract, ["tm2"] + allp, ["pd"])
            for gi in range(4):
                pa, pr = next_pa()
                MM(pa[:, 0:T], poolw_b[:, l * 4 + gi, :], pd[:, gi, 0:T], ["pd", "poolw_b"], [pr])
                ACTV(mixed[:, gi, 0:T], pa[:, 0:T], AF.Copy, [pr, "v512"], [("mx", gi)], scale=vec(0, l, gi))

            chk(5)
            kres = ["kT", "kT_h"]
            vres = ["vT", "vT_h"]
            for c in range(nchk):
                TR(psL1[:, c * 128:(c + 1) * 128], vT[:, Lc * c:Lc * c + 128], ident[:], vres, ["psL1"])
                TR(psL2[0:Lc, c * 128:(c + 1) * 128], vT[:, 128 + Lc * c:128 + Lc * (c + 1)], ident[:], vres, ["psL2"])
            CP("dve", Vwin[:, 0:nchk, :], psL1[:, 0:nchk * 128].rearrange("p (c k) -> p c k", k=128), ["psL1"], ["Vwin"])
            CP("act", Vcur[0:Lc, 0:nchk, :], psL2[0:Lc, 0:nchk * 128].rearrange("p (c k) -> p c k", k=128), ["psL2"], ["Vcur"])
            CP("pool", khalo[:, l, :, :], kT[:, :, T:T + 128], kres, ["khalo"])
            CP("pool", vhalo[:, l, :], vT[:, T:T + 128], vres, ["vhalo"])

            NQ = 4 * Lc

            def q3(ap):
                return ap.rearrange("p (g2 i) -> p g2 i", i=Lc)

            fill = []

            def fill_one(n=1):
                for _ in range(n):
                    if fill:
                        fill.pop(0)()

            def att_unit(kh, c):
                if True:
                    var = 0
                    if first_tile and g == 0:
                        var = 2 if c == 0 else (1 if c == 1 else 0)
                    if Lc == 64:
                        bA, bB, bres = (biasNeg[:, :] if var == 2 else biasA[:, var, kh, :]), biasB[:, kh, :], ["biasA", "biasB"]
                    else:
                        bA, bB, bres = biasA32[:, kh, :], biasB32[:, kh, :], ["biasA32", "biasB32"]
                    e1, e2 = E1[0], E2[0]
                    r1, r2 = "E1_0", "E2_0"
                    qs = slice(Lc * c, Lc * (c + 1))
                    def lmm(par, first, last):
                        ps_ = slice(par * 64, (par + 1) * 64)
                        cs_ = slice(par * NQ, (par + 1) * NQ)
                        MM(psL1[:, cs_], kT[ps_, kh, Lc * c:Lc * c + 128], qT[ps_, 4 * kh:4 * kh + 4, qs], kres + ["qT"], ["psL1"],
                           start=first, stop=last, signal=last, tp=(par * 64, 0))
                        MM(psL2[0:Lc, cs_], kT[ps_, kh, 128 + Lc * c:128 + Lc * (c + 1)], qT[ps_, 4 * kh:4 * kh + 4, qs], kres + ["qT"], ["psL2"],
                           start=first, stop=last, signal=last, tp=(par * 64, 0))
                    for par in range(2):
                        cs_ = slice(par * NQ, (par + 1) * NQ)
                        MM(psL1[:, cs_], identb[:], bA[:, cs_], ["identb"] + bres, ["psL1"], start=(par == 0), stop=False, signal=False)
                        MM(psL2[0:Lc, cs_], identb[:, 0:Lc], bB[:, cs_], ["identb"] + bres, ["psL2"], start=(par == 0), stop=False, signal=False)
                        lmm(par, False, par == 1)
                    ACTV(e1[:, 0:2 * NQ], psL1[:, 0:2 * NQ], AF.Exp, ["psL1"], [r1], scale=0.125)
                    ACTV(e2[0:Lc, 0:2 * NQ], psL2[0:Lc, 0:2 * NQ], AF.Exp, ["psL2"], [r2], scale=0.125)
                    for par in range(2):
                        ps_ = slice(par * 64, (par + 1) * 64)
                        cs_ = slice(par * NQ, (par + 1) * NQ)
                        hs = slice(kh * 64, (kh + 1) * 64)
                        MM(psO[ps_, 0:NQ], Vwin[:, c, hs], e1[:, cs_], ["Vwin", r1], ["psO"], start=True, stop=False, signal=False, tp=(0, par * 64))
                        MM(psO[ps_, 0:NQ], Vcur[0:Lc, c, hs], e2[0:Lc, cs_], ["Vcur", r2], ["psO"], start=False, stop=False, signal=False, tp=(0, par * 64))
                        MM(psO[ps_, 256:256 + NQ], ones_b[:, 0:64], e1[:, cs_], ["ones_b", r1], ["psO"], start=False, stop=False, signal=False, tp=(0, par * 64))
                        MM(psO[ps_, 256:256 + NQ], ones_b[0:Lc, 0:64], e2[0:Lc, cs_], ["ones_b", r2], ["psO"], start=False, stop=True, signal=(par == 1), tp=(0, par * 64))
                    TTO("dve", q3(rden[:, 0:NQ]), q3(psO[:, 256:256 + NQ]), es_t[:, l, kh, :].unsqueeze(2).to_broadcast([128, 4, Lc]), ALU.add, ["psO", "es_t"], ["rden"])
                    S.op("dve", "reciprocal", ["rden"], ["rden"], out=rden[:, 0:NQ], in_=rden[:, 0:NQ])
                    TTO("dve", mixed[:, 4 + 4 * kh:8 + 4 * kh, qs], q3(psO[:, 0:NQ]), q3(rden[:, 0:NQ]), ALU.mult, ["psO", "rden"],
                        [("mx", 4 + 4 * kh + j) for j in range(4)])

            for kh in range(2):
                for c in range(nchk):
                    fill.append(lambda kh=kh, c=c: att_unit(kh, c))

            def evac2(e_, pa, pr):
                m = e_ - 28
                i = m % 2
                ACTV(sgt[i][:, 0:T], pa[:, 0:T], AF.Silu, [pr], [f"sgt{i}"])
                TTO("pool" if m % 2 else "dve", mixed[:, m, 0:T], mixed[:, m, 0:T], sgt[i][:, 0:T], ALU.mult, [("mx", m), f"sgt{i}"], [("mx", m)])

            for b in range(14, 20):
                fill.append(lambda b=b: proj_block(wib, "wib", l, b, lambda kc: xn[:, kc, 0:T], lambda kc: ("xn", kc), T, evac2))

            chk(6)
            xsr = [("xs", j) for j in range(13)]
            ACTV(twal[0:64, 0:T], WA[0:64, 0:T], AF.Tanh, xsr, ["twal"])
            CP("pool", twal[64:128, 0:T], WA[64:128, 0:T], xsr, ["twal2"])
            CP("pool", STb[:], ST[:, l, :, :], ["ST"], ["STb"])

            def c3(ap):
                return ap[:, 0:T].rearrange("p (c t) -> p c t", t=Lc)

            tr = lambda i: f"tm{i}"
            for p in range(4):
                cs128 = slice(p * 128, (p + 1) * 128)
                paw, prw = next_pa()
                paa, pra = next_pa()
                MM(paw[:, 0:T], lora_b[0:64, l, cs128], twal[0:64, 0:T], ["lora_b", "twal"], [prw])
                MM(paa[:, 0:T], lora_b[64:128, l, cs128], twal[64:128, 0:T], ["lora_b", "twal2"], [pra], tp=(64, 0))
                ACTV(tm[0][:, 0:T], paw[:, 0:T], AF.Sigmoid, [prw, "v512"], [tr(0)], bias=vec(1, l, p))
                ACTV(tm[1][:, 0:T], paa[:, 0:T], AF.Sigmoid, [pra, "v512"], [tr(1)], bias=vec(2, l, p))
                S.op("dve", "tensor_tensor_scan", [tr(0), "cmask"], [tr(2)], out=tm[2][:, 0:T], data0=cmask[:, 0:T], data1=tm[0][:, 0:T], initial=0.0, op0=ALU.mult, op1=ALU.add)
                TTO("pool", tm[3][:, 0:T], tm[2][:, 0:T], tm[0][:, 0:T], ALU.subtract, [tr(2), tr(0)], [tr(3)])
                ACTV(tm[4][:, 0:T], tm[2][:, 0:T], AF.Exp, [tr(2)], [tr(4)], scale=CDEC)
                ACTV(tm[5][:, 0:T], tm[2][:, 0:T], AF.Exp, [tr(2)], [tr(5)], scale=-CDEC)
                ACTV(tm[6][:, 0:T], tm[3][:, 0:T], AF.Exp, [tr(3)], [tr(6)], scale=-CDEC)
                for hb in range(2):
                    CP("pool", glast[:, 2 * p + hb, 0:nchk], c3(tm[5])[hb * 64:(hb + 1) * 64, :, Lc - 1], [tr(5)], ["glast"])
                TTO("dve", c3(tm[7]), c3(tm[2]), c3(tm[2])[:, :, Lc - 1:Lc].to_broadcast([128, nchk, Lc]), ALU.subtract, [tr(2)], [tr(7)])
                ACTV(tm[7][:, 0:T], tm[7][:, 0:T], AF.Exp, [tr(7)], [tr(7)], scale=CDEC)
                S.op("dve", "tensor_scalar_mul", xsr + ["v512"], [tr(8)], out=tm[8][:, 0:T], in0=Kr[:, p, 0:T], scalar1=vec(3, l, p))
                ACTV(tmb[:, 0:T], tm[8][:, 0:T], AF.Square, [tr(8)], ["tmb"])
                pas, prs = next_pa()
                MM(pas[:, 0:T], bones_b[:], tmb[:, 0:T], ["bones_b", "tmb"], [prs])
                ACTV(tm[9][:, 0:T], pas[:, 0:T], AF.Sqrt, [prs], [tr(9)])
                S.op("dve", "tensor_scalar_max", [tr(9)], [tr(9)], out=tm[9][:, 0:T], in0=tm[9][:, 0:T], scalar1=1e-12)
                S.op("dve", "reciprocal", [tr(9)], [tr(9)], out=tm[9][:, 0:T], in_=tm[9][:, 0:T])
                TTO("dve", tm[8][:, 0:T], tm[8][:, 0:T], tm[9][:, 0:T], ALU.mult, [tr(8), tr(9)], [tr(8)])
                TS("dve", tm[9][:, 0:T], tm[1][:, 0:T], -1.0, vec(4, l, p), ALU.add, ALU.mult, [tr(1), "v512", tr(8)], [tr(9)])
                STT("dve", tm[9][:, 0:T], tm[9][:, 0:T], 1.0, Kr[:, p, 0:T], ALU.add, ALU.mult, [tr(9)] + xsr, [tr(9)])
                TTO("pool", tm[10][:, 0:T], Rr[:, p, 0:T], tm[9][:, 0:T], ALU.mult, [tr(9)] + xsr, [tr(10)])
                S.op("pool", "tensor_scalar_mul", [tr(10), "v512"], ["tmb2"], out=tmb2[:, 0:T], in0=tm[10][:, 0:T], scalar1=vec(5, l, p))
                pab, prb = next_pa()
                MM(pab[:, 0:T], bones_b[:], tmb2[:, 0:T], ["bones_b", "tmb2"], [prb])
                TTO("dve", bonus[:, p, 0:T], pab[:, 0:T], Vr[:, p, 0:T], ALU.mult, [prb] + xsr, ["bonus"])
                TTO("dve", tm[11][:, 0:T], tm[8][:, 0:T], tm[1][:, 0:T], ALU.mult, [tr(8), tr(1)], [tr(11)])
                for hb in range(2):
                    hp = slice(hb * 64, (hb + 1) * 64)
                    hh_ = 2 * p + hb
                    TTO("dve", rt[:, hh_, 0:T], Rr[hp, p, 0:T], tm[5][hp, 0:T], ALU.mult, [tr(5)] + xsr, ["rt"])
                    TTO("pool", kt[:, hh_, 0:T], tm[9][hp, 0:T], tm[4][hp, 0:T], ALU.mult, [tr(9), tr(4)], ["kt"])
                    TTO("pool", bt[:, hh_, 0:T], tm[11][hp, 0:T], tm[4][hp, 0:T], ALU.mult, [tr(11), tr(4)], ["bt"])
                    STT("dve", at[:, hh_, 0:T], tm[8][hp, 0:T], -1.0, tm[6][hp, 0:T], ALU.mult, ALU.mult, [tr(8), tr(6)], ["at"])
                TTO("dve", tm[0][:, 0:T], tm[9][:, 0:T], tm[7][:, 0:T], ALU.mult, [tr(9), tr(7)], [tr(0)])
                TTO("pool", tm[2][:, 0:T], tm[11][:, 0:T], tm[7][:, 0:T], ALU.mult, [tr(11), tr(7)], [tr(2)])
                for srcb, sres, dstT, dres, bank in ((Vr[:, p, :], xsr, Vt, "Vt", 0), (tm[0], [tr(0)], Kpt, "Kpt", 1), (tm[2], [tr(2)], Bpt, "Bpt", 2)):
                    for c in range(nchk):
                        TR(psR[bank][0:Lc, c * 128:(c + 1) * 128], srcb[:, c * Lc:(c + 1) * Lc], ident[:], list(sres), [f"psR{bank}"])
                    CP(evac_eng(), dstT[0:Lc, 0:nchk, cs128], psR[bank][0:Lc, 0:nchk * 128].rearrange("p (c k) -> p c k", k=128), [f"psR{bank}"], [dres])
                fill_one()

            chk(7)

            def blk(t_, hh):
                return t_[0:Lc, hh * 64:hh * 64 + Lc]

            base = lambda hh: (hh % 2) * 64

            def hT(t_, hh, c):
                return t_[:, hh, c * Lc:(c + 1) * Lc]

            def mm8(bank, lhs_fn, rhs_fn, reads, based):
                for hh in range(8):
                    MM(blk(psR[bank], hh), lhs_fn(hh), rhs_fn(hh), reads, [f"psR{bank}"], signal=(hh == 7))

            for c in range(nchk):
                mm8(0, lambda hh: hT(bt, hh, c), lambda hh: hT(at, hh, c), ["bt", "at"], True)
                mm8(1, lambda hh: hT(at, hh, c), lambda hh: hT(bt, hh, c), ["bt", "at"], True)
                mm8(2, lambda hh: hT(kt, hh, c), lambda hh: hT(at, hh, c), ["kt", "at"], True)
                chk(7.01)
                TTO("dve", v8(Pm[0][0:Lc, :]), v8(psR[0][0:Lc, :]), m8(0), ALU.mult, ["psR0", "msk"], ["Pm0"])
                chk(7.02)
                TTO("dve", v8(Qm[0][0:Lc, :]), v8(psR[1][0:Lc, :]), m8(1), ALU.mult, ["psR1", "msk"], ["Qm0"])
                TTO("dve", v8(AakT[0:Lc, :]), v8(psR[2][0:Lc, :]), m8(0), ALU.mult, ["psR2", "msk"], ["AakT"])
                chk(7.04)
                TTO("pool", v8(Rm[0][0:Lc, :]), v8(Pm[0][0:Lc, :]), m8(3), ALU.add, ["Pm0", "msk"], ["Rm0"])
                chk(7.05)
                mm8(0, lambda hh: hT(bt, hh, c), lambda hh: hT(rt, hh, c), ["bt", "rt"], True)
                mm8(1, lambda hh: hT(kt, hh, c), lambda hh: hT(rt, hh, c), ["kt", "rt"], True)
                TTO("dve", v8(ArbT[0:Lc, :]), v8(psR[0][0:Lc, :]), m8(2), ALU.mult, ["psR0", "msk"], ["ArbT"])
                TTO("dve", v8(ArkT[0:Lc, :]), v8(psR[1][0:Lc, :]), m8(2), ALU.mult, ["psR1", "msk"], ["ArkT"])
                chk(7.1)
                for s_ in range(1, K + 1):
                    cur, nxt = (s_ - 1) % 2, s_ % 2
                    ro, rn = (s_ - 2) % 2, (s_ - 1) % 2
                    if s_ <= K - 2:
                        mm8(0, lambda hh: blk(Qm[cur], hh), lambda hh: blk(Pm[cur], hh), [f"Qm{cur}", f"Pm{cur}"], False)
                    if s_ <= K - 1:
                        mm8(1, lambda hh: blk(Pm[cur], hh), lambda hh: blk(Qm[cur], hh), [f"Qm{cur}", f"Pm{cur}"], False)
                    if s_ >= 2:
                        mm8(2, lambda hh: blk(Qm[cur], hh), lambda hh: blk(Rm[ro], hh), [f"Qm{cur}", f"Rm{ro}"], False)
                    if s_ % 2 == 1:
                        fill_one()
                    if s_ <= K - 2:
                        CP("act", v8(Pm[nxt][0:Lc, :]), v8(psR[0][0:Lc, :]), ["psR0"], [f"Pm{nxt}"])
                    if s_ <= K - 1:
                        CP("act", v8(Qm[nxt][0:Lc, :]), v8(psR[1][0:Lc, :]), ["psR1"], [f"Qm{nxt}"])
                    if s_ >= 2:
                        TTO("dve", v8(Rm[rn][0:Lc, :]), v8(psR[2][0:Lc, :]), v8(Rm[ro][0:Lc, :]), ALU.add, ["psR2", f"Rm{ro}"], [f"Rm{rn}"])
                MT = Rm[(K - 1) % 2]
                mres = f"Rm{(K - 1) % 2}"
                chk(7.2)
                for hh in range(8):
                    hs = slice(hh * 64, (hh + 1) * 64)
                    b_ = base(hh)
                    MM(psR[0][0:Lc, hs], blk(AakT, hh), Vt[0:Lc, c, hs], ["AakT", "Vt"], ["psR0"], start=True, stop=False, signal=False)
                    MM(psR[0][0:Lc, hs], hT(at, hh, c), STb[:, hh, :], ["at", "STb"], ["psR0"], start=False, stop=True, signal=(hh == 7))
                CP("act", RHSb[0:Lc, :], psR[0][0:Lc, :], ["psR0"], ["RHSb"])
                for hh in range(8):
                    hs = slice(hh * 64, (hh + 1) * 64)
                    MM(psR[1][0:Lc, hs], blk(MT, hh), RHSb[0:Lc, hs], [mres, "RHSb"], ["psR1"], signal=(hh == 7))
                CP("act", SAb[0:Lc, :], psR[1][0:Lc, :], ["psR1"], ["SAb"])
                chk(7.3)
                for hh in range(8):
                    hs = slice(hh * 64, (hh + 1) * 64)
                    b_ = base(hh)
                    oo = psR[2][b_:b_ + 64, (hh // 2) * 64:(hh // 2) * 64 + Lc]
                    MM(oo, STb[:, hh, :], hT(rt, hh, c), ["STb", "rt"], ["psR2"], start=True, stop=False, signal=False, tp=(0, b_))
                    MM(oo, SAb[0:Lc, hs], blk(ArbT, hh), ["SAb", "ArbT"], ["psR2"], start=False, stop=False, signal=False, tp=(0, b_))
                    MM(oo, Vt[0:Lc, c, hs], blk(ArkT, hh), ["Vt", "ArkT"], ["psR2"], start=False, stop=True, signal=(hh == 7), tp=(0, b_))
                CP("act", OT[:, :, c * Lc:(c + 1) * Lc], v4o(psR[2][:, 0:256]), ["psR2"], ["OT"])
                chk(7.4)
                for hh in range(8):
                    hs = slice(hh * 64, (hh + 1) * 64)
                    b_ = base(hh)
                    oo = psR[0][0:64, hs]
                    MM(oo, Bpt[0:Lc, c, hs], SAb[0:Lc, hs], ["Bpt", "SAb"], ["psR0"], start=True, stop=False, signal=False)
                    MM(oo, Kpt[0:Lc, c, hs], Vt[0:Lc, c, hs], ["Kpt", "Vt"], ["psR0"], start=False, stop=True, signal=(hh == 7))
                TTO("pool", STg[:], ST[:, l, :, :], glast[:, :, c:c + 1].to_broadcast([64, 8, 64]), ALU.mult, ["ST", "glast"], ["STg"])
                ps_s = psR[0][0:64, :].rearrange("p (a i) -> p a i", i=64)
                TTO("dve", STb[:], STg[:], ps_s, ALU.add, ["STg", "psR0"], ["STb"])
                TTO("dve", ST[:, l, :, :], STg[:], ps_s, ALU.add, ["STg", "psR0"], ["ST"])

            chk(8)
            fill_one(len(fill))
            for p in range(4):
                pam, prm = next_pa()
                MM(pam[:, 0:T], bmean[:], OT[:, p, 0:T], ["bmean", "OT"], [prm])
                TTO("dve", tm[0][:, 0:T], OT[:, p, 0:T], pam[:, 0:T], ALU.subtract, ["OT", prm], [tr(0)])
                ACTV(tm[1][:, 0:T], tm[0][:, 0:T], AF.Square, [tr(0)], [tr(1)])
                pav, prv = next_pa()
                MM(pav[:, 0:T], bmean[:], tm[1][:, 0:T], ["bmean", tr(1)], [prv])
                ACTV(tm[2][:, 0:T], pav[:, 0:T], AF.Sqrt, [prv], [tr(2)], bias=64e-5)
                S.op("dve", "reciprocal", [tr(2)], [tr(2)], out=tm[2][:, 0:T], in_=tm[2][:, 0:T])
                TTO("dve", tm[0][:, 0:T], tm[0][:, 0:T], tm[2][:, 0:T], ALU.mult, [tr(0), tr(2)], [tr(0)])
                TS("dve", tm[0][:, 0:T], tm[0][:, 0:T], vec(6, l, p), vec(7, l, p), ALU.mult, ALU.add, [tr(0), "v512"], [tr(0)])
                TTO("dve", mixed[:, 12 + p, 0:T], tm[0][:, 0:T], bonus[:, p, 0:T], ALU.add, [tr(0), "bonus"], [("mx", 12 + p)])

            chk(9)
            proj(wib, "wib", l, range(20, 22), lambda kc: xn[:, kc, 0:T], lambda kc: ("xn", kc), T, evac2)

            chk(10)
            def evac3(e_, pa, pr):
                TTO("dve", h[:, e_, 0:T], h[:, e_, 0:T], pa[:, 0:T], ALU.add, [pr, ("h", e_)], [("h", e_)])

            proj(wob, "wob", l, range(0, NBLK_OUT), lambda kc: mixed[:, kc, 0:T], lambda kc: ("mx", kc), T, evac3)

        def load_x(src, row0, T):
            nb = (T + 127) // 128
            for tb in range(nb):
                n = min(128, T - tb * 128)
                for hf in range(2):
                    S.dma("sp", "x", stage[0:n, :], src[row0 + tb * 128:row0 + tb * 128 + n, hf * 1024:(hf + 1) * 1024], writes=["stage"])
                    for k8 in range(8):
                        kc = hf * 8 + k8
                        pa, pr = next_pa()
                        TR(pa[:, 0:n], stage[0:n, k8 * 128:(k8 + 1) * 128], ident[0:n, 0:n], ["stage"], [pr])
                        CP(evac_eng(), h[:, kc, tb * 128:tb * 128 + n], pa[:, 0:n], [pr], [("h", kc)])

        def store_y(dst, row0, T):
            rms(T)
            nb = (T + 127) // 128
            for tb in range(nb):
                n = min(128, T - tb * 128)
                ts = slice(tb * 128, tb * 128 + n)
                for hf in range(2):
                    for k8 in range(8):
                        kc = hf * 8 + k8
                        t_ = tm[kc % 4]
                        STT("dve", t_[:, 0:n], h[:, kc, ts], nw[:, L * 16 + kc:L * 16 + kc + 1], rstd[:, ts], ALU.mult, ALU.mult, [("h", kc), "rstd", "nw"], [f"tm{kc % 4}"])
                        pa, pr = next_pa()
                        TR(pa[0:n, 0:128], t_[:, 0:n], ident[:], [f"tm{kc % 4}"], [pr])
                        CP(evac_eng(), stage[0:n, k8 * 128:(k8 + 1) * 128], pa[0:n, 0:128], [pr], ["stage"])
                    S.dma("pool", "yout", dst[row0 + tb * 128:row0 + tb * 128 + n, hf * 1024:(hf + 1) * 1024], stage[0:n, :], reads=["stage"])

        def layer_state_kv(g, l, T):
            o = outs["p" if g == 0 else "s"]
            for nm, srcb, sres, ti_ in (("k", k32, ["k32", "k32_h"], 10), ("v", vT, ["vT", "vT_h"], 11)):
                pa, pr = next_pa()
                TR(pa[:, 0:128], srcb[:, T:T + 128], ident[:], sres, [pr])
                CP("dve", tm[ti_][:, 0:128], pa[:, 0:128], [pr], [f"tm{ti_}"])
                S.dma("pool", "sout", o[nm][l], tm[ti_][:, 0:128], reads=[f"tm{ti_}"])

        def store_states(g):
            o = outs["p" if g == 0 else "s"]
            for l in range(L):
                for gi in range(4):
                    pa, pr = next_pa()
                    TR(pa[0:16, 0:128], hist[:, l, gi, :], ident[:], ["hist"], [pr])
                    CP("dve", stage[0:16, gi * 128:(gi + 1) * 128], pa[0:16, 0:128], [pr], ["stage"])
                S.dma("pool", "sout", o["pool"][l], stage[1:16, 0:512], reads=["stage"])
                pa, pr = next_pa()
                TR(pa[0:13, 0:128], shst[:, l, :], ident[:], shres, [pr])
                CP("dve", stage[0:13, 512:640], pa[0:13, 0:128], [pr], ["stage"])
                S.dma("pool", "sout", o["shift"][l], stage[0:13, 512:640], reads=["stage"])
                for q4 in range(2):
                    pa, pr = next_pa()
                    for hq in range(4):
                        TR(pa[0:64, hq * 64:(hq + 1) * 64], ST[:, l, 4 * q4 + hq, :], ident[0:64, 0:64], ["ST"], [pr])
                    CP("dve", stage[0:64, 512 + q4 * 256:512 + (q4 + 1) * 256], pa[0:64, 0:256], [pr], ["stage"])
                S.dma("pool", "sout", o["wkv"][l].rearrange("h i j -> i h j"), stage[0:64, 512:1024].rearrange("i (h j) -> i h j", h=8), reads=["stage"])

        def load_sample_states():
            for l in range(L):
                S.dma("sp", "ld", stage[0:15, 0:512], st_pool[l], writes=["stage"])
                for gi in range(4):
                    pa, pr = next_pa()
                    TR(pa[:, 0:15], stage[0:15, gi * 128:(gi + 1) * 128], ident[0:15, 0:15], ["stage"], [pr])
                    CP("dve", hist[:, l, gi, 1:16], pa[:, 0:15], [pr], ["hist"])
                S.dma("sp", "ld", stage[:, 512:640], st_k[l], writes=["stage"])
                S.dma("sp", "ld", stage[:, 640:768], st_v[l], writes=["stage"])
                pa, pr = next_pa()
                TR(pa[:, 0:128], stage[:, 512:640], ident[:], ["stage"], [pr])
                CP("dve", k32halo[:, l, :], pa[:, 0:128], [pr], ["k32halo"])
                for kh in range(2):
                    for half in range(2):
                        CP("dve", khalo[half * 64:(half + 1) * 64, l, kh, :], pa[kh * 64:(kh + 1) * 64, 0:128], [pr], ["khalo"])
                pa, pr = next_pa()
                TR(pa[:, 0:128], stage[:, 640:768], ident[:], ["stage"], [pr])
                CP("dve", vhalo[:, l, :], pa[:, 0:128], [pr], ["vhalo"])
                S.dma("sp", "ld", stage[0:13, 768:896], st_shift[l], writes=["stage"])
                pa, pr = next_pa()
                TR(pa[:, 0:13], stage[0:13, 768:896], ident[0:13, 0:13], ["stage"], [pr])
                CP("dve", shst[:, l, :], pa[:, 0:13], [pr], shres)
                S.dma("sp", "ld", stage[0:64, 0:512].rearrange("i (h j) -> i h j", h=8), st_wkv[l].rearrange("h i j -> i h j"), writes=["stage"])
                for q4 in range(2):
                    pa, pr = next_pa()
                    for hq in range(4):
                        hh_ = 4 * q4 + hq
                        TR(pa[0:64, hq * 64:(hq + 1) * 64], stage[0:64, hh_ * 64:(hh_ + 1) * 64], ident[0:64, 0:64], ["stage"], [pr])
                    CP("dve", ST[:, l, 4 * q4:4 * q4 + 4, :], pa[0:64, 0:256].rearrange("p (a i) -> p a i", i=64), [pr], ["ST"])

        def main_program():
            for ti in range(NT):
                load_x(xp, ti * TT, TT)
                chk(2)
                for l in range(L):
                    layer(l, 0, TT, 64, ti == 0)
                    if ti == NT - 1:
                        layer_state_kv(0, l, TT)
                store_y(yp, ti * TT, TT)
                chk(20)
            store_states(0)
            chk(21)
            load_sample_states()
            load_x(xs, 0, DEC)
            chk(22)
            for l in range(L):
                layer(l, 1, DEC, 32, False)
                layer_state_kv(1, l, DEC)
            store_y(ys, 0, DEC)
            store_states(1)

        try:
            chk(1)
            main_program()
        except _Stop:
            pass
        for _i in range(int(os.environ.get("KEXTRA", "0"))):
            S.dma("sp", "ld", stage[0:32, 0:16], table[:, :], writes=["stage"])
        S.finish("pool")
        S.finish("sp")
        S.emit(block)
        print("[kernel] dma counts", {k: sum(1 for it in v if it[0] == "o" and it[1] == "dma_start") for k, v in S.q.items()}, "sem max", {k: S.cnt[k] for k in S.ENG}, flush=True)
        print(f"[kernel] ops={S.nops} q=" + ",".join(f"{k}:{len(v)}" for k, v in S.q.items()), flush=True)
    return nc


_NC_CACHE = {}


def _run(inputs, SEQ, DEPTH, n_prompt, n_sample, n_cores=8):
    L = DEPTH
    key = (SEQ, DEPTH)
    if key not in _NC_CACHE:
        _NC_CACHE[key] = build_nc(SEQ, DEPTH)
    nc = _NC_CACHE[key]
    f = lambda a: np.ascontiguousarray(np.asarray(a, dtype=np.float32))
    I = {k: f(v) for k, v in inputs.items()}
    consts = host_consts()
    vec512 = np.stack([I["pool_scale"], I["rwkv_w0"], I["rwkv_a0"], I["rwkv_k_k"], I["rwkv_k_a"], I["rwkv_r_k"].reshape(L, 512),
                       I["rwkv_lnx_w"], I["rwkv_lnx_b"]], axis=0).reshape(8 * L * 4, 128)
    shared = dict(
        norm_w=I["norm_w"].reshape(L * 16, 128), fnorm_w=I["final_norm_w"].reshape(16, 128), w_in=I["w_in"], w_out=I["w_out"],
        pool_w=I["pool_w"], vec512=np.ascontiguousarray(vec512), sinks=I["attn_sinks"].reshape(1, L * 16), table=I["rel_bias_table"],
        mu=I["rwkv_mu"].reshape(L * 13, 128), w_up=I["rwkv_w_up"], a_up=I["rwkv_a_up"], **consts)
    in_maps = []
    for c in range(n_cores):
        bp = c % n_prompt
        bs = c % n_sample
        m = dict(shared)
        m["xp"] = I["x_prompt"][bp]
        m["xs"] = I["x_sample"][bs]
        m["st_pool"] = np.ascontiguousarray(I["state_pool"][:, bs])
        m["st_k"] = np.ascontiguousarray(I["cache_swa_k"][:, bs].reshape(L, 128, 128))
        m["st_v"] = np.ascontiguousarray(I["cache_swa_v"][:, bs].reshape(L, 128, 128))
        m["st_shift"] = np.ascontiguousarray(I["state_rwkv_shift"][:, bs].reshape(L, 13, 128))
        m["st_wkv"] = np.ascontiguousarray(I["state_rwkv_wkv"][:, bs])
        in_maps.append(m)
    res = run_bass_kernel_spmd(nc, in_maps, core_ids=list(range(n_cores)))
    R = res.results
    n_prompt = min(n_prompt, n_cores)
    n_sample = min(n_sample, n_cores)
    y_prompt = np.stack([R[b]["yp"] for b in range(n_prompt)], axis=0)
    y_sample = np.stack([R[b]["ys"] for b in range(n_sample)], axis=0)

    def gather(g, n):
        pool = np.stack([R[b][f"o_pool_{g}"] for b in range(n)], axis=1)
        k = np.stack([R[b][f"o_k_{g}"].reshape(L, 128, 2, 64) for b in range(n)], axis=1)
        v = np.stack([R[b][f"o_v_{g}"].reshape(L, 128, 2, 64) for b in range(n)], axis=1)
        sh = np.stack([R[b][f"o_shift_{g}"].reshape(L, DSH) for b in range(n)], axis=1)
        wkv = np.stack([R[b][f"o_wkv_{g}"] for b in range(n)], axis=1)
        return [pool, k, v, sh, wkv]

    outs = [y_prompt, y_sample] + gather("p", n_prompt) + gather("s", n_sample)
    return tuple(np.ascontiguousarray(o.astype(np.float32)) for o in outs)


def kernel(**inputs):
    return _run(inputs, 8192, 4, 4, 8)
```

```python
import math
from contextlib import ExitStack

import numpy as np
import concourse.bass as bass
import concourse.mybir as mybir
from concourse.bass_utils import run_bass_kernel_spmd

F32 = mybir.dt.float32
BF16 = mybir.dt.bfloat16
AF = mybir.ActivationFunctionType
ALU = mybir.AluOpType

D = 2048
NCH = 16
TT = 256
DEC = 32
NBLK_IN = 22
NBLK_OUT = 8
DSH = 1664
CDEC = math.exp(-0.5)
NEGB = -30000.0
NSLOT = 3


class Sched:
    ENG = ("pe", "act", "dve", "pool", "sp")
    ROT = {"ld": 6, "cast": 8, "x": 2, "yout": 2, "sout": 4}

    def __init__(self, sems, dma_sems):
        self.rot = {}
        self.sem = sems
        self.cnt = {k: 0 for k in self.ENG}
        self.seen = {k: {} for k in self.ENG}
        self.dma_sems = dma_sems
        self.dma_map = {}
        self.lastw = {}
        self.reads = {}
        self.q = {k: [] for k in self.ENG}
        self.nops = 0

    def _wait(self, eng, deps):
        best = {}
        for ev in deps:
            if ev is None:
                continue
            k, s, v = ev
            if k not in best or best[k][1] < v:
                best[k] = (s, v)
        for k, (s, v) in best.items():
            if self.seen[eng].get(k, 0) >= v:
                continue
            if k == eng and eng == "pe":
                continue
            if k == eng and v > self.cnt[eng]:
                raise RuntimeError(f"self-deadlock on {eng}")
            self.q[eng].append(("w", s, v))
            self.seen[eng][k] = v

    def _deps(self, reads, writes):
        deps = []
        for r in reads:
            deps.append(self.lastw.get(r))
        for w in writes:
            deps.append(self.lastw.get(w))
            deps.extend(self.reads.get(w, {}).values())
        return deps

    def _commit(self, ev, reads, writes):
        k = ev[0]
        for r in reads:
            self.reads.setdefault(r, {})[k] = ev
        for w in writes:
            self.lastw[w] = ev
            self.reads[w] = {}

    def op(self, eng, meth, reads, writes, signal=True, **kw):
        writes = list(writes) + [r for r in reads if isinstance(r, str) and r[:2] in ("ps", "pa") and r not in writes]
        self._wait(eng, self._deps(reads, writes))
        ev = (eng, self.sem[eng], self.cnt[eng] + 1)
        if signal:
            self.q[eng].append(("o", meth, kw, self.sem[eng], 1))
            self.cnt[eng] += 1
        else:
            self.q[eng].append(("o", meth, kw, None, 0))
        self._commit(ev, reads, writes)
        self.nops += 1

    def dma(self, eng, key, out, in_, reads=(), writes=()):
        if isinstance(key, str) and key in self.ROT:
            n = self.rot.get(key, 0)
            self.rot[key] = n + 1
            key = f"{key}{n % self.ROT[key]}"
        if key not in self.dma_map:
            self.dma_map[key] = [self.dma_sems.pop(), 0]
        ent = self.dma_map[key]
        prev = [("dma:" + str(key), ent[0], ent[1])] if ent[1] else []
        self._wait(eng, self._deps(reads, writes) + prev)
        ent[1] += 16
        self.q[eng].append(("o", "dma_start", dict(out=out, in_=in_), ent[0], 16))
        ev = ("dma:" + str(key), ent[0], ent[1])
        self._commit(ev, reads, writes)
        self.nops += 1

    def finish(self, eng):
        deps = []
        for key, (s, v) in self.dma_map.items():
            if v:
                deps.append(("dma:" + str(key), s, v))
        for k in self.ENG:
            if self.cnt[k] and k != eng:
                deps.append((k, self.sem[k], self.cnt[k]))
        self._wait(eng, deps)

    def replay(self, eng, e):
        for it in self.q[eng]:
            if it[0] == "w":
                e.wait_ge(it[1], it[2])
            else:
                ins = getattr(e, it[1])(**it[2])
                if it[3] is not None:
                    ins.then_inc(it[3], it[4])

    def emit(self, block):
        @block.tensor
        def _(e):
            self.replay("pe", e)

        @block.scalar
        def _(e):
            self.replay("act", e)

        @block.vector
        def _(e):
            self.replay("dve", e)

        @block.gpsimd
        def _(e):
            self.replay("pool", e)

        @block.sync
        def _(e):
            self.replay("sp", e)


def _bucket_np(rel):
    nb, me = 16, 8
    ret = np.where(rel > 0, nb, 0)
    n = np.abs(rel)
    nf = np.maximum(n, 1).astype(np.float32)
    large = me + (np.log(nf / np.float32(me)) / np.float32(math.log(128 / me)) * np.float32(nb - me)).astype(np.int32)
    large = np.minimum(large, nb - 1)
    return ret + np.where(n < me, n, large)


def host_consts():
    c = {}
    s = np.arange(192)[:, None]
    i = np.arange(64)[None, :]
    b = _bucket_np((s - 128 - i).astype(np.int32))
    oh = np.zeros((32, 64, 192), np.float32)
    for ii in range(64):
        oh[b[:, ii], ii, np.arange(192)] = 1.0
    c["c_oh"] = oh
    t = np.arange(64)
    su = (t[:, None] < t[None, :]).astype(np.float32)
    c["c_msk"] = np.ascontiguousarray(np.stack([su, su.T, (t[:, None] <= t[None, :]).astype(np.float32), np.eye(64, dtype=np.float32)], axis=1))
    c["c_ident"] = np.eye(128, dtype=np.float32)
    bo = np.zeros((128, 128), np.float32)
    bo[:64, :64] = 1.0
    bo[64:, 64:] = 1.0
    c["c_bones"] = bo
    cm = np.ones((128, TT), np.float32)
    cm[:, ::64] = 0.0
    c["c_cmask"] = cm
    rc = np.zeros((128, 4, 16), np.float32)
    for g, w in enumerate((2, 4, 8, 16)):
        rc[:, g, :] = 1.0 / np.minimum(np.arange(16) + 1, w)
    c["c_rcnt"] = rc
    return c


class _Stop(Exception):
    pass


def build_nc(SEQ, DEPTH):
    import os
    KSTOP = float(os.environ.get("KSTOP", "0"))

    def chk(level):
        if KSTOP and level >= KSTOP:
            raise _Stop()

    L = DEPTH
    NT = SEQ // TT
    nc = bass.Bass("TRN2", target_bir_lowering=False)
    dt_in = lambda n, sh: nc.dram_tensor(n, sh, F32, kind="ExternalInput").ap()
    dt_out = lambda n, sh: nc.dram_tensor(n, sh, F32, kind="ExternalOutput").ap()
    xp = dt_in("xp", [SEQ, D])
    xs = dt_in("xs", [DEC, D])
    st_pool = dt_in("st_pool", [L, 15, 512])
    st_k = dt_in("st_k", [L, 128, 128])
    st_v = dt_in("st_v", [L, 128, 128])
    st_shift = dt_in("st_shift", [L, 13, 128])
    st_wkv = dt_in("st_wkv", [L, 8, 64, 64])
    norm_w = dt_in("norm_w", [L * 16, 128])
    fnorm_w = dt_in("fnorm_w", [16, 128])
    w_in = dt_in("w_in", [L, D, 5504])
    w_out = dt_in("w_out", [L, D, D])
    pool_w = dt_in("pool_w", [L, 4, 128, 128])
    vec512 = dt_in("vec512", [8 * L * 4, 128])
    sinks = dt_in("sinks", [1, L * 16])
    table = dt_in("table", [32, 16])
    mu = dt_in("mu", [L * 13, 128])
    w_up = dt_in("w_up", [L, 64, 512])
    a_up = dt_in("a_up", [L, 64, 512])
    c_oh = dt_in("c_oh", [32, 64, 192])
    c_msk = dt_in("c_msk", [64, 4, 64])
    c_ident = dt_in("c_ident", [128, 128])
    c_bones = dt_in("c_bones", [128, 128])
    c_cmask = dt_in("c_cmask", [128, TT])
    c_rcnt = dt_in("c_rcnt", [128, 4, 16])

    yp = dt_out("yp", [SEQ, D])
    ys = dt_out("ys", [DEC, D])
    outs = {}
    for g in ("p", "s"):
        outs[g] = dict(
            pool=dt_out(f"o_pool_{g}", [L, 15, 512]),
            k=dt_out(f"o_k_{g}", [L, 128, 128]),
            v=dt_out(f"o_v_{g}", [L, 128, 128]),
            shift=dt_out(f"o_shift_{g}", [L, 13, 128]),
            wkv=dt_out(f"o_wkv_{g}", [L, 8, 64, 64]),
        )
    wib = nc.dram_tensor("wib", [L, NBLK_IN, 128, NCH, 256], BF16, kind="Internal").ap()
    wob = nc.dram_tensor("wob", [L, NBLK_OUT, 128, NCH, 256], BF16, kind="Internal").ap()

    with ExitStack() as es:
        def SB(n, sh, dt=F32):
            return es.enter_context(nc.sbuf_tensor(n, sh, dt))

        def PS(n, sh, dt=F32):
            return es.enter_context(nc.psum_tensor(n, sh, dt))

        ident = SB("ident", [128, 128])
        identb = SB("identb", [128, 128], BF16)
        bones_b = SB("bones_b", [128, 128], BF16)
        bmean = SB("bmean", [128, 128])
        ones_b = SB("ones_b", [128, 128], BF16)
        msk = SB("msk", [64, 4, 64])
        cmask = SB("cmask", [128, TT])
        rcnt = SB("rcnt", [128, 4, 16])
        biasA = SB("biasA", [128, 2, 2, 512], BF16)
        biasNeg = SB("biasNeg", [128, 512], BF16)
        biasB = SB("biasB", [128, 2, 512], BF16)
        nw = SB("nw", [128, L * 16 + 16])
        v512 = SB("v512", [128, 8 * L * 4])
        muT = SB("muT", [128, L * 13])
        omu = SB("omu", [128, L * 13])
        es_raw = SB("es_raw", [128, L * 16])
        es_t = SB("es_t", [128, L, 2, 4])
        poolw_b = SB("poolw_b", [128, L * 4, 128], BF16)
        lora_b = SB("lora_b", [128, L, 512], BF16)
        stage = SB("stage", [128, 1024])
        tab8 = SB("tab8", [32, 16])
        tab8p = SB("tab8p", [32, 2, 2, 4])
        biasA32 = SB("biasA32", [128, 2, 256], BF16)
        biasB32 = SB("biasB32", [128, 2, 256], BF16)

        def vec(vi, l, c):
            j = (vi * L + l) * 4 + c
            return v512[:, j:j + 1]

        h = SB("h", [128, NCH, TT])
        xn = SB("xn", [128, NCH, TT], BF16)
        mixed = SB("mixed", [128, NCH, TT], BF16)
        rstd = SB("rstd", [128, TT])
        wslot = [SB(f"wslot{i}", [128, NCH, 256], BF16) for i in range(NSLOT)]
        pext = SB("pext", [128, 4, 16 + TT])
        ptA = SB("ptA", [128, 4, 16 + TT])
        ptB = SB("ptB", [128, 4, 16 + TT])
        pd = SB("pd", [128, 4, TT], BF16)
        qT = SB("qT", [128, 8, TT], BF16)
        kT = SB("kT", [128, 2, 128 + TT], BF16)
        vT = SB("vT", [128, 128 + TT])
        k32 = SB("k32", [128, 128 + TT])
        Vwin = SB("Vwin", [128, 4, 128], BF16)
        Vcur = SB("Vcur", [64, 4, 128], BF16)
        E1 = [SB(f"E1_{i}", [128, 512], BF16) for i in range(1)]
        E2 = [SB(f"E2_{i}", [64, 512], BF16) for i in range(1)]
        rden = SB("rden", [128, 256])
        sgt = [SB(f"sgt{i}", [128, TT]) for i in range(2)]
        Rr = SB("Rr", [128, 4, TT])
        Kr = SB("Kr", [128, 4, TT])
        Vr = SB("Vr", [128, 4, TT])
        WA = SB("WA", [128, TT])
        twal = SB("twal", [128, TT], BF16)
        tm = [SB(f"tm{i}", [128, TT]) for i in range(12)]
        tmb = SB("tmb", [128, TT], BF16)
        tmb2 = SB("tmb2", [128, TT], BF16)
        rt = SB("rt", [64, 8, TT], BF16)
        kt = SB("kt", [64, 8, TT], BF16)
        bt = SB("bt", [64, 8, TT], BF16)
        at = SB("at", [64, 8, TT], BF16)
        glast = SB("glast", [64, 8, 4])
        bonus = SB("bonus", [128, 4, TT])
        Vt = SB("Vt", [64, 4, 512], BF16)
        Kpt = SB("Kpt", [64, 4, 512], BF16)
        Bpt = SB("Bpt", [64, 4, 512], BF16)
        Pm = [SB(f"Pm{i}", [64, 512], BF16) for i in range(2)]
        Qm = [SB(f"Qm{i}", [64, 512], BF16) for i in range(2)]
        Rm = [SB(f"Rm{i}", [64, 512], BF16) for i in range(2)]
        AakT = SB("AakT", [64, 512], BF16)
        ArbT = SB("ArbT", [64, 512], BF16)
        ArkT = SB("ArkT", [64, 512], BF16)
        RHSb = SB("RHSb", [64, 512], BF16)
        SAb = SB("SAb", [64, 512], BF16)
        STg = SB("STg", [64, 8, 64])
        STb = SB("STb", [64, 8, 64], BF16)
        OT = SB("OT", [128, 4, TT])
        hist = SB("hist", [128, L, 4, 16])
        khalo = SB("khalo", [128, L, 2, 128], BF16)
        k32halo = SB("k32halo", [128, L, 128])
        vhalo = SB("vhalo", [128, L, 128])
        shst = SB("shst", [128, L, 13])
        ST = SB("ST", [64, L, 8, 64])

        psA = [PS(f"psA{i}", [128, 2, 256]) for i in range(2)]
        psL1 = PS("psL1", [128, 512])
        psL2 = PS("psL2", [128, 512])
        psO = PS("psO", [128, 512])
        psR = [PS(f"psR{i}", [128, 512]) for i in range(3)]

        sems = {k: es.enter_context(nc.semaphore("s_" + k)) for k in Sched.ENG}
        dsems = [es.enter_context(nc.semaphore(f"dq{i}")) for i in range(40)]
        block = es.enter_context(nc.Block())
        S = Sched(sems, dsems)

        pa_ctr = [0]

        def next_pa():
            i = pa_ctr[0] % 2
            pa_ctr[0] += 1
            return psA[i][:, 0, :], f"pa{i}"

        ev_ctr = [0]

        def evac_eng():
            ev_ctr[0] += 1
            return "act" if ev_ctr[0] % 2 else "dve"

        def CP(eng, out, in_, r, w):
            S.op(eng, "copy" if eng == "act" else "tensor_copy", r, w, out=out, in_=in_)

        def TTO(eng, out, in0, in1, op, r, w):
            S.op(eng, "tensor_tensor", r, w, out=out, in0=in0, in1=in1, op=op)

        def STT(eng, out, in0, scalar, in1, op0, op1, r, w):
            S.op(eng, "scalar_tensor_tensor", r, w, out=out, in0=in0, scalar=scalar, in1=in1, op0=op0, op1=op1)

        def TS(eng, out, in0, s1, s2, op0, op1, r, w):
            S.op(eng, "tensor_scalar", r, w, out=out, in0=in0, scalar1=s1, scalar2=s2, op0=op0, op1=op1)

        def ACTV(out, in_, func, r, w, **kw):
            S.op("act", "activation", r, w, out=out, in_=in_, func=func, **kw)

        def MM(out, lhsT, rhs, r, w, start=True, stop=True, signal=True, tp=None):
            kw = dict(out=out, lhsT=lhsT, rhs=rhs, start=start, stop=stop)
            if tp is not None:
                kw["tile_position"] = tp
            S.op("pe", "matmul", r, w, signal=signal, **kw)

        def TR(out, in_, idn, r, w):
            S.op("pe", "transpose", r + ["ident"], w, out=out, in_=in_, identity=idn)

        def MSET(eng, ap, val, r, w):
            S.op(eng, "memset", r, w, ap=ap, constant=val)

        ci = [0]

        def cast_dma(dst, src, res):
            S.dma("pool", "cast", dst, src, writes=[res])
            ci[0] += 1

        def ext_src(e):
            if e < 4:
                return e * 128
            if e < 12:
                return 1024 + (e - 4) * 128
            if e < 14:
                return None
            if e == 14:
                return 2176
            if e < 28:
                return 3328 + (e - 15) * 128
            if e < 32:
                return 512 + (e - 28) * 128
            if e < 40:
                return 2304 + (e - 32) * 128
            return 4992 + (e - 40) * 128

        for l in range(L):
            wv = w_in[l].rearrange("(kc p) c -> p kc c", p=128)
            for b in range(NBLK_IN):
                res = ("wib", l, b)
                s0, s1 = ext_src(2 * b), ext_src(2 * b + 1)
                if s0 is None:
                    for dc, sc in ((0, 2048), (64, 2048), (128, 2112), (192, 2112)):
                        cast_dma(wib[l, b, :, :, dc:dc + 64], wv[:, :, sc:sc + 64], res)
                elif s1 == s0 + 128:
                    cast_dma(wib[l, b], wv[:, :, s0:s0 + 256], res)
                else:
                    cast_dma(wib[l, b, :, :, 0:128], wv[:, :, s0:s0 + 128], res)
                    cast_dma(wib[l, b, :, :, 128:256], wv[:, :, s1:s1 + 128], res)
            wo = w_out[l].rearrange("(kc p) c -> p kc c", p=128)
            for b in range(NBLK_OUT):
                cast_dma(wob[l, b], wo[:, :, b * 256:(b + 1) * 256], ("wob", l, b))

        def ld(dst, src, res):
            S.dma("sp", "ld", dst, src, writes=[res])

        ld(ident[:], c_ident[:, :], "ident")
        ld(msk[:], c_msk[:, :, :], "msk")
        ld(cmask[:], c_cmask[:, :], "cmask")
        ld(rcnt[:], c_rcnt[:, :, :], "rcnt")
        ld(bmean[:], c_bones[:, :], "bmean")
        CP("dve", identb[:], ident[:], ["ident"], ["identb"])
        CP("dve", bones_b[:], bmean[:], ["bmean"], ["bones_b"])
        S.op("dve", "tensor_scalar_mul", ["bmean", "bones_b"], ["bmean"], out=bmean[:], in0=bmean[:], scalar1=1.0 / 64.0)
        MSET("pool", ones_b[:], 1.0, [], ["ones_b"])

        def load_T(dst, src, nrows, res):
            for r0 in range(0, nrows, 128):
                n = min(128, nrows - r0)
                S.dma("sp", "ld", stage[0:n, 0:128], src[r0:r0 + n, :], writes=["stage"])
                pa, pr = next_pa()
                TR(pa[:, 0:n], stage[0:n, 0:128], ident[0:n, 0:n], ["stage"], [pr])
                CP("dve", dst[:, r0:r0 + n], pa[:, 0:n], [pr], [res])

        load_T(nw[:, 0:L * 16], norm_w, L * 16, "nw")
        load_T(nw[:, L * 16:L * 16 + 16], fnorm_w, 16, "nw")
        load_T(v512, vec512, 8 * L * 4, "v512")
        load_T(muT, mu, L * 13, "muT")
        TS("dve", omu[:], muT[:], -1.0, 1.0, ALU.mult, ALU.add, ["muT"], ["omu"])

        S.dma("sp", "ld", es_raw[:], sinks.to_broadcast([128, L * 16]), writes=["es_raw"])
        ACTV(es_raw[:], es_raw[:], AF.Exp, ["es_raw"], ["es_raw"])
        for par in range(2):
            CP("dve", es_t[par * 64:(par + 1) * 64, :, :, :],
               es_raw[par * 64:(par + 1) * 64, :].rearrange("p (l kh g2 q) -> p l kh g2 q", l=L, kh=2, g2=4, q=2)[:, :, :, :, par],
               ["es_raw"], ["es_t"])

        for l in range(L):
            S.dma("sp", "ld", stage[:, 0:512].rearrange("p (g d) -> p g d", g=4), pool_w[l].rearrange("g c d -> c g d"), writes=["stage"])
            CP("dve", poolw_b[:, l * 4:(l + 1) * 4, :], stage[:, 0:512].rearrange("p (g d) -> p g d", g=4), ["stage"], ["poolw_b"])
            S.dma("sp", "ld", stage[0:64, 512:1024], w_up[l], writes=["stage"])
            S.dma("sp", "ld", stage[64:128, 512:1024], a_up[l], writes=["stage"])
            CP("dve", lora_b[:, l, :], stage[:, 512:1024], ["stage"], ["lora_b"])

        S.dma("sp", "ld", tab8[:], table[:, :], writes=["tab8"])
        S.op("dve", "tensor_scalar_mul", ["tab8"], ["tab8"], out=tab8[:], in0=tab8[:], scalar1=8.0)
        CP("dve", tab8p[:, :, :, :], tab8[:, :].rearrange("b (kh g2 q) -> b kh q g2", kh=2, g2=4, q=2), ["tab8"], ["tab8p"])
        ohs = stage[0:32, 0:768].rearrange("p (a s) -> p a s", s=192)
        MSET("pool", biasB[:], 0.0, [], ["biasB"])
        MSET("pool", biasB32[:], 0.0, [], ["biasB32"])
        for kh in range(2):
            for i0 in range(0, 64, 4):
                S.dma("sp", "ld", ohs, c_oh[:, i0:i0 + 4, :], writes=["stage"])
                for ii in range(4):
                    i = i0 + ii
                    rhs8 = tab8p[:, kh, :, :].rearrange("b q g -> b (q g)")
                    MM(psL1[:, :].rearrange("p (g i) -> p g i", i=64)[:, :, i], ohs[:, ii, 0:128], rhs8, ["stage", "tab8p"], ["psL1"])
                    MM(psL2[0:64, :].rearrange("p (g i) -> p g i", i=64)[:, :, i], ohs[:, ii, 128:192], rhs8, ["stage", "tab8p"], ["psL2"])
            CP("dve", biasA[:, 0, kh, :], psL1[:, :], ["psL1"], ["biasA"])
            CP("dve", biasB[0:64, kh, :], psL2[0:64, :], ["psL2"], ["biasB"])
        MSET("pool", biasNeg[:], NEGB, [], ["biasA"])
        CP("pool", biasA[:, 1, :, :], biasA[:, 0, :, :], ["biasA"], ["biasA"])
        MSET("pool", biasA[0:64, 1, :, :], NEGB, ["biasA"], ["biasA"])
        for kh in range(2):
            CP("pool", biasA32[:, kh, :].rearrange("p (a i) -> p a i", i=32), biasA[:, 0, kh, :].rearrange("p (a i) -> p a i", i=64)[:, :, 0:32], ["biasA"], ["biasA32"])
            CP("pool", biasB32[0:32, kh, :].rearrange("p (a i) -> p a i", i=32), biasB[0:32, kh, :].rearrange("p (a i) -> p a i", i=64)[:, :, 0:32], ["biasB"], ["biasB32"])

        hres = [("h", kc) for kc in range(NCH)]
        xnres = [("xn", kc) for kc in range(NCH)]
        mxres = [("mx", m) for m in range(NCH)]
        shres = [("shst", j) for j in range(13)]
        for t_, r_ in ((hist, ["hist"]), (khalo, ["khalo"]), (k32halo, ["k32halo"]), (vhalo, ["vhalo"]), (shst, shres), (ST, ["ST"])):
            MSET("pool", t_[:], 0.0, [], r_)

        wctr = [0]

        def proj(wsrc, wres, l, blocks, rhs_fn, rhs_res_fn, T, evac_fn, plevel=99):
            for b in blocks:
                proj_block(wsrc, wres, l, b, rhs_fn, rhs_res_fn, T, evac_fn, plevel)

        def proj_block(wsrc, wres, l, b, rhs_fn, rhs_res_fn, T, evac_fn, plevel=99):
            if True:
                slot = wctr[0] % NSLOT
                wctr[0] += 1
                S.dma("sp", ("w", slot), wslot[slot][:], wsrc[l, b], reads=[(wres, l, b)], writes=[("ws", slot)])
                for half in range(2):
                    pa, pr = next_pa()
                    for kc in range(NCH):
                        MM(pa[:, 0:T], wslot[slot][:, kc, half * 128:(half + 1) * 128], rhs_fn(kc), [("ws", slot), rhs_res_fn(kc)], [pr],
                           start=(kc == 0), stop=(kc == NCH - 1), signal=(kc == NCH - 1))
                    evac_fn(2 * b + half, pa, pr)
                    chk(plevel + (2 * b + half + 1) * 0.01)

        def rms(T):
            pa, pr = next_pa()
            for kc in range(NCH):
                ACTV(xn[:, kc, 0:T], h[:, kc, 0:T], AF.Square, [("h", kc)], [("xn", kc)])
            for kc in range(NCH):
                MM(pa[:, 0:T], ones_b[:], xn[:, kc, 0:T], [("xn", kc), "ones_b"], [pr], start=(kc == 0), stop=(kc == NCH - 1), signal=(kc == NCH - 1))
            ACTV(rstd[:, 0:T], pa[:, 0:T], AF.Ln, [pr], ["rstd"], scale=1.0 / D, bias=1e-5)
            ACTV(rstd[:, 0:T], rstd[:, 0:T], AF.Exp, ["rstd"], ["rstd"], scale=-0.5)

        def layer(l, g, T, Lc, first_tile):
            nchk = T // Lc
            K = int(round(math.log2(Lc)))

            def v4(ap, par):
                return ap.rearrange("p (g2 q i) -> p g2 q i", q=2, i=64)[:, :, par, 0:Lc]

            def v8(ap):
                return ap.rearrange("p (g i) -> p g i", i=64)[:, :, 0:Lc]

            def v4o(ap):
                return ap.rearrange("p (g2 i) -> p g2 i", i=64)[:, :, 0:Lc]

            def m8(k):
                return msk[0:Lc, k, 0:Lc].unsqueeze(1).to_broadcast([Lc, 8, Lc])

            rms(T)
            for kc in range(NCH):
                STT("dve", xn[:, kc, 0:T], h[:, kc, 0:T], nw[:, l * 16 + kc:l * 16 + kc + 1], rstd[:, 0:T], ALU.mult, ALU.mult,
                    [("h", kc), "rstd", "nw"], [("xn", kc)])
            chk(3)
            CP("pool", pext[:, :, 0:16], hist[:, l, :, :], ["hist"], ["pext_h"])
            CP("pool", kT[:, :, 0:128], khalo[:, l, :, :], ["khalo"], ["kT_h"])
            CP("pool", vT[:, 0:128], vhalo[:, l, :], ["vhalo"], ["vT_h"])
            CP("pool", k32[:, 0:128], k32halo[:, l, :], ["k32halo"], ["k32_h"])

            def evac1(e_, pa, pr):
                eng = evac_eng()
                if e_ < 4:
                    CP(eng, pext[:, e_, 16:16 + T], pa[:, 0:T], [pr], [("pext", e_)])
                elif e_ < 12:
                    CP(eng, qT[:, e_ - 4, 0:T], pa[:, 0:T], [pr], ["qT"])
                elif e_ < 14:
                    kh = e_ - 12
                    CP(eng, kT[:, kh, 128:128 + T], pa[:, 0:T], [pr], ["kT"])
                    if not os.environ.get("KNO32"):
                        eng2 = eng
                        CP(eng2, k32[kh * 64:(kh + 1) * 64, 128:128 + T], pa[kh * 64:(kh + 1) * 64, 0:T], [pr], ["k32"])
                elif e_ == 14:
                    CP(eng, vT[:, 128:128 + T], pa[:, 0:T], [pr], ["vT"])
                else:
                    j = e_ - 15
                    dst = (Rr[:, j, :] if j < 4 else Kr[:, j - 4, :] if j < 8 else Vr[:, j - 8, :] if j < 12 else WA[:, :])
                    dres = ("xs", j)
                    tmp, tres = tm[j % 2], f"tm{j % 2}"
                    mcol = muT[:, l * 13 + j:l * 13 + j + 1]
                    ocol = omu[:, l * 13 + j:l * 13 + j + 1]
                    ACTV(tmp[:, 0:T], pa[:, 0:T], AF.Copy, [pr, "omu"], [tres], scale=ocol)
                    STT("dve", dst[:, 1:T], pa[:, 0:T - 1], mcol, tmp[:, 1:T], ALU.mult, ALU.add, [pr, tres, "muT"], [dres])
                    STT("dve", dst[:, 0:1], shst[:, l, j:j + 1], mcol, tmp[:, 0:1], ALU.mult, ALU.add, [tres, "muT", ("shst", j)], [dres])
                    CP("dve", shst[:, l, j:j + 1], pa[:, T - 1:T], [pr], [("shst", j)])

            proj(wib, "wib", l, range(0, 14), lambda kc: xn[:, kc, 0:T], lambda kc: ("xn", kc), T, evac1, plevel=3)

            chk(4)
            allp = [("pext", i) for i in range(4)] + ["pext_h"]
            W = 16 + T
            TTO("dve", ptA[:, :, 1:W], pext[:, :, 1:W], pext[:, :, 0:W - 1], ALU.add, allp, ["ptA"])
            CP("pool", hist[:, l, :, :], pext[:, :, T:T + 16], allp, ["hist"])
            TTO("pool", ptB[:, 1:4, 3:W], ptA[:, 1:4, 3:W], ptA[:, 1:4, 1:W - 2], ALU.add, ["ptA"], ["ptB"])
            TTO("dve", ptA[:, 2:4, 7:W], ptB[:, 2:4, 7:W], ptB[:, 2:4, 3:W - 4], ALU.add, ["ptB"], ["ptA"])
            TTO("pool", ptB[:, 3:4, 15:W], ptA[:, 3:4, 15:W], ptA[:, 3:4, 7:W - 8], ALU.add, ["ptA"], ["ptB"])
            for gi, w in enumerate((2, 4, 8, 16)):
                src, sres = (ptA, "ptA") if gi % 2 == 0 else (ptB, "ptB")
                STT("dve", pd[:, gi, 0:T], src[:, gi, 16:16 + T], 1.0 / w, pext[:, gi, 16:16 + T], ALU.mult, ALU.subtract, [sres] + allp, ["pd"])
                if first_tile and g == 0:
                    TTO("dve", tm[2][:, 0:16], src[:, gi, 16:32], rcnt[:, gi, :], ALU.mult, [sres, "rcnt"], ["tm2"])
                    TTO("dve", pd[:, gi, 0:16], tm[2][:, 0:16], pext[:, gi, 16:32], ALU.subtract, ["tm2"] + allp, ["pd"])
            for gi in range(4):
                pa, pr = next_pa()
                MM(pa[:, 0:T], poolw_b[:, l * 4 + gi, :], pd[:, gi, 0:T], ["pd", "poolw_b"], [pr])
                ACTV(mixed[:, gi, 0:T], pa[:, 0:T], AF.Copy, [pr, "v512"], [("mx", gi)], scale=vec(0, l, gi))

            chk(5)
            kres = ["kT", "kT_h"]
            vres = ["vT", "vT_h"]
            for c in range(nchk):
                TR(psL1[:, c * 128:(c + 1) * 128], vT[:, Lc * c:Lc * c + 128], ident[:], vres, ["psL1"])
                TR(psL2[0:Lc, c * 128:(c + 1) * 128], vT[:, 128 + Lc * c:128 + Lc * (c + 1)], ident[:], vres, ["psL2"])
            CP("dve", Vwin[:, 0:nchk, :], psL1[:, 0:nchk * 128].rearrange("p (c k) -> p c k", k=128), ["psL1"], ["Vwin"])
            CP("act", Vcur[0:Lc, 0:nchk, :], psL2[0:Lc, 0:nchk * 128].rearrange("p (c k) -> p c k", k=128), ["psL2"], ["Vcur"])
            CP("pool", khalo[:, l, :, :], kT[:, :, T:T + 128], kres, ["khalo"])
            CP("pool", vhalo[:, l, :], vT[:, T:T + 128], vres, ["vhalo"])

            NQ = 4 * Lc

            def q3(ap):
                return ap.rearrange("p (g2 i) -> p g2 i", i=Lc)

            fill = []

            def fill_one(n=1):
                for _ in range(n):
                    if fill:
                        fill.pop(0)()

            def att_unit(kh, c):
                if True:
                    var = 0
                    if first_tile and g == 0:
                        var = 2 if c == 0 else (1 if c == 1 else 0)
                    if Lc == 64:
                        bA, bB, bres = (biasNeg[:, :] if var == 2 else biasA[:, var, kh, :]), biasB[:, kh, :], ["biasA", "biasB"]
                    else:
                        bA, bB, bres = biasA32[:, kh, :], biasB32[:, kh, :], ["biasA32", "biasB32"]
                    e1, e2 = E1[0], E2[0]
                    r1, r2 = "E1_0", "E2_0"
                    qs = slice(Lc * c, Lc * (c + 1))
                    def lmm(par, first, last):
                        ps_ = slice(par * 64, (par + 1) * 64)
                        cs_ = slice(par * NQ, (par + 1) * NQ)
                        MM(psL1[:, cs_], kT[ps_, kh, Lc * c:Lc * c + 128], qT[ps_, 4 * kh:4 * kh + 4, qs], kres + ["qT"], ["psL1"],
                           start=first, stop=last, signal=last, tp=(par * 64, 0))
                        MM(psL2[0:Lc, cs_], kT[ps_, kh, 128 + Lc * c:128 + Lc * (c + 1)], qT[ps_, 4 * kh:4 * kh + 4, qs], kres + ["qT"], ["psL2"],
                           start=first, stop=last, signal=last, tp=(par * 64, 0))
                    for par in range(2):
                        cs_ = slice(par * NQ, (par + 1) * NQ)
                        MM(psL1[:, cs_], identb[:], bA[:, cs_], ["identb"] + bres, ["psL1"], start=(par == 0), stop=False, signal=False)
                        MM(psL2[0:Lc, cs_], identb[:, 0:Lc], bB[:, cs_], ["identb"] + bres, ["psL2"], start=(par == 0), stop=False, signal=False)
                        lmm(par, False, par == 1)
                    ACTV(e1[:, 0:2 * NQ], psL1[:, 0:2 * NQ], AF.Exp, ["psL1"], [r1], scale=0.125)
                    ACTV(e2[0:Lc, 0:2 * NQ], psL2[0:Lc, 0:2 * NQ], AF.Exp, ["psL2"], [r2], scale=0.125)
                    for par in range(2):
                        ps_ = slice(par * 64, (par + 1) * 64)
                        cs_ = slice(par * NQ, (par + 1) * NQ)
                        hs = slice(kh * 64, (kh + 1) * 64)
                        MM(psO[ps_, 0:NQ], Vwin[:, c, hs], e1[:, cs_], ["Vwin", r1], ["psO"], start=True, stop=False, signal=False, tp=(0, par * 64))
                        MM(psO[ps_, 0:NQ], Vcur[0:Lc, c, hs], e2[0:Lc, cs_], ["Vcur", r2], ["psO"], start=False, stop=False, signal=False, tp=(0, par * 64))
                        MM(psO[ps_, 256:256 + NQ], ones_b[:, 0:64], e1[:, cs_], ["ones_b", r1], ["psO"], start=False, stop=False, signal=False, tp=(0, par * 64))
                        MM(psO[ps_, 256:256 + NQ], ones_b[0:Lc, 0:64], e2[0:Lc, cs_], ["ones_b", r2], ["psO"], start=False, stop=True, signal=(par == 1), tp=(0, par * 64))
                    TTO("dve", q3(rden[:, 0:NQ]), q3(psO[:, 256:256 + NQ]), es_t[:, l, kh, :].unsqueeze(2).to_broadcast([128, 4, Lc]), ALU.add, ["psO", "es_t"], ["rden"])
                    ACTV(rden[:, 0:NQ], rden[:, 0:NQ], AF.Ln, ["rden"], ["rden"])
                    ACTV(rden[:, 0:NQ], rden[:, 0:NQ], AF.Exp, ["rden"], ["rden"], scale=-1.0)
                    TTO("dve", mixed[:, 4 + 4 * kh:8 + 4 * kh, qs], q3(psO[:, 0:NQ]), q3(rden[:, 0:NQ]), ALU.mult, ["psO", "rden"],
                        [("mx", 4 + 4 * kh + j) for j in range(4)])

            for kh in range(2):
                for c in range(nchk):
                    fill.append(lambda kh=kh, c=c: att_unit(kh, c))
            n_per_pair = (len(fill) + 3) // 4

            def evac2(e_, pa, pr):
                m = e_ - 28
                i = m % 2
                ACTV(sgt[i][:, 0:T], pa[:, 0:T], AF.Silu, [pr], [f"sgt{i}"])
                TTO("pool" if m % 2 else "dve", mixed[:, m, 0:T], mixed[:, m, 0:T], sgt[i][:, 0:T], ALU.mult, [("mx", m), f"sgt{i}"], [("mx", m)])


            chk(6)
            xsr = [("xs", j) for j in range(13)]
            ACTV(twal[0:64, 0:T], WA[0:64, 0:T], AF.Tanh, xsr, ["twal"])
            CP("pool", twal[64:128, 0:T], WA[64:128, 0:T], xsr, ["twal2"])
            CP("pool", STb[:], ST[:, l, :, :], ["ST"], ["STb"])

            def c3(ap):
                return ap[:, 0:T].rearrange("p (c t) -> p c t", t=Lc)

            tr = lambda i: f"tm{i}"
            for p in range(4):
                cs128 = slice(p * 128, (p + 1) * 128)
                paw, prw = next_pa()
                paa, pra = next_pa()
                MM(paw[:, 0:T], lora_b[0:64, l, cs128], twal[0:64, 0:T], ["lora_b", "twal"], [prw])
                MM(paa[:, 0:T], lora_b[64:128, l, cs128], twal[64:128, 0:T], ["lora_b", "twal2"], [pra], tp=(64, 0))
                ACTV(tm[0][:, 0:T], paw[:, 0:T], AF.Sigmoid, [prw, "v512"], [tr(0)], bias=vec(1, l, p))
                ACTV(tm[1][:, 0:T], paa[:, 0:T], AF.Sigmoid, [pra, "v512"], [tr(1)], bias=vec(2, l, p))
                fill_one()
                S.op("dve", "tensor_tensor_scan", [tr(0), "cmask"], [tr(2)], out=tm[2][:, 0:T], data0=cmask[:, 0:T], data1=tm[0][:, 0:T], initial=0.0, op0=ALU.mult, op1=ALU.add)
                TTO("pool", tm[3][:, 0:T], tm[2][:, 0:T], tm[0][:, 0:T], ALU.subtract, [tr(2), tr(0)], [tr(3)])
                ACTV(tm[4][:, 0:T], tm[2][:, 0:T], AF.Exp, [tr(2)], [tr(4)], scale=CDEC)
                ACTV(tm[5][:, 0:T], tm[2][:, 0:T], AF.Exp, [tr(2)], [tr(5)], scale=-CDEC)
                ACTV(tm[6][:, 0:T], tm[3][:, 0:T], AF.Exp, [tr(3)], [tr(6)], scale=-CDEC)
                for hb in range(2):
                    CP("pool", glast[:, 2 * p + hb, 0:nchk], c3(tm[5])[hb * 64:(hb + 1) * 64, :, Lc - 1], [tr(5)], ["glast"])
                TTO("dve", c3(tm[7]), c3(tm[2]), c3(tm[2])[:, :, Lc - 1:Lc].to_broadcast([128, nchk, Lc]), ALU.subtract, [tr(2)], [tr(7)])
                ACTV(tm[7][:, 0:T], tm[7][:, 0:T], AF.Exp, [tr(7)], [tr(7)], scale=CDEC)
                S.op("dve", "tensor_scalar_mul", xsr + ["v512"], [tr(8)], out=tm[8][:, 0:T], in0=Kr[:, p, 0:T], scalar1=vec(3, l, p))
                ACTV(tmb[:, 0:T], tm[8][:, 0:T], AF.Square, [tr(8)], ["tmb"])
                pas, prs = next_pa()
                MM(pas[:, 0:T], bones_b[:], tmb[:, 0:T], ["bones_b", "tmb"], [prs])
                fill_one()
                S.op("dve", "tensor_scalar_max", [prs], [tr(9)], out=tm[9][:, 0:T], in0=pas[:, 0:T], scalar1=1e-24)
                ACTV(tm[9][:, 0:T], tm[9][:, 0:T], AF.Ln, [tr(9)], [tr(9)])
                ACTV(tm[9][:, 0:T], tm[9][:, 0:T], AF.Exp, [tr(9)], [tr(9)], scale=-0.5)
                TTO("dve", tm[8][:, 0:T], tm[8][:, 0:T], tm[9][:, 0:T], ALU.mult, [tr(8), tr(9)], [tr(8)])
                TS("dve", tm[9][:, 0:T], tm[1][:, 0:T], -1.0, vec(4, l, p), ALU.add, ALU.mult, [tr(1), "v512", tr(8)], [tr(9)])
                STT("dve", tm[9][:, 0:T], tm[9][:, 0:T], 1.0, Kr[:, p, 0:T], ALU.add, ALU.mult, [tr(9)] + xsr, [tr(9)])
                STT("dve", tmb2[:, 0:T], Rr[:, p, 0:T], vec(5, l, p), tm[9][:, 0:T], ALU.mult, ALU.mult, [tr(9), "v512"] + xsr, ["tmb2"])
                pab, prb = next_pa()
                MM(pab[:, 0:T], bones_b[:], tmb2[:, 0:T], ["bones_b", "tmb2"], [prb])
                TTO("dve", bonus[:, p, 0:T], pab[:, 0:T], Vr[:, p, 0:T], ALU.mult, [prb] + xsr, ["bonus"])
                TTO("dve", tm[11][:, 0:T], tm[8][:, 0:T], tm[1][:, 0:T], ALU.mult, [tr(8), tr(1)], [tr(11)])
                for hb in range(2):
                    hp = slice(hb * 64, (hb + 1) * 64)
                    hh_ = 2 * p + hb
                    TTO("dve", rt[:, hh_, 0:T], Rr[hp, p, 0:T], tm[5][hp, 0:T], ALU.mult, [tr(5)] + xsr, ["rt"])
                    TTO("pool", kt[:, hh_, 0:T], tm[9][hp, 0:T], tm[4][hp, 0:T], ALU.mult, [tr(9), tr(4)], ["kt"])
                    TTO("pool", bt[:, hh_, 0:T], tm[11][hp, 0:T], tm[4][hp, 0:T], ALU.mult, [tr(11), tr(4)], ["bt"])
                    STT("dve", at[:, hh_, 0:T], tm[8][hp, 0:T], -1.0, tm[6][hp, 0:T], ALU.mult, ALU.mult, [tr(8), tr(6)], ["at"])
                TTO("dve", tm[0][:, 0:T], tm[9][:, 0:T], tm[7][:, 0:T], ALU.mult, [tr(9), tr(7)], [tr(0)])
                TTO("pool", tm[2][:, 0:T], tm[11][:, 0:T], tm[7][:, 0:T], ALU.mult, [tr(11), tr(7)], [tr(2)])
                for srcb, sres, dstT, dres, bank in ((Vr[:, p, :], xsr, Vt, "Vt", 0), (tm[0], [tr(0)], Kpt, "Kpt", 1), (tm[2], [tr(2)], Bpt, "Bpt", 2)):
                    for c in range(nchk):
                        TR(psR[bank][0:Lc, c * 128:(c + 1) * 128], srcb[:, c * Lc:(c + 1) * Lc], ident[:], list(sres), [f"psR{bank}"])
                    CP(evac_eng(), dstT[0:Lc, 0:nchk, cs128], psR[bank][0:Lc, 0:nchk * 128].rearrange("p (c k) -> p c k", k=128), [f"psR{bank}"], [dres])

            fill_one(len(fill))
            chk(7)

            def blk(t_, hh):
                return t_[0:Lc, hh * 64:hh * 64 + Lc]

            base = lambda hh: (hh % 2) * 64

            def hT(t_, hh, c):
                return t_[:, hh, c * Lc:(c + 1) * Lc]

            def mm8(bank, lhs_fn, rhs_fn, reads, based):
                for hh in range(8):
                    MM(blk(psR[bank], hh), lhs_fn(hh), rhs_fn(hh), reads, [f"psR{bank}"], signal=(hh == 7))

            for c in range(nchk):
                mm8(0, lambda hh: hT(bt, hh, c), lambda hh: hT(at, hh, c), ["bt", "at"], True)
                mm8(1, lambda hh: hT(at, hh, c), lambda hh: hT(bt, hh, c), ["bt", "at"], True)
                mm8(2, lambda hh: hT(kt, hh, c), lambda hh: hT(at, hh, c), ["kt", "at"], True)
                chk(7.01)
                TTO("dve", v8(Pm[0][0:Lc, :]), v8(psR[0][0:Lc, :]), m8(0), ALU.mult, ["psR0", "msk"], ["Pm0"])
                chk(7.02)
                TTO("dve", v8(Qm[0][0:Lc, :]), v8(psR[1][0:Lc, :]), m8(1), ALU.mult, ["psR1", "msk"], ["Qm0"])
                TTO("dve", v8(AakT[0:Lc, :]), v8(psR[2][0:Lc, :]), m8(0), ALU.mult, ["psR2", "msk"], ["AakT"])
                chk(7.04)
                TTO("pool", v8(Rm[0][0:Lc, :]), v8(Pm[0][0:Lc, :]), m8(3), ALU.add, ["Pm0", "msk"], ["Rm0"])
                chk(7.05)
                mm8(0, lambda hh: hT(bt, hh, c), lambda hh: hT(rt, hh, c), ["bt", "rt"], True)
                mm8(1, lambda hh: hT(kt, hh, c), lambda hh: hT(rt, hh, c), ["kt", "rt"], True)
                TTO("dve", v8(ArbT[0:Lc, :]), v8(psR[0][0:Lc, :]), m8(2), ALU.mult, ["psR0", "msk"], ["ArbT"])
                TTO("dve", v8(ArkT[0:Lc, :]), v8(psR[1][0:Lc, :]), m8(2), ALU.mult, ["psR1", "msk"], ["ArkT"])
                chk(7.1)
                for s_ in range(1, K + 1):
                    cur, nxt = (s_ - 1) % 2, s_ % 2
                    ro, rn = (s_ - 2) % 2, (s_ - 1) % 2
                    if s_ <= K - 2:
                        mm8(0, lambda hh: blk(Qm[cur], hh), lambda hh: blk(Pm[cur], hh), [f"Qm{cur}", f"Pm{cur}"], False)
                    if s_ <= K - 1:
                        mm8(1, lambda hh: blk(Pm[cur], hh), lambda hh: blk(Qm[cur], hh), [f"Qm{cur}", f"Pm{cur}"], False)
                    if s_ >= 2:
                        mm8(2, lambda hh: blk(Qm[cur], hh), lambda hh: blk(Rm[ro], hh), [f"Qm{cur}", f"Rm{ro}"], False)
                    if s_ % 2 == 1:
                        fill_one()
                    if s_ <= K - 2:
                        CP("act", v8(Pm[nxt][0:Lc, :]), v8(psR[0][0:Lc, :]), ["psR0"], [f"Pm{nxt}"])
                    if s_ <= K - 1:
                        CP("act", v8(Qm[nxt][0:Lc, :]), v8(psR[1][0:Lc, :]), ["psR1"], [f"Qm{nxt}"])
                    if s_ >= 2:
                        TTO("dve", v8(Rm[rn][0:Lc, :]), v8(psR[2][0:Lc, :]), v8(Rm[ro][0:Lc, :]), ALU.add, ["psR2", f"Rm{ro}"], [f"Rm{rn}"])
                MT = Rm[(K - 1) % 2]
                mres = f"Rm{(K - 1) % 2}"
                chk(7.2)
                for hh in range(8):
                    hs = slice(hh * 64, (hh + 1) * 64)
                    b_ = base(hh)
                    MM(psR[0][0:Lc, hs], blk(AakT, hh), Vt[0:Lc, c, hs], ["AakT", "Vt"], ["psR0"], start=True, stop=False, signal=False)
                    MM(psR[0][0:Lc, hs], hT(at, hh, c), STb[:, hh, :], ["at", "STb"], ["psR0"], start=False, stop=True, signal=(hh == 7))
                CP("act", RHSb[0:Lc, :], psR[0][0:Lc, :], ["psR0"], ["RHSb"])
                for hh in range(8):
                    hs = slice(hh * 64, (hh + 1) * 64)
                    MM(psR[1][0:Lc, hs], blk(MT, hh), RHSb[0:Lc, hs], [mres, "RHSb"], ["psR1"], signal=(hh == 7))
                CP("act", SAb[0:Lc, :], psR[1][0:Lc, :], ["psR1"], ["SAb"])
                chk(7.3)
                for hh in range(8):
                    hs = slice(hh * 64, (hh + 1) * 64)
                    b_ = base(hh)
                    oo = psR[2][b_:b_ + 64, (hh // 2) * 64:(hh // 2) * 64 + Lc]
                    MM(oo, STb[:, hh, :], hT(rt, hh, c), ["STb", "rt"], ["psR2"], start=True, stop=False, signal=False, tp=(0, b_))
                    MM(oo, SAb[0:Lc, hs], blk(ArbT, hh), ["SAb", "ArbT"], ["psR2"], start=False, stop=False, signal=False, tp=(0, b_))
                    MM(oo, Vt[0:Lc, c, hs], blk(ArkT, hh), ["Vt", "ArkT"], ["psR2"], start=False, stop=True, signal=(hh == 7), tp=(0, b_))
                CP("act", OT[:, :, c * Lc:(c + 1) * Lc], v4o(psR[2][:, 0:256]), ["psR2"], ["OT"])
                chk(7.4)
                for hh in range(8):
                    hs = slice(hh * 64, (hh + 1) * 64)
                    b_ = base(hh)
                    oo = psR[0][0:64, hs]
                    MM(oo, Bpt[0:Lc, c, hs], SAb[0:Lc, hs], ["Bpt", "SAb"], ["psR0"], start=True, stop=False, signal=False)
                    MM(oo, Kpt[0:Lc, c, hs], Vt[0:Lc, c, hs], ["Kpt", "Vt"], ["psR0"], start=False, stop=True, signal=(hh == 7))
                TTO("pool", STg[:], ST[:, l, :, :], glast[:, :, c:c + 1].to_broadcast([64, 8, 64]), ALU.mult, ["ST", "glast"], ["STg"])
                ps_s = psR[0][0:64, :].rearrange("p (a i) -> p a i", i=64)
                TTO("dve", STb[:], STg[:], ps_s, ALU.add, ["STg", "psR0"], ["STb"])
                TTO("dve", ST[:, l, :, :], STg[:], ps_s, ALU.add, ["STg", "psR0"], ["ST"])

            chk(8)
            fill_one(len(fill))
            gt = lambda p, i: (tm[3 * p + i], f"tm{3 * p + i}")
            pms = []
            for p in range(4):
                pam, prm = next_pa()
                MM(pam[:, 0:T], bmean[:], OT[:, p, 0:T], ["bmean", "OT"], [prm])
                TTO("dve", gt(p, 0)[0][:, 0:T], OT[:, p, 0:T], pam[:, 0:T], ALU.subtract, ["OT", prm], [gt(p, 0)[1]])
                ACTV(gt(p, 1)[0][:, 0:T], gt(p, 0)[0][:, 0:T], AF.Square, [gt(p, 0)[1]], [gt(p, 1)[1]])
            for p in range(4):
                pav, prv = next_pa()
                MM(pav[:, 0:T], bmean[:], gt(p, 1)[0][:, 0:T], ["bmean", gt(p, 1)[1]], [prv])
                ACTV(gt(p, 2)[0][:, 0:T], pav[:, 0:T], AF.Ln, [prv], [gt(p, 2)[1]], bias=64e-5)
            for p in range(4):
                ACTV(gt(p, 2)[0][:, 0:T], gt(p, 2)[0][:, 0:T], AF.Exp, [gt(p, 2)[1]], [gt(p, 2)[1]], scale=-0.5)
            for p in range(4):
                a0_, r0_ = gt(p, 0)
                TTO("dve", a0_[:, 0:T], a0_[:, 0:T], gt(p, 2)[0][:, 0:T], ALU.mult, [r0_, gt(p, 2)[1]], [r0_])
                TS("dve", a0_[:, 0:T], a0_[:, 0:T], vec(6, l, p), vec(7, l, p), ALU.mult, ALU.add, [r0_, "v512"], [r0_])
                TTO("dve", mixed[:, 12 + p, 0:T], a0_[:, 0:T], bonus[:, p, 0:T], ALU.add, [r0_, "bonus"], [("mx", 12 + p)])

            chk(9)
            proj(wib, "wib", l, range(14, 22), lambda kc: xn[:, kc, 0:T], lambda kc: ("xn", kc), T, evac2)

            chk(10)
            def evac3(e_, pa, pr):
                TTO("dve", h[:, e_, 0:T], h[:, e_, 0:T], pa[:, 0:T], ALU.add, [pr, ("h", e_)], [("h", e_)])

            proj(wob, "wob", l, range(0, NBLK_OUT), lambda kc: mixed[:, kc, 0:T], lambda kc: ("mx", kc), T, evac3)

        def load_x(src, row0, T):
            nb = (T + 127) // 128
            for tb in range(nb):
                n = min(128, T - tb * 128)
                for hf in range(2):
                    S.dma("sp", "x", stage[0:n, :], src[row0 + tb * 128:row0 + tb * 128 + n, hf * 1024:(hf + 1) * 1024], writes=["stage"])
                    for k8 in range(8):
                        kc = hf * 8 + k8
                        pa, pr = next_pa()
                        TR(pa[:, 0:n], stage[0:n, k8 * 128:(k8 + 1) * 128], ident[0:n, 0:n], ["stage"], [pr])
                        CP(evac_eng(), h[:, kc, tb * 128:tb * 128 + n], pa[:, 0:n], [pr], [("h", kc)])

        def store_y(dst, row0, T):
            rms(T)
            nb = (T + 127) // 128
            for tb in range(nb):
                n = min(128, T - tb * 128)
                ts = slice(tb * 128, tb * 128 + n)
                for hf in range(2):
                    for k8 in range(8):
                        kc = hf * 8 + k8
                        t_ = tm[kc % 4]
                        STT("dve", t_[:, 0:n], h[:, kc, ts], nw[:, L * 16 + kc:L * 16 + kc + 1], rstd[:, ts], ALU.mult, ALU.mult, [("h", kc), "rstd", "nw"], [f"tm{kc % 4}"])
                        pa, pr = next_pa()
                        TR(pa[0:n, 0:128], t_[:, 0:n], ident[:], [f"tm{kc % 4}"], [pr])
                        CP(evac_eng(), stage[0:n, k8 * 128:(k8 + 1) * 128], pa[0:n, 0:128], [pr], ["stage"])
                    S.dma("pool", "yout", dst[row0 + tb * 128:row0 + tb * 128 + n, hf * 1024:(hf + 1) * 1024], stage[0:n, :], reads=["stage"])

        def layer_state_kv(g, l, T):
            o = outs["p" if g == 0 else "s"]
            for nm, srcb, sres, ti_ in (("k", k32, ["k32", "k32_h"], 10), ("v", vT, ["vT", "vT_h"], 11)):
                pa, pr = next_pa()
                TR(pa[:, 0:128], srcb[:, T:T + 128], ident[:], sres, [pr])
                CP("dve", tm[ti_][:, 0:128], pa[:, 0:128], [pr], [f"tm{ti_}"])
                S.dma("pool", "sout", o[nm][l], tm[ti_][:, 0:128], reads=[f"tm{ti_}"])

        def store_states(g):
            o = outs["p" if g == 0 else "s"]
            for l in range(L):
                for gi in range(4):
                    pa, pr = next_pa()
                    TR(pa[0:16, 0:128], hist[:, l, gi, :], ident[:], ["hist"], [pr])
                    CP("dve", stage[0:16, gi * 128:(gi + 1) * 128], pa[0:16, 0:128], [pr], ["stage"])
                S.dma("pool", "sout", o["pool"][l], stage[1:16, 0:512], reads=["stage"])
                pa, pr = next_pa()
                TR(pa[0:13, 0:128], shst[:, l, :], ident[:], shres, [pr])
                CP("dve", stage[0:13, 512:640], pa[0:13, 0:128], [pr], ["stage"])
                S.dma("pool", "sout", o["shift"][l], stage[0:13, 512:640], reads=["stage"])
                for q4 in range(2):
                    pa, pr = next_pa()
                    for hq in range(4):
                        TR(pa[0:64, hq * 64:(hq + 1) * 64], ST[:, l, 4 * q4 + hq, :], ident[0:64, 0:64], ["ST"], [pr])
                    CP("dve", stage[0:64, 512 + q4 * 256:512 + (q4 + 1) * 256], pa[0:64, 0:256], [pr], ["stage"])
                S.dma("pool", "sout", o["wkv"][l].rearrange("h i j -> i h j"), stage[0:64, 512:1024].rearrange("i (h j) -> i h j", h=8), reads=["stage"])

        def load_sample_states():
            for l in range(L):
                S.dma("sp", "ld", stage[0:15, 0:512], st_pool[l], writes=["stage"])
                for gi in range(4):
                    pa, pr = next_pa()
                    TR(pa[:, 0:15], stage[0:15, gi * 128:(gi + 1) * 128], ident[0:15, 0:15], ["stage"], [pr])
                    CP("dve", hist[:, l, gi, 1:16], pa[:, 0:15], [pr], ["hist"])
                S.dma("sp", "ld", stage[:, 512:640], st_k[l], writes=["stage"])
                S.dma("sp", "ld", stage[:, 640:768], st_v[l], writes=["stage"])
                pa, pr = next_pa()
                TR(pa[:, 0:128], stage[:, 512:640], ident[:], ["stage"], [pr])
                CP("dve", k32halo[:, l, :], pa[:, 0:128], [pr], ["k32halo"])
                for kh in range(2):
                    for half in range(2):
                        CP("dve", khalo[half * 64:(half + 1) * 64, l, kh, :], pa[kh * 64:(kh + 1) * 64, 0:128], [pr], ["khalo"])
                pa, pr = next_pa()
                TR(pa[:, 0:128], stage[:, 640:768], ident[:], ["stage"], [pr])
                CP("dve", vhalo[:, l, :], pa[:, 0:128], [pr], ["vhalo"])
                S.dma("sp", "ld", stage[0:13, 768:896], st_shift[l], writes=["stage"])
                pa, pr = next_pa()
                TR(pa[:, 0:13], stage[0:13, 768:896], ident[0:13, 0:13], ["stage"], [pr])
                CP("dve", shst[:, l, :], pa[:, 0:13], [pr], shres)
                S.dma("sp", "ld", stage[0:64, 0:512].rearrange("i (h j) -> i h j", h=8), st_wkv[l].rearrange("h i j -> i h j"), writes=["stage"])
                for q4 in range(2):
                    pa, pr = next_pa()
                    for hq in range(4):
                        hh_ = 4 * q4 + hq
                        TR(pa[0:64, hq * 64:(hq + 1) * 64], stage[0:64, hh_ * 64:(hh_ + 1) * 64], ident[0:64, 0:64], ["stage"], [pr])
                    CP("dve", ST[:, l, 4 * q4:4 * q4 + 4, :], pa[0:64, 0:256].rearrange("p (a i) -> p a i", i=64), [pr], ["ST"])

        def main_program():
            for ti in range(NT):
                load_x(xp, ti * TT, TT)
                chk(2)
                for l in range(L):
                    layer(l, 0, TT, 64, ti == 0)
                    if ti == NT - 1:
                        layer_state_kv(0, l, TT)
                store_y(yp, ti * TT, TT)
                chk(20)
            store_states(0)
            chk(21)
            load_sample_states()
            load_x(xs, 0, DEC)
            chk(22)
            for l in range(L):
                layer(l, 1, DEC, 32, False)
                layer_state_kv(1, l, DEC)
            store_y(ys, 0, DEC)
            store_states(1)

        try:
            chk(1)
            main_program()
        except _Stop:
            pass
        for _i in range(int(os.environ.get("KEXTRA", "0"))):
            S.dma("sp", "ld", stage[0:32, 0:16], table[:, :], writes=["stage"])
        S.finish("pool")
        S.finish("sp")
        S.emit(block)
        print("[kernel] dma counts", {k: sum(1 for it in v if it[0] == "o" and it[1] == "dma_start") for k, v in S.q.items()}, "sem max", {k: S.cnt[k] for k in S.ENG}, flush=True)
        print(f"[kernel] ops={S.nops} q=" + ",".join(f"{k}:{len(v)}" for k, v in S.q.items()), flush=True)
    return nc


_NC_CACHE = {}


def _run(inputs, SEQ, DEPTH, n_prompt, n_sample, n_cores=8):
    L = DEPTH
    key = (SEQ, DEPTH)
    if key not in _NC_CACHE:
        _NC_CACHE[key] = build_nc(SEQ, DEPTH)
    nc = _NC_CACHE[key]
    f = lambda a: np.ascontiguousarray(np.asarray(a, dtype=np.float32))
    I = {k: f(v) for k, v in inputs.items()}
    consts = host_consts()
    vec512 = np.stack([I["pool_scale"], I["rwkv_w0"], I["rwkv_a0"], I["rwkv_k_k"], I["rwkv_k_a"], I["rwkv_r_k"].reshape(L, 512),
                       I["rwkv_lnx_w"], I["rwkv_lnx_b"]], axis=0).reshape(8 * L * 4, 128)
    shared = dict(
        norm_w=I["norm_w"].reshape(L * 16, 128), fnorm_w=I["final_norm_w"].reshape(16, 128), w_in=I["w_in"], w_out=I["w_out"],
        pool_w=I["pool_w"], vec512=np.ascontiguousarray(vec512), sinks=I["attn_sinks"].reshape(1, L * 16), table=I["rel_bias_table"],
        mu=I["rwkv_mu"].reshape(L * 13, 128), w_up=I["rwkv_w_up"], a_up=I["rwkv_a_up"], **consts)
    in_maps = []
    for c in range(n_cores):
        bp = c % n_prompt
        bs = c % n_sample
        m = dict(shared)
        m["xp"] = I["x_prompt"][bp]
        m["xs"] = I["x_sample"][bs]
        m["st_pool"] = np.ascontiguousarray(I["state_pool"][:, bs])
        m["st_k"] = np.ascontiguousarray(I["cache_swa_k"][:, bs].reshape(L, 128, 128))
        m["st_v"] = np.ascontiguousarray(I["cache_swa_v"][:, bs].reshape(L, 128, 128))
        m["st_shift"] = np.ascontiguousarray(I["state_rwkv_shift"][:, bs].reshape(L, 13, 128))
        m["st_wkv"] = np.ascontiguousarray(I["state_rwkv_wkv"][:, bs])
        in_maps.append(m)
    res = run_bass_kernel_spmd(nc, in_maps, core_ids=list(range(n_cores)))
    R = res.results
    n_prompt = min(n_prompt, n_cores)
    n_sample = min(n_sample, n_cores)
    y_prompt = np.stack([R[b]["yp"] for b in range(n_prompt)], axis=0)
    y_sample = np.stack([R[b]["ys"] for b in range(n_sample)], axis=0)

    def gather(g, n):
        pool = np.stack([R[b][f"o_pool_{g}"] for b in range(n)], axis=1)
        k = np.stack([R[b][f"o_k_{g}"].reshape(L, 128, 2, 64) for b in range(n)], axis=1)
        v = np.stack([R[b][f"o_v_{g}"].reshape(L, 128, 2, 64) for b in range(n)], axis=1)
        sh = np.stack([R[b][f"o_shift_{g}"].reshape(L, DSH) for b in range(n)], axis=1)
        wkv = np.stack([R[b][f"o_wkv_{g}"] for b in range(n)], axis=1)
        return [pool, k, v, sh, wkv]

    outs = [y_prompt, y_sample] + gather("p", n_prompt) + gather("s", n_sample)
    return tuple(np.ascontiguousarray(o.astype(np.float32)) for o in outs)


def kernel(**inputs):
    return _run(inputs, 8192, 4, 4, 8)
```

```python
import math
from contextlib import ExitStack

import numpy as np
import concourse.bass as bass
import concourse.mybir as mybir
from concourse.bass_utils import run_bass_kernel_spmd

F32 = mybir.dt.float32
BF16 = mybir.dt.bfloat16
AF = mybir.ActivationFunctionType
ALU = mybir.AluOpType

D = 2048
NCH = 16
TT = 256
DEC = 32
NBLK_IN = 22
NBLK_OUT = 8
DSH = 1664
CDEC = math.exp(-0.5)
NEGB = -30000.0
NSLOT = 3


class Sched:
    ENG = ("pe", "act", "dve", "pool", "sp")
    ROT = {"ld": 6, "cast": 8, "x": 2, "yout": 2, "sout": 4}

    def __init__(self, sems, dma_sems):
        self.rot = {}
        self.sem = sems
        self.cnt = {k: 0 for k in self.ENG}
        self.seen = {k: {} for k in self.ENG}
        self.dma_sems = dma_sems
        self.dma_map = {}
        self.lastw = {}
        self.reads = {}
        self.q = {k: [] for k in self.ENG}
        self.nops = 0

    def _wait(self, eng, deps):
        best = {}
        for ev in deps:
            if ev is None:
                continue
            k, s, v = ev
            if k not in best or best[k][1] < v:
                best[k] = (s, v)
        for k, (s, v) in best.items():
            if self.seen[eng].get(k, 0) >= v:
                continue
            if k == eng and eng == "pe":
                continue
            if k == eng and v > self.cnt[eng]:
                raise RuntimeError(f"self-deadlock on {eng}")
            self.q[eng].append(("w", s, v))
            self.seen[eng][k] = v

    def _deps(self, reads, writes):
        deps = []
        for r in reads:
            deps.append(self.lastw.get(r))
        for w in writes:
            deps.append(self.lastw.get(w))
            deps.extend(self.reads.get(w, {}).values())
        return deps

    def _commit(self, ev, reads, writes):
        k = ev[0]
        for r in reads:
            self.reads.setdefault(r, {})[k] = ev
        for w in writes:
            self.lastw[w] = ev
            self.reads[w] = {}

    def op(self, eng, meth, reads, writes, signal=True, **kw):
        writes = list(writes) + [r for r in reads if isinstance(r, str) and r[:2] in ("ps", "pa") and r not in writes]
        self._wait(eng, self._deps(reads, writes))
        ev = (eng, self.sem[eng], self.cnt[eng] + 1)
        if signal:
            self.q[eng].append(("o", meth, kw, self.sem[eng], 1))
            self.cnt[eng] += 1
        else:
            self.q[eng].append(("o", meth, kw, None, 0))
        self._commit(ev, reads, writes)
        self.nops += 1

    def dma(self, eng, key, out, in_, reads=(), writes=()):
        if isinstance(key, str) and key in self.ROT:
            n = self.rot.get(key, 0)
            self.rot[key] = n + 1
            key = f"{key}{n % self.ROT[key]}"
        if key not in self.dma_map:
            self.dma_map[key] = [self.dma_sems.pop(), 0]
        ent = self.dma_map[key]
        prev = [("dma:" + str(key), ent[0], ent[1])] if ent[1] else []
        self._wait(eng, self._deps(reads, writes) + prev)
        ent[1] += 16
        self.q[eng].append(("o", "dma_start", dict(out=out, in_=in_), ent[0], 16))
        ev = ("dma:" + str(key), ent[0], ent[1])
        self._commit(ev, reads, writes)
        self.nops += 1

    def finish(self, eng):
        deps = []
        for key, (s, v) in self.dma_map.items():
            if v:
                deps.append(("dma:" + str(key), s, v))
        for k in self.ENG:
            if self.cnt[k] and k != eng:
                deps.append((k, self.sem[k], self.cnt[k]))
        self._wait(eng, deps)

    def replay(self, eng, e):
        for it in self.q[eng]:
            if it[0] == "w":
                e.wait_ge(it[1], it[2])
            else:
                ins = getattr(e, it[1])(**it[2])
                if it[3] is not None:
                    ins.then_inc(it[3], it[4])

    def emit(self, block):
        @block.tensor
        def _(e):
            self.replay("pe", e)

        @block.scalar
        def _(e):
            self.replay("act", e)

        @block.vector
        def _(e):
            self.replay("dve", e)

        @block.gpsimd
        def _(e):
            self.replay("pool", e)

        @block.sync
        def _(e):
            self.replay("sp", e)


def _bucket_np(rel):
    nb, me = 16, 8
    ret = np.where(rel > 0, nb, 0)
    n = np.abs(rel)
    nf = np.maximum(n, 1).astype(np.float32)
    large = me + (np.log(nf / np.float32(me)) / np.float32(math.log(128 / me)) * np.float32(nb - me)).astype(np.int32)
    large = np.minimum(large, nb - 1)
    return ret + np.where(n < me, n, large)


def host_consts():
    c = {}
    s = np.arange(192)[:, None]
    i = np.arange(64)[None, :]
    b = _bucket_np((s - 128 - i).astype(np.int32))
    oh = np.zeros((32, 64, 192), np.float32)
    for ii in range(64):
        oh[b[:, ii], ii, np.arange(192)] = 1.0
    c["c_oh"] = oh
    t = np.arange(64)
    su = (t[:, None] < t[None, :]).astype(np.float32)
    c["c_msk"] = np.ascontiguousarray(np.stack([su, su.T, (t[:, None] <= t[None, :]).astype(np.float32), np.eye(64, dtype=np.float32)], axis=1))
    c["c_ident"] = np.eye(128, dtype=np.float32)
    bo = np.zeros((128, 128), np.float32)
    bo[:64, :64] = 1.0
    bo[64:, 64:] = 1.0
    c["c_bones"] = bo
    cm = np.ones((128, TT), np.float32)
    cm[:, ::64] = 0.0
    c["c_cmask"] = cm
    rc = np.zeros((128, 4, 16), np.float32)
    for g, w in enumerate((2, 4, 8, 16)):
        rc[:, g, :] = 1.0 / np.minimum(np.arange(16) + 1, w)
    c["c_rcnt"] = rc
    return c


class _Stop(Exception):
    pass


def build_nc(SEQ, DEPTH):
    import os
    KSTOP = float(os.environ.get("KSTOP", "0"))

    def chk(level):
        if KSTOP and level >= KSTOP:
            raise _Stop()

    L = DEPTH
    NT = SEQ // TT
    nc = bass.Bass("TRN2", target_bir_lowering=False)
    dt_in = lambda n, sh: nc.dram_tensor(n, sh, F32, kind="ExternalInput").ap()
    dt_out = lambda n, sh: nc.dram_tensor(n, sh, F32, kind="ExternalOutput").ap()
    xp = dt_in("xp", [SEQ, D])
    xs = dt_in("xs", [DEC, D])
    st_pool = dt_in("st_pool", [L, 15, 512])
    st_k = dt_in("st_k", [L, 128, 128])
    st_v = dt_in("st_v", [L, 128, 128])
    st_shift = dt_in("st_shift", [L, 13, 128])
    st_wkv = dt_in("st_wkv", [L, 8, 64, 64])
    norm_w = dt_in("norm_w", [L * 16, 128])
    fnorm_w = dt_in("fnorm_w", [16, 128])
    w_in = dt_in("w_in", [L, D, 5504])
    w_out = dt_in("w_out", [L, D, D])
    pool_w = dt_in("pool_w", [L, 4, 128, 128])
    vec512 = dt_in("vec512", [8 * L * 4, 128])
    sinks = dt_in("sinks", [1, L * 16])
    table = dt_in("table", [32, 16])
    mu = dt_in("mu", [L * 13, 128])
    w_up = dt_in("w_up", [L, 64, 512])
    a_up = dt_in("a_up", [L, 64, 512])
    c_oh = dt_in("c_oh", [32, 64, 192])
    c_msk = dt_in("c_msk", [64, 4, 64])
    c_ident = dt_in("c_ident", [128, 128])
    c_bones = dt_in("c_bones", [128, 128])
    c_cmask = dt_in("c_cmask", [128, TT])
    c_rcnt = dt_in("c_rcnt", [128, 4, 16])

    yp = dt_out("yp", [SEQ, D])
    ys = dt_out("ys", [DEC, D])
    outs = {}
    for g in ("p", "s"):
        outs[g] = dict(
            pool=dt_out(f"o_pool_{g}", [L, 15, 512]),
            k=dt_out(f"o_k_{g}", [L, 128, 128]),
            v=dt_out(f"o_v_{g}", [L, 128, 128]),
            shift=dt_out(f"o_shift_{g}", [L, 13, 128]),
            wkv=dt_out(f"o_wkv_{g}", [L, 8, 64, 64]),
        )
    wib = nc.dram_tensor("wib", [L, NBLK_IN, 128, NCH, 256], BF16, kind="Internal").ap()
    wob = nc.dram_tensor("wob", [L, NBLK_OUT, 128, NCH, 256], BF16, kind="Internal").ap()

    with ExitStack() as es:
        def SB(n, sh, dt=F32):
            return es.enter_context(nc.sbuf_tensor(n, sh, dt))

        def PS(n, sh, dt=F32):
            return es.enter_context(nc.psum_tensor(n, sh, dt))

        ident = SB("ident", [128, 128])
        identb = SB("identb", [128, 128], BF16)
        bones_b = SB("bones_b", [128, 128], BF16)
        bmean = SB("bmean", [128, 128])
        ones_b = SB("ones_b", [128, 128], BF16)
        msk = SB("msk", [64, 4, 64])
        cmask = SB("cmask", [128, TT])
        rcnt = SB("rcnt", [128, 4, 16])
        biasA = SB("biasA", [128, 2, 2, 512], BF16)
        biasNeg = SB("biasNeg", [128, 512], BF16)
        biasB = SB("biasB", [128, 2, 512], BF16)
        nw = SB("nw", [128, L * 16 + 16])
        v512 = SB("v512", [128, 8 * L * 4])
        muT = SB("muT", [128, L * 13])
        omu = SB("omu", [128, L * 13])
        es_raw = SB("es_raw", [128, L * 16])
        es_t = SB("es_t", [128, L, 2, 4])
        poolw_b = SB("poolw_b", [128, L * 4, 128], BF16)
        lora_b = SB("lora_b", [128, L, 512], BF16)
        stage = SB("stage", [128, 1024])
        tab8 = SB("tab8", [32, 16])
        tab8p = SB("tab8p", [32, 2, 2, 4])
        biasA32 = SB("biasA32", [128, 2, 256], BF16)
        biasB32 = SB("biasB32", [128, 2, 256], BF16)

        def vec(vi, l, c):
            j = (vi * L + l) * 4 + c
            return v512[:, j:j + 1]

        h = SB("h", [128, NCH, TT])
        xn = SB("xn", [128, NCH, TT], BF16)
        mixed = SB("mixed", [128, NCH, TT], BF16)
        rstd = SB("rstd", [128, TT])
        wslot = [SB(f"wslot{i}", [128, NCH, 256], BF16) for i in range(NSLOT)]
        pext = SB("pext", [128, 4, 16 + TT])
        ptA = SB("ptA", [128, 4, 16 + TT])
        ptB = SB("ptB", [128, 4, 16 + TT])
        pd = SB("pd", [128, 4, TT], BF16)
        qT = SB("qT", [128, 8, TT], BF16)
        kT = SB("kT", [128, 2, 128 + TT], BF16)
        vT = SB("vT", [128, 128 + TT])
        k32 = SB("k32", [128, 128 + TT])
        Vwin = SB("Vwin", [128, 4, 128], BF16)
        Vcur = SB("Vcur", [64, 4, 128], BF16)
        E1 = [SB(f"E1_{i}", [128, 512], BF16) for i in range(1)]
        E2 = [SB(f"E2_{i}", [64, 512], BF16) for i in range(1)]
        rden = SB("rden", [128, 256])
        sgt = [SB(f"sgt{i}", [128, TT]) for i in range(2)]
        Rr = SB("Rr", [128, 4, TT])
        Kr = SB("Kr", [128, 4, TT])
        Vr = SB("Vr", [128, 4, TT])
        WA = SB("WA", [128, TT])
        twal = SB("twal", [128, TT], BF16)
        tm = [SB(f"tm{i}", [128, TT]) for i in range(12)]
        tmb = SB("tmb", [128, TT], BF16)
        tmb2 = SB("tmb2", [128, TT], BF16)
        rt = SB("rt", [64, 8, TT], BF16)
        kt = SB("kt", [64, 8, TT], BF16)
        bt = SB("bt", [64, 8, TT], BF16)
        at = SB("at", [64, 8, TT], BF16)
        glast = SB("glast", [64, 8, 4])
        bonus = SB("bonus", [128, 4, TT])
        Vt = SB("Vt", [64, 4, 512], BF16)
        Kpt = SB("Kpt", [64, 4, 512], BF16)
        Bpt = SB("Bpt", [64, 4, 512], BF16)
        Pm = [SB(f"Pm{i}", [64, 512], BF16) for i in range(2)]
        Qm = [SB(f"Qm{i}", [64, 512], BF16) for i in range(2)]
        Rm = [SB(f"Rm{i}", [64, 512], BF16) for i in range(2)]
        AakT = SB("AakT", [64, 512], BF16)
        ArbT = SB("ArbT", [64, 512], BF16)
        ArkT = SB("ArkT", [64, 512], BF16)
        RHSb = SB("RHSb", [64, 512], BF16)
        SAb = SB("SAb", [64, 512], BF16)
        STg = SB("STg", [64, 8, 64])
        STb = SB("STb", [64, 8, 64], BF16)
        OT = SB("OT", [128, 4, TT])
        hist = SB("hist", [128, L, 4, 16])
        khalo = SB("khalo", [128, L, 2, 128], BF16)
        k32halo = SB("k32halo", [128, L, 128])
        vhalo = SB("vhalo", [128, L, 128])
        shst = SB("shst", [128, L, 13])
        ST = SB("ST", [64, L, 8, 64])

        psA = [PS(f"psA{i}", [128, 2, 256]) for i in range(2)]
        psL1 = PS("psL1", [128, 512])
        psL2 = PS("psL2", [128, 512])
        psO = PS("psO", [128, 512])
        psR = [PS(f"psR{i}", [128, 512]) for i in range(3)]

        sems = {k: es.enter_context(nc.semaphore("s_" + k)) for k in Sched.ENG}
        dsems = [es.enter_context(nc.semaphore(f"dq{i}")) for i in range(40)]
        block = es.enter_context(nc.Block())
        S = Sched(sems, dsems)

        pa_ctr = [0]

        def next_pa():
            i = pa_ctr[0] % 2
            pa_ctr[0] += 1
            return psA[i][:, 0, :], f"pa{i}"

        pb_ctr = [0]

        def next_pb():
            opts = [(psA[0][:, 0, :], "pa0"), (psA[1][:, 0, :], "pa1"), (psL1[:, 0:256], "psL1"), (psL2[:, 0:256], "psL2"), (psO[:, 0:256], "psO")]
            i = pb_ctr[0] % len(opts)
            pb_ctr[0] += 1
            return opts[i]

        ev_ctr = [0]

        def evac_eng():
            ev_ctr[0] += 1
            return "act" if ev_ctr[0] % 2 else "dve"

        def CP(eng, out, in_, r, w):
            S.op(eng, "copy" if eng == "act" else "tensor_copy", r, w, out=out, in_=in_)

        def TTO(eng, out, in0, in1, op, r, w):
            S.op(eng, "tensor_tensor", r, w, out=out, in0=in0, in1=in1, op=op)

        def STT(eng, out, in0, scalar, in1, op0, op1, r, w):
            S.op(eng, "scalar_tensor_tensor", r, w, out=out, in0=in0, scalar=scalar, in1=in1, op0=op0, op1=op1)

        def TS(eng, out, in0, s1, s2, op0, op1, r, w):
            S.op(eng, "tensor_scalar", r, w, out=out, in0=in0, scalar1=s1, scalar2=s2, op0=op0, op1=op1)

        def ACTV(out, in_, func, r, w, **kw):
            S.op("act", "activation", r, w, out=out, in_=in_, func=func, **kw)

        def MM(out, lhsT, rhs, r, w, start=True, stop=True, signal=True, tp=None):
            kw = dict(out=out, lhsT=lhsT, rhs=rhs, start=start, stop=stop)
            if tp is not None:
                kw["tile_position"] = tp
            S.op("pe", "matmul", r, w, signal=signal, **kw)

        def TR(out, in_, idn, r, w):
            S.op("pe", "transpose", r + ["ident"], w, out=out, in_=in_, identity=idn)

        def MSET(eng, ap, val, r, w):
            S.op(eng, "memset", r, w, ap=ap, constant=val)

        ci = [0]

        def cast_dma(dst, src, res):
            S.dma("pool", "cast", dst, src, writes=[res])
            ci[0] += 1

        def ext_src(e):
            if e < 4:
                return e * 128
            if e < 12:
                return 1024 + (e - 4) * 128
            if e < 14:
                return None
            if e == 14:
                return 2176
            if e < 28:
                return 3328 + (e - 15) * 128
            if e < 32:
                return 512 + (e - 28) * 128
            if e < 40:
                return 2304 + (e - 32) * 128
            return 4992 + (e - 40) * 128

        def cast_layer(l):
            wv = w_in[l].rearrange("(kc p) c -> p kc c", p=128)
            for b in range(NBLK_IN):
                res = ("wib", l, b)
                s0, s1 = ext_src(2 * b), ext_src(2 * b + 1)
                if s0 is None:
                    for dc, sc in ((0, 2048), (64, 2048), (128, 2112), (192, 2112)):
                        cast_dma(wib[l, b, :, :, dc:dc + 64], wv[:, :, sc:sc + 64], res)
                elif s1 == s0 + 128:
                    cast_dma(wib[l, b], wv[:, :, s0:s0 + 256], res)
                else:
                    cast_dma(wib[l, b, :, :, 0:128], wv[:, :, s0:s0 + 128], res)
                    cast_dma(wib[l, b, :, :, 128:256], wv[:, :, s1:s1 + 128], res)
            wo = w_out[l].rearrange("(kc p) c -> p kc c", p=128)
            for b in range(NBLK_OUT):
                cast_dma(wob[l, b], wo[:, :, b * 256:(b + 1) * 256], ("wob", l, b))

        cast_layer(0)

        def ld(dst, src, res):
            S.dma("sp", "ld", dst, src, writes=[res])

        ld(ident[:], c_ident[:, :], "ident")
        ld(msk[:], c_msk[:, :, :], "msk")
        ld(cmask[:], c_cmask[:, :], "cmask")
        ld(rcnt[:], c_rcnt[:, :, :], "rcnt")
        ld(bmean[:], c_bones[:, :], "bmean")
        CP("dve", identb[:], ident[:], ["ident"], ["identb"])
        CP("dve", bones_b[:], bmean[:], ["bmean"], ["bones_b"])
        S.op("dve", "tensor_scalar_mul", ["bmean", "bones_b"], ["bmean"], out=bmean[:], in0=bmean[:], scalar1=1.0 / 64.0)
        MSET("pool", ones_b[:], 1.0, [], ["ones_b"])

        def load_T(dst, src, nrows, res):
            for r0 in range(0, nrows, 128):
                n = min(128, nrows - r0)
                S.dma("sp", "ld", stage[0:n, 0:128], src[r0:r0 + n, :], writes=["stage"])
                pa, pr = next_pa()
                TR(pa[:, 0:n], stage[0:n, 0:128], ident[0:n, 0:n], ["stage"], [pr])
                CP("dve", dst[:, r0:r0 + n], pa[:, 0:n], [pr], [res])

        load_T(nw[:, 0:L * 16], norm_w, L * 16, "nw")
        load_T(nw[:, L * 16:L * 16 + 16], fnorm_w, 16, "nw")
        load_T(v512, vec512, 8 * L * 4, "v512")
        load_T(muT, mu, L * 13, "muT")
        TS("dve", omu[:], muT[:], -1.0, 1.0, ALU.mult, ALU.add, ["muT"], ["omu"])

        S.dma("sp", "ld", es_raw[:], sinks.to_broadcast([128, L * 16]), writes=["es_raw"])
        ACTV(es_raw[:], es_raw[:], AF.Exp, ["es_raw"], ["es_raw"])
        for par in range(2):
            CP("dve", es_t[par * 64:(par + 1) * 64, :, :, :],
               es_raw[par * 64:(par + 1) * 64, :].rearrange("p (l kh g2 q) -> p l kh g2 q", l=L, kh=2, g2=4, q=2)[:, :, :, :, par],
               ["es_raw"], ["es_t"])

        for l in range(L):
            S.dma("sp", "ld", stage[:, 0:512].rearrange("p (g d) -> p g d", g=4), pool_w[l].rearrange("g c d -> c g d"), writes=["stage"])
            CP("dve", poolw_b[:, l * 4:(l + 1) * 4, :], stage[:, 0:512].rearrange("p (g d) -> p g d", g=4), ["stage"], ["poolw_b"])
            S.dma("sp", "ld", stage[0:64, 512:1024], w_up[l], writes=["stage"])
            S.dma("sp", "ld", stage[64:128, 512:1024], a_up[l], writes=["stage"])
            CP("dve", lora_b[:, l, :], stage[:, 512:1024], ["stage"], ["lora_b"])

        S.dma("sp", "ld", tab8[:], table[:, :], writes=["tab8"])
        S.op("dve", "tensor_scalar_mul", ["tab8"], ["tab8"], out=tab8[:], in0=tab8[:], scalar1=8.0)
        CP("dve", tab8p[:, :, :, :], tab8[:, :].rearrange("b (kh g2 q) -> b kh q g2", kh=2, g2=4, q=2), ["tab8"], ["tab8p"])
        ohs = stage[0:32, 0:768].rearrange("p (a s) -> p a s", s=192)
        MSET("pool", biasB[:], 0.0, [], ["biasB"])
        MSET("pool", biasB32[:], 0.0, [], ["biasB32"])
        for kh in range(2):
            for i0 in range(0, 64, 4):
                S.dma("sp", "ld", ohs, c_oh[:, i0:i0 + 4, :], writes=["stage"])
                for ii in range(4):
                    i = i0 + ii
                    rhs8 = tab8p[:, kh, :, :].rearrange("b q g -> b (q g)")
                    MM(psL1[:, :].rearrange("p (g i) -> p g i", i=64)[:, :, i], ohs[:, ii, 0:128], rhs8, ["stage", "tab8p"], ["psL1"])
                    MM(psL2[0:64, :].rearrange("p (g i) -> p g i", i=64)[:, :, i], ohs[:, ii, 128:192], rhs8, ["stage", "tab8p"], ["psL2"])
            CP("dve", biasA[:, 0, kh, :], psL1[:, :], ["psL1"], ["biasA"])
            CP("dve", biasB[0:64, kh, :], psL2[0:64, :], ["psL2"], ["biasB"])
        MSET("pool", biasNeg[:], NEGB, [], ["biasA"])
        CP("pool", biasA[:, 1, :, :], biasA[:, 0, :, :], ["biasA"], ["biasA"])
        MSET("pool", biasA[0:64, 1, :, :], NEGB, ["biasA"], ["biasA"])
        for kh in range(2):
            CP("pool", biasA32[:, kh, :].rearrange("p (a i) -> p a i", i=32), biasA[:, 0, kh, :].rearrange("p (a i) -> p a i", i=64)[:, :, 0:32], ["biasA"], ["biasA32"])
            CP("pool", biasB32[0:32, kh, :].rearrange("p (a i) -> p a i", i=32), biasB[0:32, kh, :].rearrange("p (a i) -> p a i", i=64)[:, :, 0:32], ["biasB"], ["biasB32"])

        hres = [("h", kc) for kc in range(NCH)]
        xnres = [("xn", kc) for kc in range(NCH)]
        mxres = [("mx", m) for m in range(NCH)]
        shres = [("shst", j) for j in range(13)]
        for t_, r_ in ((hist, ["hist"]), (khalo, ["khalo"]), (k32halo, ["k32halo"]), (vhalo, ["vhalo"]), (shst, shres), (ST, ["ST"])):
            MSET("pool", t_[:], 0.0, [], r_)

        wctr = [0]

        def proj(wsrc, wres, l, blocks, rhs_fn, rhs_res_fn, T, evac_fn, plevel=99):
            for b in blocks:
                proj_block(wsrc, wres, l, b, rhs_fn, rhs_res_fn, T, evac_fn, plevel)

        def proj_block(wsrc, wres, l, b, rhs_fn, rhs_res_fn, T, evac_fn, plevel=99):
            if True:
                slot = wctr[0] % NSLOT
                wctr[0] += 1
                S.dma("sp", ("w", slot), wslot[slot][:], wsrc[l, b], reads=[(wres, l, b)], writes=[("ws", slot)])
                for half in range(2):
                    pa, pr = next_pa()
                    for kc in range(NCH):
                        MM(pa[:, 0:T], wslot[slot][:, kc, half * 128:(half + 1) * 128], rhs_fn(kc), [("ws", slot), rhs_res_fn(kc)], [pr],
                           start=(kc == 0), stop=(kc == NCH - 1), signal=(kc == NCH - 1))
                    evac_fn(2 * b + half, pa, pr)
                    chk(plevel + (2 * b + half + 1) * 0.01)

        def rms(T):
            pa, pr = next_pa()
            for kc in range(NCH):
                ACTV(xn[:, kc, 0:T], h[:, kc, 0:T], AF.Square, [("h", kc)], [("xn", kc)])
            for kc in range(NCH):
                MM(pa[:, 0:T], ones_b[:], xn[:, kc, 0:T], [("xn", kc), "ones_b"], [pr], start=(kc == 0), stop=(kc == NCH - 1), signal=(kc == NCH - 1))
            ACTV(rstd[:, 0:T], pa[:, 0:T], AF.Ln, [pr], ["rstd"], scale=1.0 / D, bias=1e-5)
            ACTV(rstd[:, 0:T], rstd[:, 0:T], AF.Exp, ["rstd"], ["rstd"], scale=-0.5)

        def layer(l, g, T, Lc, first_tile):
            nchk = T // Lc
            K = int(round(math.log2(Lc)))

            def v4(ap, par):
                return ap.rearrange("p (g2 q i) -> p g2 q i", q=2, i=64)[:, :, par, 0:Lc]

            def v8(ap):
                return ap.rearrange("p (g i) -> p g i", i=64)[:, :, 0:Lc]

            def v4o(ap):
                return ap.rearrange("p (g2 i) -> p g2 i", i=64)[:, :, 0:Lc]

            def m8(k):
                return msk[0:Lc, k, 0:Lc].unsqueeze(1).to_broadcast([Lc, 8, Lc])

            rms(T)
            for kc in range(NCH):
                STT("dve", xn[:, kc, 0:T], h[:, kc, 0:T], nw[:, l * 16 + kc:l * 16 + kc + 1], rstd[:, 0:T], ALU.mult, ALU.mult,
                    [("h", kc), "rstd", "nw"], [("xn", kc)])
            chk(3)
            CP("pool", pext[:, :, 0:16], hist[:, l, :, :], ["hist"], ["pext_h"])
            CP("pool", kT[:, :, 0:128], khalo[:, l, :, :], ["khalo"], ["kT_h"])
            CP("pool", vT[:, 0:128], vhalo[:, l, :], ["vhalo"], ["vT_h"])
            CP("pool", k32[:, 0:128], k32halo[:, l, :], ["k32halo"], ["k32_h"])

            def evac1(e_, pa, pr):
                eng = evac_eng()
                if e_ < 4:
                    CP(eng, pext[:, e_, 16:16 + T], pa[:, 0:T], [pr], [("pext", e_)])
                elif e_ < 12:
                    CP(eng, qT[:, e_ - 4, 0:T], pa[:, 0:T], [pr], ["qT"])
                elif e_ < 14:
                    kh = e_ - 12
                    CP(eng, kT[:, kh, 128:128 + T], pa[:, 0:T], [pr], ["kT"])
                    if not os.environ.get("KNO32"):
                        eng2 = eng
                        CP(eng2, k32[kh * 64:(kh + 1) * 64, 128:128 + T], pa[kh * 64:(kh + 1) * 64, 0:T], [pr], ["k32"])
                elif e_ == 14:
                    CP(eng, vT[:, 128:128 + T], pa[:, 0:T], [pr], ["vT"])
                else:
                    j = e_ - 15
                    dst = (Rr[:, j, :] if j < 4 else Kr[:, j - 4, :] if j < 8 else Vr[:, j - 8, :] if j < 12 else WA[:, :])
                    dres = ("xs", j)
                    tmp, tres = tm[j % 2], f"tm{j % 2}"
                    mcol = muT[:, l * 13 + j:l * 13 + j + 1]
                    ocol = omu[:, l * 13 + j:l * 13 + j + 1]
                    ACTV(tmp[:, 0:T], pa[:, 0:T], AF.Copy, [pr, "omu"], [tres], scale=ocol)
                    STT("dve", dst[:, 1:T], pa[:, 0:T - 1], mcol, tmp[:, 1:T], ALU.mult, ALU.add, [pr, tres, "muT"], [dres])
                    STT("dve", dst[:, 0:1], shst[:, l, j:j + 1], mcol, tmp[:, 0:1], ALU.mult, ALU.add, [tres, "muT", ("shst", j)], [dres])
                    CP("dve", shst[:, l, j:j + 1], pa[:, T - 1:T], [pr], [("shst", j)])

            proj(wib, "wib", l, range(0, 14), lambda kc: xn[:, kc, 0:T], lambda kc: ("xn", kc), T, evac1, plevel=3)

            chk(4)
            allp = [("pext", i) for i in range(4)] + ["pext_h"]
            W = 16 + T
            TTO("dve", ptA[:, :, 1:W], pext[:, :, 1:W], pext[:, :, 0:W - 1], ALU.add, allp, ["ptA"])
            CP("pool", hist[:, l, :, :], pext[:, :, T:T + 16], allp, ["hist"])
            TTO("pool", ptB[:, 1:4, 3:W], ptA[:, 1:4, 3:W], ptA[:, 1:4, 1:W - 2], ALU.add, ["ptA"], ["ptB"])
            TTO("dve", ptA[:, 2:4, 7:W], ptB[:, 2:4, 7:W], ptB[:, 2:4, 3:W - 4], ALU.add, ["ptB"], ["ptA"])
            TTO("pool", ptB[:, 3:4, 15:W], ptA[:, 3:4, 15:W], ptA[:, 3:4, 7:W - 8], ALU.add, ["ptA"], ["ptB"])
            for gi, w in enumerate((2, 4, 8, 16)):
                src, sres = (ptA, "ptA") if gi % 2 == 0 else (ptB, "ptB")
                STT("dve", pd[:, gi, 0:T], src[:, gi, 16:16 + T], 1.0 / w, pext[:, gi, 16:16 + T], ALU.mult, ALU.subtract, [sres] + allp, ["pd"])
                if first_tile and g == 0:
                    TTO("dve", tm[2][:, 0:16], src[:, gi, 16:32], rcnt[:, gi, :], ALU.mult, [sres, "rcnt"], ["tm2"])
                    TTO("dve", pd[:, gi, 0:16], tm[2][:, 0:16], pext[:, gi, 16:32], ALU.subtract, ["tm2"] + allp, ["pd"])
            for gi in range(4):
                pa, pr = next_pa()
                MM(pa[:, 0:T], poolw_b[:, l * 4 + gi, :], pd[:, gi, 0:T], ["pd", "poolw_b"], [pr])
                ACTV(mixed[:, gi, 0:T], pa[:, 0:T], AF.Copy, [pr, "v512"], [("mx", gi)], scale=vec(0, l, gi))

            chk(5)
            kres = ["kT", "kT_h"]
            vres = ["vT", "vT_h"]
            for c in range(nchk):
                TR(psL1[:, c * 128:(c + 1) * 128], vT[:, Lc * c:Lc * c + 128], ident[:], vres, ["psL1"])
                TR(psL2[0:Lc, c * 128:(c + 1) * 128], vT[:, 128 + Lc * c:128 + Lc * (c + 1)], ident[:], vres, ["psL2"])
            CP("dve", Vwin[:, 0:nchk, :], psL1[:, 0:nchk * 128].rearrange("p (c k) -> p c k", k=128), ["psL1"], ["Vwin"])
            CP("act", Vcur[0:Lc, 0:nchk, :], psL2[0:Lc, 0:nchk * 128].rearrange("p (c k) -> p c k", k=128), ["psL2"], ["Vcur"])
            CP("pool", khalo[:, l, :, :], kT[:, :, T:T + 128], kres, ["khalo"])
            CP("pool", vhalo[:, l, :], vT[:, T:T + 128], vres, ["vhalo"])

            NQ = 4 * Lc

            def q3(ap):
                return ap.rearrange("p (g2 i) -> p g2 i", i=Lc)

            fill = []

            def fill_one(n=1):
                for _ in range(n):
                    if fill:
                        fill.pop(0)()

            def att_unit(kh, c):
                if True:
                    var = 0
                    if first_tile and g == 0:
                        var = 2 if c == 0 else (1 if c == 1 else 0)
                    if Lc == 64:
                        bA, bB, bres = (biasNeg[:, :] if var == 2 else biasA[:, var, kh, :]), biasB[:, kh, :], ["biasA", "biasB"]
                    else:
                        bA, bB, bres = biasA32[:, kh, :], biasB32[:, kh, :], ["biasA32", "biasB32"]
                    e1, e2 = E1[0], E2[0]
                    r1, r2 = "E1_0", "E2_0"
                    qs = slice(Lc * c, Lc * (c + 1))
                    def lmm(par, first, last):
                        ps_ = slice(par * 64, (par + 1) * 64)
                        cs_ = slice(par * NQ, (par + 1) * NQ)
                        MM(psL1[:, cs_], kT[ps_, kh, Lc * c:Lc * c + 128], qT[ps_, 4 * kh:4 * kh + 4, qs], kres + ["qT"], ["psL1"],
                           start=first, stop=last, signal=last, tp=(par * 64, 0))
                        MM(psL2[0:Lc, cs_], kT[ps_, kh, 128 + Lc * c:128 + Lc * (c + 1)], qT[ps_, 4 * kh:4 * kh + 4, qs], kres + ["qT"], ["psL2"],
                           start=first, stop=last, signal=last, tp=(par * 64, 0))
                    for par in range(2):
                        cs_ = slice(par * NQ, (par + 1) * NQ)
                        MM(psL1[:, cs_], identb[:], bA[:, cs_], ["identb"] + bres, ["psL1"], start=(par == 0), stop=False, signal=False)
                        MM(psL2[0:Lc, cs_], identb[:, 0:Lc], bB[:, cs_], ["identb"] + bres, ["psL2"], start=(par == 0), stop=False, signal=False)
                        lmm(par, False, par == 1)
                    ACTV(e1[:, 0:2 * NQ], psL1[:, 0:2 * NQ], AF.Exp, ["psL1"], [r1], scale=0.125)
                    ACTV(e2[0:Lc, 0:2 * NQ], psL2[0:Lc, 0:2 * NQ], AF.Exp, ["psL2"], [r2], scale=0.125)
                    for par in range(2):
                        ps_ = slice(par * 64, (par + 1) * 64)
                        cs_ = slice(par * NQ, (par + 1) * NQ)
                        hs = slice(kh * 64, (kh + 1) * 64)
                        MM(psO[ps_, 0:NQ], Vwin[:, c, hs], e1[:, cs_], ["Vwin", r1], ["psO"], start=True, stop=False, signal=False, tp=(0, par * 64))
                        MM(psO[ps_, 0:NQ], Vcur[0:Lc, c, hs], e2[0:Lc, cs_], ["Vcur", r2], ["psO"], start=False, stop=False, signal=False, tp=(0, par * 64))
                        MM(psO[ps_, 256:256 + NQ], ones_b[:, 0:64], e1[:, cs_], ["ones_b", r1], ["psO"], start=False, stop=False, signal=False, tp=(0, par * 64))
                        MM(psO[ps_, 256:256 + NQ], ones_b[0:Lc, 0:64], e2[0:Lc, cs_], ["ones_b", r2], ["psO"], start=False, stop=True, signal=(par == 1), tp=(0, par * 64))
                    TTO("dve", q3(rden[:, 0:NQ]), q3(psO[:, 256:256 + NQ]), es_t[:, l, kh, :].unsqueeze(2).to_broadcast([128, 4, Lc]), ALU.add, ["psO", "es_t"], ["rden"])
                    ACTV(rden[:, 0:NQ], rden[:, 0:NQ], AF.Ln, ["rden"], ["rden"])
                    ACTV(rden[:, 0:NQ], rden[:, 0:NQ], AF.Exp, ["rden"], ["rden"], scale=-1.0)
                    TTO("dve", mixed[:, 4 + 4 * kh:8 + 4 * kh, qs], q3(psO[:, 0:NQ]), q3(rden[:, 0:NQ]), ALU.mult, ["psO", "rden"],
                        [("mx", 4 + 4 * kh + j) for j in range(4)])

            for kh in range(2):
                for c in range(nchk):
                    fill.append(lambda kh=kh, c=c: att_unit(kh, c))
            n_per_pair = (len(fill) + 3) // 4

            def evac2(e_, pa, pr):
                m = e_ - 28
                i = m % 2
                ACTV(sgt[i][:, 0:T], pa[:, 0:T], AF.Silu, [pr], [f"sgt{i}"])
                TTO("pool" if m % 2 else "dve", mixed[:, m, 0:T], mixed[:, m, 0:T], sgt[i][:, 0:T], ALU.mult, [("mx", m), f"sgt{i}"], [("mx", m)])


            chk(6)
            xsr = [("xs", j) for j in range(13)]
            ACTV(twal[0:64, 0:T], WA[0:64, 0:T], AF.Tanh, xsr, ["twal"])
            CP("pool", twal[64:128, 0:T], WA[64:128, 0:T], xsr, ["twal2"])
            CP("pool", STb[:], ST[:, l, :, :], ["ST"], ["STb"])

            def c3(ap):
                return ap[:, 0:T].rearrange("p (c t) -> p c t", t=Lc)

            tr = lambda i: f"tm{i}"
            for p in range(4):
                cs128 = slice(p * 128, (p + 1) * 128)
                paw, prw = next_pa()
                paa, pra = next_pa()
                MM(paw[:, 0:T], lora_b[0:64, l, cs128], twal[0:64, 0:T], ["lora_b", "twal"], [prw])
                MM(paa[:, 0:T], lora_b[64:128, l, cs128], twal[64:128, 0:T], ["lora_b", "twal2"], [pra], tp=(64, 0))
                ACTV(tm[0][:, 0:T], paw[:, 0:T], AF.Sigmoid, [prw, "v512"], [tr(0)], bias=vec(1, l, p))
                ACTV(tm[1][:, 0:T], paa[:, 0:T], AF.Sigmoid, [pra, "v512"], [tr(1)], bias=vec(2, l, p))
                fill_one()
                S.op("dve", "tensor_tensor_scan", [tr(0), "cmask"], [tr(2)], out=tm[2][:, 0:T], data0=cmask[:, 0:T], data1=tm[0][:, 0:T], initial=0.0, op0=ALU.mult, op1=ALU.add)
                TTO("pool", tm[3][:, 0:T], tm[2][:, 0:T], tm[0][:, 0:T], ALU.subtract, [tr(2), tr(0)], [tr(3)])
                ACTV(tm[4][:, 0:T], tm[2][:, 0:T], AF.Exp, [tr(2)], [tr(4)], scale=CDEC)
                ACTV(tm[5][:, 0:T], tm[2][:, 0:T], AF.Exp, [tr(2)], [tr(5)], scale=-CDEC)
                ACTV(tm[6][:, 0:T], tm[3][:, 0:T], AF.Exp, [tr(3)], [tr(6)], scale=-CDEC)
                for hb in range(2):
                    CP("pool", glast[:, 2 * p + hb, 0:nchk], c3(tm[5])[hb * 64:(hb + 1) * 64, :, Lc - 1], [tr(5)], ["glast"])
                TTO("dve", c3(tm[7]), c3(tm[2]), c3(tm[2])[:, :, Lc - 1:Lc].to_broadcast([128, nchk, Lc]), ALU.subtract, [tr(2)], [tr(7)])
                ACTV(tm[7][:, 0:T], tm[7][:, 0:T], AF.Exp, [tr(7)], [tr(7)], scale=CDEC)
                S.op("dve", "tensor_scalar_mul", xsr + ["v512"], [tr(8)], out=tm[8][:, 0:T], in0=Kr[:, p, 0:T], scalar1=vec(3, l, p))
                ACTV(tmb[:, 0:T], tm[8][:, 0:T], AF.Square, [tr(8)], ["tmb"])
                pas, prs = next_pa()
                MM(pas[:, 0:T], bones_b[:], tmb[:, 0:T], ["bones_b", "tmb"], [prs])
                fill_one()
                S.op("dve", "tensor_scalar_max", [prs], [tr(9)], out=tm[9][:, 0:T], in0=pas[:, 0:T], scalar1=1e-24)
                ACTV(tm[9][:, 0:T], tm[9][:, 0:T], AF.Ln, [tr(9)], [tr(9)])
                ACTV(tm[9][:, 0:T], tm[9][:, 0:T], AF.Exp, [tr(9)], [tr(9)], scale=-0.5)
                TTO("dve", tm[8][:, 0:T], tm[8][:, 0:T], tm[9][:, 0:T], ALU.mult, [tr(8), tr(9)], [tr(8)])
                TS("dve", tm[9][:, 0:T], tm[1][:, 0:T], -1.0, vec(4, l, p), ALU.add, ALU.mult, [tr(1), "v512", tr(8)], [tr(9)])
                STT("dve", tm[9][:, 0:T], tm[9][:, 0:T], 1.0, Kr[:, p, 0:T], ALU.add, ALU.mult, [tr(9)] + xsr, [tr(9)])
                STT("dve", tmb2[:, 0:T], Rr[:, p, 0:T], vec(5, l, p), tm[9][:, 0:T], ALU.mult, ALU.mult, [tr(9), "v512"] + xsr, ["tmb2"])
                pab, prb = next_pa()
                MM(pab[:, 0:T], bones_b[:], tmb2[:, 0:T], ["bones_b", "tmb2"], [prb])
                TTO("dve", bonus[:, p, 0:T], pab[:, 0:T], Vr[:, p, 0:T], ALU.mult, [prb] + xsr, ["bonus"])
                TTO("dve", tm[11][:, 0:T], tm[8][:, 0:T], tm[1][:, 0:T], ALU.mult, [tr(8), tr(1)], [tr(11)])
                for hb in range(2):
                    hp = slice(hb * 64, (hb + 1) * 64)
                    hh_ = 2 * p + hb
                    TTO("dve", rt[:, hh_, 0:T], Rr[hp, p, 0:T], tm[5][hp, 0:T], ALU.mult, [tr(5)] + xsr, ["rt"])
                    TTO("pool", kt[:, hh_, 0:T], tm[9][hp, 0:T], tm[4][hp, 0:T], ALU.mult, [tr(9), tr(4)], ["kt"])
                    TTO("pool", bt[:, hh_, 0:T], tm[11][hp, 0:T], tm[4][hp, 0:T], ALU.mult, [tr(11), tr(4)], ["bt"])
                    STT("dve", at[:, hh_, 0:T], tm[8][hp, 0:T], -1.0, tm[6][hp, 0:T], ALU.mult, ALU.mult, [tr(8), tr(6)], ["at"])
                TTO("dve", tm[0][:, 0:T], tm[9][:, 0:T], tm[7][:, 0:T], ALU.mult, [tr(9), tr(7)], [tr(0)])
                TTO("pool", tm[2][:, 0:T], tm[11][:, 0:T], tm[7][:, 0:T], ALU.mult, [tr(11), tr(7)], [tr(2)])
                for srcb, sres, dstT, dres, bank in ((Vr[:, p, :], xsr, Vt, "Vt", 0), (tm[0], [tr(0)], Kpt, "Kpt", 1), (tm[2], [tr(2)], Bpt, "Bpt", 2)):
                    for c in range(nchk):
                        TR(psR[bank][0:Lc, c * 128:(c + 1) * 128], srcb[:, c * Lc:(c + 1) * Lc], ident[:], list(sres), [f"psR{bank}"])
                    CP(evac_eng(), dstT[0:Lc, 0:nchk, cs128], psR[bank][0:Lc, 0:nchk * 128].rearrange("p (c k) -> p c k", k=128), [f"psR{bank}"], [dres])

            fill_one(len(fill))
            chk(7)

            def blk(t_, hh):
                return t_[0:Lc, hh * 64:hh * 64 + Lc]

            base = lambda hh: (hh % 2) * 64

            def hT(t_, hh, c):
                return t_[:, hh, c * Lc:(c + 1) * Lc]

            def mm8(bank, lhs_fn, rhs_fn, reads, based):
                for hh in range(8):
                    MM(blk(psR[bank], hh), lhs_fn(hh), rhs_fn(hh), reads, [f"psR{bank}"], signal=(hh == 7))

            for c in range(nchk):
                mm8(0, lambda hh: hT(bt, hh, c), lambda hh: hT(at, hh, c), ["bt", "at"], True)
                mm8(1, lambda hh: hT(at, hh, c), lambda hh: hT(bt, hh, c), ["bt", "at"], True)
                mm8(2, lambda hh: hT(kt, hh, c), lambda hh: hT(at, hh, c), ["kt", "at"], True)
                chk(7.01)
                TTO("dve", v8(Pm[0][0:Lc, :]), v8(psR[0][0:Lc, :]), m8(0), ALU.mult, ["psR0", "msk"], ["Pm0"])
                chk(7.02)
                TTO("dve", v8(Qm[0][0:Lc, :]), v8(psR[1][0:Lc, :]), m8(1), ALU.mult, ["psR1", "msk"], ["Qm0"])
                TTO("dve", v8(AakT[0:Lc, :]), v8(psR[2][0:Lc, :]), m8(0), ALU.mult, ["psR2", "msk"], ["AakT"])
                chk(7.04)
                TTO("pool", v8(Rm[0][0:Lc, :]), v8(Pm[0][0:Lc, :]), m8(3), ALU.add, ["Pm0", "msk"], ["Rm0"])
                chk(7.05)
                mm8(0, lambda hh: hT(bt, hh, c), lambda hh: hT(rt, hh, c), ["bt", "rt"], True)
                mm8(1, lambda hh: hT(kt, hh, c), lambda hh: hT(rt, hh, c), ["kt", "rt"], True)
                TTO("dve", v8(ArbT[0:Lc, :]), v8(psR[0][0:Lc, :]), m8(2), ALU.mult, ["psR0", "msk"], ["ArbT"])
                TTO("dve", v8(ArkT[0:Lc, :]), v8(psR[1][0:Lc, :]), m8(2), ALU.mult, ["psR1", "msk"], ["ArkT"])
                chk(7.1)
                for s_ in range(1, K + 1):
                    cur, nxt = (s_ - 1) % 2, s_ % 2
                    ro, rn = (s_ - 2) % 2, (s_ - 1) % 2
                    if s_ <= K - 2:
                        mm8(0, lambda hh: blk(Qm[cur], hh), lambda hh: blk(Pm[cur], hh), [f"Qm{cur}", f"Pm{cur}"], False)
                    if s_ <= K - 1:
                        mm8(1, lambda hh: blk(Pm[cur], hh), lambda hh: blk(Qm[cur], hh), [f"Qm{cur}", f"Pm{cur}"], False)
                    if s_ >= 2:
                        mm8(2, lambda hh: blk(Qm[cur], hh), lambda hh: blk(Rm[ro], hh), [f"Qm{cur}", f"Rm{ro}"], False)
                    if s_ % 2 == 1:
                        fill_one()
                    if s_ <= K - 2:
                        CP("act", v8(Pm[nxt][0:Lc, :]), v8(psR[0][0:Lc, :]), ["psR0"], [f"Pm{nxt}"])
                    if s_ <= K - 1:
                        CP("act", v8(Qm[nxt][0:Lc, :]), v8(psR[1][0:Lc, :]), ["psR1"], [f"Qm{nxt}"])
                    if s_ >= 2:
                        TTO("dve", v8(Rm[rn][0:Lc, :]), v8(psR[2][0:Lc, :]), v8(Rm[ro][0:Lc, :]), ALU.add, ["psR2", f"Rm{ro}"], [f"Rm{rn}"])
                MT = Rm[(K - 1) % 2]
                mres = f"Rm{(K - 1) % 2}"
                chk(7.2)
                for hh in range(8):
                    hs = slice(hh * 64, (hh + 1) * 64)
                    b_ = base(hh)
                    MM(psR[0][0:Lc, hs], blk(AakT, hh), Vt[0:Lc, c, hs], ["AakT", "Vt"], ["psR0"], start=True, stop=False, signal=False)
                    MM(psR[0][0:Lc, hs], hT(at, hh, c), STb[:, hh, :], ["at", "STb"], ["psR0"], start=False, stop=True, signal=(hh == 7))
                CP("act", RHSb[0:Lc, :], psR[0][0:Lc, :], ["psR0"], ["RHSb"])
                for hh in range(8):
                    hs = slice(hh * 64, (hh + 1) * 64)
                    MM(psR[1][0:Lc, hs], blk(MT, hh), RHSb[0:Lc, hs], [mres, "RHSb"], ["psR1"], signal=(hh == 7))
                CP("act", SAb[0:Lc, :], psR[1][0:Lc, :], ["psR1"], ["SAb"])
                chk(7.3)
                for hh in range(8):
                    hs = slice(hh * 64, (hh + 1) * 64)
                    b_ = base(hh)
                    oo = psR[2][b_:b_ + 64, (hh // 2) * 64:(hh // 2) * 64 + Lc]
                    MM(oo, STb[:, hh, :], hT(rt, hh, c), ["STb", "rt"], ["psR2"], start=True, stop=False, signal=False, tp=(0, b_))
                    MM(oo, SAb[0:Lc, hs], blk(ArbT, hh), ["SAb", "ArbT"], ["psR2"], start=False, stop=False, signal=False, tp=(0, b_))
                    MM(oo, Vt[0:Lc, c, hs], blk(ArkT, hh), ["Vt", "ArkT"], ["psR2"], start=False, stop=True, signal=(hh == 7), tp=(0, b_))
                CP("act", OT[:, :, c * Lc:(c + 1) * Lc], v4o(psR[2][:, 0:256]), ["psR2"], ["OT"])
                chk(7.4)
                for hh in range(8):
                    hs = slice(hh * 64, (hh + 1) * 64)
                    b_ = base(hh)
                    oo = psR[0][0:64, hs]
                    MM(oo, Bpt[0:Lc, c, hs], SAb[0:Lc, hs], ["Bpt", "SAb"], ["psR0"], start=True, stop=False, signal=False)
                    MM(oo, Kpt[0:Lc, c, hs], Vt[0:Lc, c, hs], ["Kpt", "Vt"], ["psR0"], start=False, stop=True, signal=(hh == 7))
                TTO("pool", STg[:], ST[:, l, :, :], glast[:, :, c:c + 1].to_broadcast([64, 8, 64]), ALU.mult, ["ST", "glast"], ["STg"])
                ps_s = psR[0][0:64, :].rearrange("p (a i) -> p a i", i=64)
                TTO("dve", STb[:], STg[:], ps_s, ALU.add, ["STg", "psR0"], ["STb"])
                TTO("dve", ST[:, l, :, :], STg[:], ps_s, ALU.add, ["STg", "psR0"], ["ST"])

            chk(8)
            fill_one(len(fill))
            gt = lambda p, i: (tm[3 * p + i], f"tm{3 * p + i}")
            pms = []
            for p in range(4):
                pam, prm = next_pa()
                MM(pam[:, 0:T], bmean[:], OT[:, p, 0:T], ["bmean", "OT"], [prm])
                TTO("dve", gt(p, 0)[0][:, 0:T], OT[:, p, 0:T], pam[:, 0:T], ALU.subtract, ["OT", prm], [gt(p, 0)[1]])
                ACTV(gt(p, 1)[0][:, 0:T], gt(p, 0)[0][:, 0:T], AF.Square, [gt(p, 0)[1]], [gt(p, 1)[1]])
            for p in range(4):
                pav, prv = next_pa()
                MM(pav[:, 0:T], bmean[:], gt(p, 1)[0][:, 0:T], ["bmean", gt(p, 1)[1]], [prv])
                ACTV(gt(p, 2)[0][:, 0:T], pav[:, 0:T], AF.Ln, [prv], [gt(p, 2)[1]], bias=64e-5)
            for p in range(4):
                ACTV(gt(p, 2)[0][:, 0:T], gt(p, 2)[0][:, 0:T], AF.Exp, [gt(p, 2)[1]], [gt(p, 2)[1]], scale=-0.5)
            for p in range(4):
                a0_, r0_ = gt(p, 0)
                TTO("dve", a0_[:, 0:T], a0_[:, 0:T], gt(p, 2)[0][:, 0:T], ALU.mult, [r0_, gt(p, 2)[1]], [r0_])
                TS("dve", a0_[:, 0:T], a0_[:, 0:T], vec(6, l, p), vec(7, l, p), ALU.mult, ALU.add, [r0_, "v512"], [r0_])
                TTO("dve", mixed[:, 12 + p, 0:T], a0_[:, 0:T], bonus[:, p, 0:T], ALU.add, [r0_, "bonus"], [("mx", 12 + p)])

            chk(9)
            proj(wib, "wib", l, range(14, 22), lambda kc: xn[:, kc, 0:T], lambda kc: ("xn", kc), T, evac2)

            chk(10)
            def evac3(e_, pa, pr):
                TTO("dve", h[:, e_, 0:T], h[:, e_, 0:T], pa[:, 0:T], ALU.add, [pr, ("h", e_)], [("h", e_)])

            proj(wob, "wob", l, range(0, NBLK_OUT), lambda kc: mixed[:, kc, 0:T], lambda kc: ("mx", kc), T, evac3)

        def load_x(src, row0, T):
            nb = (T + 127) // 128
            for tb in range(nb):
                n = min(128, T - tb * 128)
                for hf in range(2):
                    S.dma("sp", "x", stage[0:n, :], src[row0 + tb * 128:row0 + tb * 128 + n, hf * 1024:(hf + 1) * 1024], writes=["stage"])
                    for k8 in range(8):
                        kc = hf * 8 + k8
                        pa, pr = next_pb()
                        TR(pa[:, 0:n], stage[0:n, k8 * 128:(k8 + 1) * 128], ident[0:n, 0:n], ["stage"], [pr])
                        CP(evac_eng(), h[:, kc, tb * 128:tb * 128 + n], pa[:, 0:n], [pr], [("h", kc)])

        def store_y(dst, row0, T):
            rms(T)
            nb = (T + 127) // 128
            for tb in range(nb):
                n = min(128, T - tb * 128)
                ts = slice(tb * 128, tb * 128 + n)
                for hf in range(2):
                    for k8 in range(8):
                        kc = hf * 8 + k8
                        t_ = tm[kc % 8]
                        STT("dve", t_[:, 0:n], h[:, kc, ts], nw[:, L * 16 + kc:L * 16 + kc + 1], rstd[:, ts], ALU.mult, ALU.mult, [("h", kc), "rstd", "nw"], [f"tm{kc % 8}"])
                        pa, pr = next_pb()
                        TR(pa[0:n, 0:128], t_[:, 0:n], ident[:], [f"tm{kc % 8}"], [pr])
                        CP(evac_eng(), stage[0:n, k8 * 128:(k8 + 1) * 128], pa[0:n, 0:128], [pr], ["stage"])
                    S.dma("pool", "yout", dst[row0 + tb * 128:row0 + tb * 128 + n, hf * 1024:(hf + 1) * 1024], stage[0:n, :], reads=["stage"])

        def layer_state_kv(g, l, T):
            o = outs["p" if g == 0 else "s"]
            for nm, srcb, sres, ti_ in (("k", k32, ["k32", "k32_h"], 10), ("v", vT, ["vT", "vT_h"], 11)):
                pa, pr = next_pa()
                TR(pa[:, 0:128], srcb[:, T:T + 128], ident[:], sres, [pr])
                CP("dve", tm[ti_][:, 0:128], pa[:, 0:128], [pr], [f"tm{ti_}"])
                S.dma("pool", "sout", o[nm][l], tm[ti_][:, 0:128], reads=[f"tm{ti_}"])

        def store_states(g):
            o = outs["p" if g == 0 else "s"]
            for l in range(L):
                for gi in range(4):
                    pa, pr = next_pa()
                    TR(pa[0:16, 0:128], hist[:, l, gi, :], ident[:], ["hist"], [pr])
                    CP("dve", stage[0:16, gi * 128:(gi + 1) * 128], pa[0:16, 0:128], [pr], ["stage"])
                S.dma("pool", "sout", o["pool"][l], stage[1:16, 0:512], reads=["stage"])
                pa, pr = next_pa()
                TR(pa[0:13, 0:128], shst[:, l, :], ident[:], shres, [pr])
                CP("dve", stage[0:13, 512:640], pa[0:13, 0:128], [pr], ["stage"])
                S.dma("pool", "sout", o["shift"][l], stage[0:13, 512:640], reads=["stage"])
                for q4 in range(2):
                    pa, pr = next_pa()
                    for hq in range(4):
                        TR(pa[0:64, hq * 64:(hq + 1) * 64], ST[:, l, 4 * q4 + hq, :], ident[0:64, 0:64], ["ST"], [pr])
                    CP("dve", stage[0:64, 512 + q4 * 256:512 + (q4 + 1) * 256], pa[0:64, 0:256], [pr], ["stage"])
                S.dma("pool", "sout", o["wkv"][l].rearrange("h i j -> i h j"), stage[0:64, 512:1024].rearrange("i (h j) -> i h j", h=8), reads=["stage"])

        def load_sample_states():
            for l in range(L):
                S.dma("sp", "ld", stage[0:15, 0:512], st_pool[l], writes=["stage"])
                for gi in range(4):
                    pa, pr = next_pa()
                    TR(pa[:, 0:15], stage[0:15, gi * 128:(gi + 1) * 128], ident[0:15, 0:15], ["stage"], [pr])
                    CP("dve", hist[:, l, gi, 1:16], pa[:, 0:15], [pr], ["hist"])
                S.dma("sp", "ld", stage[:, 512:640], st_k[l], writes=["stage"])
                S.dma("sp", "ld", stage[:, 640:768], st_v[l], writes=["stage"])
                pa, pr = next_pa()
                TR(pa[:, 0:128], stage[:, 512:640], ident[:], ["stage"], [pr])
                CP("dve", k32halo[:, l, :], pa[:, 0:128], [pr], ["k32halo"])
                for kh in range(2):
                    for half in range(2):
                        CP("dve", khalo[half * 64:(half + 1) * 64, l, kh, :], pa[kh * 64:(kh + 1) * 64, 0:128], [pr], ["khalo"])
                pa, pr = next_pa()
                TR(pa[:, 0:128], stage[:, 640:768], ident[:], ["stage"], [pr])
                CP("dve", vhalo[:, l, :], pa[:, 0:128], [pr], ["vhalo"])
                S.dma("sp", "ld", stage[0:13, 768:896], st_shift[l], writes=["stage"])
                pa, pr = next_pa()
                TR(pa[:, 0:13], stage[0:13, 768:896], ident[0:13, 0:13], ["stage"], [pr])
                CP("dve", shst[:, l, :], pa[:, 0:13], [pr], shres)
                S.dma("sp", "ld", stage[0:64, 0:512].rearrange("i (h j) -> i h j", h=8), st_wkv[l].rearrange("h i j -> i h j"), writes=["stage"])
                for q4 in range(2):
                    pa, pr = next_pa()
                    for hq in range(4):
                        hh_ = 4 * q4 + hq
                        TR(pa[0:64, hq * 64:(hq + 1) * 64], stage[0:64, hh_ * 64:(hh_ + 1) * 64], ident[0:64, 0:64], ["stage"], [pr])
                    CP("dve", ST[:, l, 4 * q4:4 * q4 + 4, :], pa[0:64, 0:256].rearrange("p (a i) -> p a i", i=64), [pr], ["ST"])

        def main_program():
            for ti in range(NT):
                load_x(xp, ti * TT, TT)
                chk(2)
                for l in range(L):
                    if ti == 0 and l + 1 < L:
                        cast_layer(l + 1)
                    layer(l, 0, TT, 64, ti == 0)
                    if ti == NT - 1:
                        layer_state_kv(0, l, TT)
                store_y(yp, ti * TT, TT)
                chk(20)
            store_states(0)
            chk(21)
            load_sample_states()
            load_x(xs, 0, DEC)
            chk(22)
            for l in range(L):
                layer(l, 1, DEC, 32, False)
                layer_state_kv(1, l, DEC)
            store_y(ys, 0, DEC)
            store_states(1)

        try:
            chk(1)
            main_program()
        except _Stop:
            pass
        for _i in range(int(os.environ.get("KEXTRA", "0"))):
            S.dma("sp", "ld", stage[0:32, 0:16], table[:, :], writes=["stage"])
        S.finish("pool")
        S.finish("sp")
        S.emit(block)
        print("[kernel] dma counts", {k: sum(1 for it in v if it[0] == "o" and it[1] == "dma_start") for k, v in S.q.items()}, "sem max", {k: S.cnt[k] for k in S.ENG}, flush=True)
        print(f"[kernel] ops={S.nops} q=" + ",".join(f"{k}:{len(v)}" for k, v in S.q.items()), flush=True)
    return nc


_NC_CACHE = {}


def _run(inputs, SEQ, DEPTH, n_prompt, n_sample, n_cores=8):
    L = DEPTH
    key = (SEQ, DEPTH)
    if key not in _NC_CACHE:
        _NC_CACHE[key] = build_nc(SEQ, DEPTH)
    nc = _NC_CACHE[key]
    f = lambda a: np.ascontiguousarray(np.asarray(a, dtype=np.float32))
    I = {k: f(v) for k, v in inputs.items()}
    consts = host_consts()
    vec512 = np.stack([I["pool_scale"], I["rwkv_w0"], I["rwkv_a0"], I["rwkv_k_k"], I["rwkv_k_a"], I["rwkv_r_k"].reshape(L, 512),
                       I["rwkv_lnx_w"], I["rwkv_lnx_b"]], axis=0).reshape(8 * L * 4, 128)
    shared = dict(
        norm_w=I["norm_w"].reshape(L * 16, 128), fnorm_w=I["final_norm_w"].reshape(16, 128), w_in=I["w_in"], w_out=I["w_out"],
        pool_w=I["pool_w"], vec512=np.ascontiguousarray(vec512), sinks=I["attn_sinks"].reshape(1, L * 16), table=I["rel_bias_table"],
        mu=I["rwkv_mu"].reshape(L * 13, 128), w_up=I["rwkv_w_up"], a_up=I["rwkv_a_up"], **consts)
    in_maps = []
    for c in range(n_cores):
        bp = c % n_prompt
        bs = c % n_sample
        m = dict(shared)
        m["xp"] = I["x_prompt"][bp]
        m["xs"] = I["x_sample"][bs]
        m["st_pool"] = np.ascontiguousarray(I["state_pool"][:, bs])
        m["st_k"] = np.ascontiguousarray(I["cache_swa_k"][:, bs].reshape(L, 128, 128))
        m["st_v"] = np.ascontiguousarray(I["cache_swa_v"][:, bs].reshape(L, 128, 128))
        m["st_shift"] = np.ascontiguousarray(I["state_rwkv_shift"][:, bs].reshape(L, 13, 128))
        m["st_wkv"] = np.ascontiguousarray(I["state_rwkv_wkv"][:, bs])
        in_maps.append(m)
    res = run_bass_kernel_spmd(nc, in_maps, core_ids=list(range(n_cores)))
    R = res.results
    n_prompt = min(n_prompt, n_cores)
    n_sample = min(n_sample, n_cores)
    y_prompt = np.stack([R[b]["yp"] for b in range(n_prompt)], axis=0)
    y_sample = np.stack([R[b]["ys"] for b in range(n_sample)], axis=0)

    def gather(g, n):
        pool = np.stack([R[b][f"o_pool_{g}"] for b in range(n)], axis=1)
        k = np.stack([R[b][f"o_k_{g}"].reshape(L, 128, 2, 64) for b in range(n)], axis=1)
        v = np.stack([R[b][f"o_v_{g}"].reshape(L, 128, 2, 64) for b in range(n)], axis=1)
        sh = np.stack([R[b][f"o_shift_{g}"].reshape(L, DSH) for b in range(n)], axis=1)
        wkv = np.stack([R[b][f"o_wkv_{g}"] for b in range(n)], axis=1)
        return [pool, k, v, sh, wkv]

    outs = [y_prompt, y_sample] + gather("p", n_prompt) + gather("s", n_sample)
    return tuple(np.ascontiguousarray(o.astype(np.float32)) for o in outs)


def kernel(**inputs):
    return _run(inputs, 8192, 4, 4, 8)
```
